# Optimizing a Trainium2 kernel written in Bass

```python
import math
import jax, jax.numpy as jnp
from jax import lax
import numpy as np

D_MODEL = 2048
BATCH = 4
SEQ = 2048
DEPTH = 4
DEC_BATCH = 32
DEC_SEQ = 8
PAST_LEN = 16384
PAGE_SIZE = 128

N_MIXERS = 2
N_ATTN_LAYERS = (DEPTH + 1) // 2
N_REC_LAYERS = DEPTH // 2
HEAD_DIM = 64
N_Q_HEADS = D_MODEL // HEAD_DIM
N_KV_HEADS = N_Q_HEADS // 8
GQA_GROUP = N_Q_HEADS // N_KV_HEADS
WINDOW = 128
CACHE_WIN = min(WINDOW, PAST_LEN)
ROT_DIM = HEAD_DIM // 4
ROPE_THETA = 500000.0
EXPAND = 128
REC_HEADS = D_MODEL // EXPAND
REC_DK = EXPAND
REC_DV = D_MODEL // REC_HEADS
REC_CHUNK = 64
D_FF = 4 * D_MODEL
EPS = 1e-6

kernel_name = 'hybrid_swa_sink_hgrn2_decode_step'


def rms_norm(x, gain):
    xf = x.astype(jnp.float32)
    y = xf * lax.rsqrt(jnp.mean(xf * xf, axis=-1, keepdims=True) + EPS)
    return (y * gain.astype(jnp.float32)).astype(x.dtype)


def partial_rope(x, pos):
    half = ROT_DIM // 2
    inv = ROPE_THETA ** (-jnp.arange(half, dtype=jnp.float32) / half)
    ang = pos.astype(jnp.float32)[:, None] * inv[None, :]
    cos = jnp.cos(ang)[:, None, :]
    sin = jnp.sin(ang)[:, None, :]
    xf = x.astype(jnp.float32)
    x1 = xf[..., :half]
    x2 = xf[..., half:ROT_DIM]
    out = jnp.concatenate([x1 * cos - x2 * sin, x2 * cos + x1 * sin, xf[..., ROT_DIM:]], axis=-1)
    return out.astype(x.dtype)


def attn_qkv(h, w_qkv, q_gain, k_gain, pos):
    B, L, _ = h.shape
    qkv = h @ w_qkv
    q, k, v = jnp.split(qkv, [N_Q_HEADS * HEAD_DIM, (N_Q_HEADS + N_KV_HEADS) * HEAD_DIM], axis=-1)
    q = q.reshape(B, L, N_Q_HEADS, HEAD_DIM)
    k = k.reshape(B, L, N_KV_HEADS, HEAD_DIM)
    v = v.reshape(B, L, N_KV_HEADS, HEAD_DIM)
    q = partial_rope(rms_norm(q, q_gain), pos)
    k = partial_rope(rms_norm(k, k_gain), pos)
    return q, k, v


def sink_attend(q, k, v, q_pos, k_pos, sinks):
    B, N, Lq = q.shape[:3]
    qg = q.reshape(B, N, Lq, N_KV_HEADS, GQA_GROUP, HEAD_DIM)
    s = jnp.einsum('bnqhgd,bnshd->bnhgqs', qg, k, preferred_element_type=jnp.float32) * (HEAD_DIM ** -0.5)
    dist = q_pos[:, :, None] - k_pos[:, None, :]
    ok = (dist >= 0) & (dist < WINDOW) & (k_pos[:, None, :] >= 0)
    s = jnp.where(ok[None, :, None, None], s, -jnp.inf)
    sink = jnp.broadcast_to(sinks.astype(jnp.float32).reshape(1, 1, N_KV_HEADS, GQA_GROUP, 1, 1), s.shape[:-1] + (1,))
    p = jax.nn.softmax(jnp.concatenate([s, sink], axis=-1), axis=-1)[..., :-1]
    o = jnp.einsum('bnhgqs,bnshd->bnqhgd', p.astype(v.dtype), v)
    return o.reshape(B, N, Lq, N_Q_HEADS * HEAD_DIM)


def swa_prompt(h, w_qkv, q_gain, k_gain, sinks, w_o):
    B, L, _ = h.shape
    pos = jnp.arange(L, dtype=jnp.int32)
    q, k, v = attn_qkv(h, w_qkv, q_gain, k_gain, pos)
    nb = L // WINDOW
    def band(t):
        tp = jnp.pad(t, ((0, 0), (WINDOW, 0), (0, 0), (0, 0)))
        prev = tp[:, :L].reshape(B, nb, WINDOW, N_KV_HEADS, HEAD_DIM)
        cur = tp[:, WINDOW:].reshape(B, nb, WINDOW, N_KV_HEADS, HEAD_DIM)
        return jnp.concatenate([prev, cur], axis=2)
    kpos = jnp.arange(-WINDOW, L, dtype=jnp.int32)
    kpos_b = jnp.concatenate([kpos[:L].reshape(nb, WINDOW), kpos[WINDOW:].reshape(nb, WINDOW)], axis=1)
    qb = q.reshape(B, nb, WINDOW, N_Q_HEADS, HEAD_DIM)
    o = sink_attend(qb, band(k), band(v), pos.reshape(nb, WINDOW), kpos_b, sinks)
    y = o.reshape(B, L, N_Q_HEADS * HEAD_DIM) @ w_o
    return y, (k[:, L - CACHE_WIN:], v[:, L - CACHE_WIN:])


def swa_sample(h, cache_k, cache_v, w_qkv, q_gain, k_gain, sinks, w_o):
    B, L, _ = h.shape
    pos = PAST_LEN + jnp.arange(L, dtype=jnp.int32)
    q, k, v = attn_qkv(h, w_qkv, q_gain, k_gain, pos)
    kk = jnp.concatenate([cache_k.astype(k.dtype), k], axis=1)
    vv = jnp.concatenate([cache_v.astype(v.dtype), v], axis=1)
    kpos = jnp.arange(PAST_LEN - CACHE_WIN, PAST_LEN + L, dtype=jnp.int32)
    o = sink_attend(q[:, None], kk[:, None], vv[:, None], pos[None], kpos[None], sinks)
    y = o.reshape(B, L, N_Q_HEADS * HEAD_DIM) @ w_o
    return y, (kk[:, -CACHE_WIN:], vv[:, -CACHE_WIN:])


def hgrn2_scan(q, k, v, logf, S0, chunk):
    B, L, H, _ = q.shape
    nc = L // chunk
    def to_chunks(t):
        return t.reshape(B, nc, chunk, H, t.shape[-1]).transpose(1, 0, 3, 2, 4)
    tri = jnp.tril(jnp.ones((chunk, chunk), dtype=bool))
    def step(S, xs):
        qc, kc, vc, gc = xs
        G = jnp.cumsum(gc, axis=2)
        dec = jnp.where(tri[:, :, None], G[:, :, :, None, :] - G[:, :, None, :, :], -jnp.inf)
        A = jnp.einsum('bhtk,bhsk,bhtsk->bhts', qc, kc, jnp.exp(dec))
        o = jnp.einsum('bhtk,bhkv->bhtv', qc * jnp.exp(G), S) + jnp.einsum('bhts,bhsv->bhtv', A, vc)
        Gl = G[:, :, -1:, :]
        S = jnp.exp(Gl[:, :, 0, :, None]) * S + jnp.einsum('bhsk,bhsv->bhkv', kc * jnp.exp(Gl - G), vc)
        return S, o
    S, o = lax.scan(step, S0, (to_chunks(q), to_chunks(k), to_chunks(v), to_chunks(logf)))
    o = o.transpose(1, 0, 3, 2, 4).reshape(B, L, H, v.shape[-1])
    return o, S


def hgrn2_mixer(h, S0, w_in, lb, o_gain, w_o):
    B, L, _ = h.shape
    f32 = jnp.float32
    qk = REC_HEADS * REC_DK
    dv = REC_HEADS * REC_DV
    proj = h @ w_in
    q, fz, i, gz = jnp.split(proj, [qk, 2 * qk, 2 * qk + dv], axis=-1)
    fz = fz.astype(f32)
    lb = lb.astype(f32)
    f = lb + (1.0 - lb) * jax.nn.sigmoid(fz)
    logf = jnp.log(f).reshape(B, L, REC_HEADS, REC_DK)
    k = ((1.0 - lb) * jax.nn.sigmoid(-fz)).reshape(B, L, REC_HEADS, REC_DK)
    q = jax.nn.silu(q.astype(f32)).reshape(B, L, REC_HEADS, REC_DK)
    v = i.astype(f32).reshape(B, L, REC_HEADS, REC_DV)
    chunk = math.gcd(L, REC_CHUNK)
    o, S = hgrn2_scan(q, k, v, logf, S0.astype(f32), chunk)
    o = rms_norm(o, o_gain.reshape(REC_HEADS, REC_DV)).reshape(B, L, dv).astype(h.dtype)
    y = (o * jax.nn.silu(gz)) @ w_o
    return y, (S.astype(h.dtype),)


def lower_bounds(logits):
    cum = jnp.cumsum(jax.nn.softmax(logits.astype(jnp.float32), axis=0), axis=0)
    return cum - cum[0:1]


def block(x, c, mix_fn, gain_mix, gain_mlp, w_ada, b_ada, w_up, w_down):
    mod = (jax.nn.silu(c) @ w_ada + b_ada)[:, None, :]
    sh1, sc1, g1, sh2, sc2, g2 = jnp.split(mod, 6, axis=-1)
    mixed, new_state = mix_fn(rms_norm(x, gain_mix) * (1 + sc1) + sh1)
    x = x + g1 * mixed
    h = rms_norm(x, gain_mlp) * (1 + sc2) + sh2
    x = x + g2 * (jnp.square(jax.nn.relu(h @ w_up)) @ w_down)
    return x, new_state


def setup_inputs(seed: int = 0) -> dict:
    key = jax.random.key(seed)
    ks = jax.random.split(key, 24)
    def nrm(k, shape, s=1.0):
        return jax.random.normal(k, shape, jnp.float32) * s
    D = D_MODEL
    qkv_w = (N_Q_HEADS + 2 * N_KV_HEADS) * HEAD_DIM
    rec_in = 2 * REC_HEADS * REC_DK + 2 * REC_HEADS * REC_DV
    return {
        'x_prompt': nrm(ks[0], (BATCH, SEQ, D)),
        'x_sample': nrm(ks[1], (DEC_BATCH, DEC_SEQ, D)),
        'cache_k_win': nrm(ks[2], (N_ATTN_LAYERS, DEC_BATCH, CACHE_WIN, N_KV_HEADS, HEAD_DIM)),
        'cache_v_win': nrm(ks[3], (N_ATTN_LAYERS, DEC_BATCH, CACHE_WIN, N_KV_HEADS, HEAD_DIM)),
        'state_hgrn': nrm(ks[4], (N_REC_LAYERS, DEC_BATCH, REC_HEADS, REC_DK, REC_DV), 0.5),
        'c_prompt': nrm(ks[5], (BATCH, D)),
        'c_sample': nrm(ks[6], (DEC_BATCH, D)),
        'norm_gain': 1.0 + nrm(ks[7], (DEPTH, 2, D), 0.02),
        'w_ada': nrm(ks[8], (DEPTH, D, 6 * D), 0.5 * D ** -0.5),
        'b_ada': nrm(ks[9], (DEPTH, 6 * D), 0.01),
        'attn_w_qkv': nrm(ks[10], (N_ATTN_LAYERS, D, qkv_w), D ** -0.5),
        'attn_q_gain': 1.0 + nrm(ks[11], (N_ATTN_LAYERS, HEAD_DIM), 0.02),
        'attn_k_gain': 1.0 + nrm(ks[12], (N_ATTN_LAYERS, HEAD_DIM), 0.02),
        'attn_sinks': nrm(ks[13], (N_ATTN_LAYERS, N_Q_HEADS), 0.5),
        'attn_w_o': nrm(ks[14], (N_ATTN_LAYERS, N_Q_HEADS * HEAD_DIM, D), (N_Q_HEADS * HEAD_DIM) ** -0.5),
        'rec_w_in': nrm(ks[15], (N_REC_LAYERS, D, rec_in), D ** -0.5),
        'rec_lb_logits': nrm(ks[16], (N_REC_LAYERS, REC_HEADS * REC_DK), 0.5),
        'rec_o_gain': 1.0 + nrm(ks[17], (N_REC_LAYERS, REC_HEADS * REC_DV), 0.02),
        'rec_w_o': nrm(ks[18], (N_REC_LAYERS, REC_HEADS * REC_DV, D), (REC_HEADS * REC_DV) ** -0.5),
        'mlp_w_up': nrm(ks[19], (DEPTH, D, D_FF), D ** -0.5),
        'mlp_w_down': nrm(ks[20], (DEPTH, D_FF, D), D_FF ** -0.5),
    }


def reference(x_prompt, x_sample, cache_k_win, cache_v_win, state_hgrn, c_prompt, c_sample,
              norm_gain, w_ada, b_ada, attn_w_qkv, attn_q_gain, attn_k_gain, attn_sinks, attn_w_o,
              rec_w_in, rec_lb_logits, rec_o_gain, rec_w_o, mlp_w_up, mlp_w_down):
    lbs = lower_bounds(rec_lb_logits)
    xp, xs = x_prompt, x_sample
    kwp, vwp, kws, vws, sp, ss = [], [], [], [], [], []
    for i in range(DEPTH):
        j = i // N_MIXERS
        common = (norm_gain[i, 0], norm_gain[i, 1], w_ada[i], b_ada[i], mlp_w_up[i], mlp_w_down[i])
        if i % N_MIXERS == 0:
            aw = (attn_w_qkv[j], attn_q_gain[j], attn_k_gain[j], attn_sinks[j], attn_w_o[j])
            xp, (kp, vp) = block(xp, c_prompt, lambda h: swa_prompt(h, *aw), *common)
            xs, (kn, vn) = block(xs, c_sample, lambda h: swa_sample(h, cache_k_win[j], cache_v_win[j], *aw), *common)
            kwp.append(kp); vwp.append(vp); kws.append(kn); vws.append(vn)
        else:
            rw = (rec_w_in[j], lbs[j], rec_o_gain[j], rec_w_o[j])
            S0p = jnp.zeros((xp.shape[0], REC_HEADS, REC_DK, REC_DV), xp.dtype)
            xp, (Sp,) = block(xp, c_prompt, lambda h: hgrn2_mixer(h, S0p, *rw), *common)
            xs, (Ss,) = block(xs, c_sample, lambda h: hgrn2_mixer(h, state_hgrn[j], *rw), *common)
            sp.append(Sp); ss.append(Ss)
    k_win_prompt = jnp.stack(kwp)
    v_win_prompt = jnp.stack(vwp)
    k_win_sample = jnp.stack(kws)
    v_win_sample = jnp.stack(vws)
    hgrn_state_prompt = jnp.stack(sp)
    hgrn_state_sample = jnp.stack(ss)
    return (xp, xs, k_win_prompt, v_win_prompt, k_win_sample, v_win_sample, hgrn_state_prompt, hgrn_state_sample)
```

```python
import math
import numpy as np
from contextlib import ExitStack
import concourse.bass as bass
import concourse.mybir as mybir
from concourse.bass_utils import run_bass_kernel_spmd

F32 = mybir.dt.float32
BF16 = mybir.dt.bfloat16
AF = mybir.ActivationFunctionType
ALU = mybir.AluOpType
AX = mybir.AxisListType

D = 2048
DC = 16
DEPTH = 4
BATCH = 4
SEQ = 2048
DEC_BATCH = 32
DEC_SEQ = 8
PAST_LEN = 16384
HD = 64
NQH = 32
NKV = 4
WIN = 128
ROT = 16
THETA = 500000.0
RH = 16
DFF = 8192
EPS = 1e-6

NCORES = 8
NSEG = 2
TP = 1024
NSQ = 4
TS = NSQ * DEC_SEQ
T = TP + TS
NSEQ = 1 + NSEG * NSQ
TCH = [(0, 512), (512, 1024), (1024, T)]
FB = 1024
CH = 32
HT = 96
CLAMP = 70.0

ENGS = ("pe", "act", "dve", "pool", "sp")


class Buf:
    __slots__ = ("name", "w", "r")

    def __init__(self, name="", w=None):
        self.name = name
        self.w = w
        self.r = []


class DmaSem:
    __slots__ = ("sem", "count")

    def __init__(self, sem):
        self.sem = sem
        self.count = 0


class Op:
    __slots__ = ("eng", "idx", "fn", "waits", "signal", "count", "dsem", "dcount")

    def __init__(self, eng, idx, fn):
        self.eng = eng
        self.idx = idx
        self.fn = fn
        self.waits = []
        self.signal = False
        self.count = 0
        self.dsem = None
        self.dcount = 0


class Sched:
    def __init__(self, nc, stack):
        self.nc = nc
        self.stack = stack
        self.ops = {e: [] for e in ENGS}
        self.sems = {e: stack.enter_context(nc.semaphore("s_" + e)) for e in ENGS}
        self.wm = {a: {b: -1 for b in ENGS} for a in ENGS}
        self.dwm = {a: {} for a in ENGS}
        self.nsem = 0
        self.fence = None
        self.pools = {}
        self.pool_i = {}
        self.all_dsems = []

    def make_pool(self, eng, n):
        self.pools[eng] = [self.dma_sem() for _ in range(n)]
        self.pool_i[eng] = 0

    def dma_sem(self, name=None):
        self.nsem += 1
        s = self.stack.enter_context(self.nc.semaphore(name or ("dq%d" % self.nsem)))
        d = DmaSem(s)
        self.all_dsems.append(d)
        return d

    def buf(self, name=""):
        return Buf(name, self.fence)

    def op(self, eng, fn, reads=(), writes=(), dsem=None):
        y = Op(eng, len(self.ops[eng]), fn)
        a = eng
        best = {}
        dbest = {}
        cands = []
        if dsem == "auto":
            pl = self.pools[eng]
            dsem = pl[self.pool_i[eng] % len(pl)]
            self.pool_i[eng] += 1
            if dsem.count > 0 and self.dwm[a].get(id(dsem), 0) < dsem.count:
                self.dwm[a][id(dsem)] = dsem.count
                y.waits.append((dsem, dsem.count))
        for b in reads:
            cands.append(b.w)
        for b in writes:
            cands.append(b.w)
            cands.extend(b.r)
        for x in cands:
            if x is None:
                continue
            if x.dsem is not None:
                if x.dsem is dsem:
                    continue
                key = id(x.dsem)
                if key not in dbest or dbest[key].dcount < x.dcount:
                    dbest[key] = x
            else:
                if x.eng == "pe" and a == "pe":
                    continue
                if x.eng not in best or best[x.eng].idx < x.idx:
                    best[x.eng] = x
        for key, x in dbest.items():
            if self.dwm[a].get(key, 0) >= x.dcount:
                continue
            self.dwm[a][key] = x.dcount
            y.waits.append((x.dsem, x.dcount))
        for b_, x in best.items():
            if self.wm[a][b_] >= x.idx:
                continue
            self.wm[a][b_] = x.idx
            y.waits.append(x)
        for b in reads:
            b.r.append(y)
        for b in writes:
            b.w = y
            b.r = []
        if dsem is not None:
            dsem.count += 16
            y.dsem = dsem
            y.dcount = dsem.count
        self.ops[eng].append(y)
        return y

    def emit(self, final_waits=()):
        nc = self.nc
        for e in ENGS:
            for y in self.ops[e]:
                for w in y.waits:
                    if isinstance(w, Op):
                        w.signal = True
        for e in ENGS:
            c = 0
            for y in self.ops[e]:
                if y.signal:
                    c += 1
                    y.count = c
        sems = self.sems
        ops = self.ops
        self.stats = {e: (len(ops[e]), sum(1 for y in ops[e] if y.signal),
                          sum(len(y.waits) for y in ops[e])) for e in ENGS}

        def run(e, h):
            for y in ops[e]:
                for w in y.waits:
                    if isinstance(w, Op):
                        h.wait_ge(sems[w.eng], w.count)
                    else:
                        h.wait_ge(w[0].sem, w[1])
                ins = y.fn(h)
                if y.dsem is not None:
                    ins.then_inc(y.dsem.sem, 16)
                elif y.signal:
                    ins.then_inc(sems[e], 1)
            if e == "sp":
                for d in self.all_dsems:
                    if d.count > 0:
                        h.wait_ge(d.sem, d.count)

        with nc.Block() as block:
            @block.tensor
            def _(h):
                run("pe", h)

            @block.scalar
            def _(h):
                run("act", h)

            @block.vector
            def _(h):
                run("dve", h)

            @block.gpsimd
            def _(h):
                run("pool", h)

            @block.sync
            def _(h):
                run("sp", h)


class Defer:
    def __init__(self, lag):
        self.q = []
        self.lag = lag

    def push(self, fn):
        self.q.append(fn)
        while len(self.q) > self.lag:
            self.q.pop(0)()

    def flush(self):
        while self.q:
            self.q.pop(0)()


class Prog:
    def __init__(self, nlayers=DEPTH, nseg=NSEG, layers=None, do_mlp=True, nheads=RH):
        self.nlayers = nlayers
        self.nseg = nseg
        self.layers = list(range(nlayers)) if layers is None else layers
        self.do_mlp = do_mlp
        self.nheads = nheads
        self.nc = bass.Bass("TRN2", target_bir_lowering=False)
        self.out_sems = []

    def mm(self, out, lhsT, rhs, start, stop, r, w):
        self.S.op("pe", lambda e: e.matmul(out, lhsT, rhs, start=start, stop=stop), r, w)

    def tr(self, out, in_, ident, r, w):
        self.S.op("pe", lambda e: e.transpose(out, in_, ident), r, w)

    def act(self, out, in_, func, r, w, **kw):
        self.S.op("act", lambda e: e.activation(out, in_, func, **kw), r, w)

    def tt(self, out, in0, in1, op, r, w, eng="dve"):
        self.S.op(eng, lambda e: e.tensor_tensor(out, in0, in1, op), r, w)

    def ts(self, out, in0, s1, s2, op0, op1, r, w, eng="dve"):
        if op1 is None:
            self.S.op(eng, lambda e: e.tensor_scalar(out, in0, s1, None, op0), r, w)
        else:
            self.S.op(eng, lambda e: e.tensor_scalar(out, in0, s1, s2, op0, op1), r, w)

    def stt(self, out, in0, scalar, in1, op0, op1, r, w, eng="dve"):
        self.S.op(eng, lambda e: e.scalar_tensor_tensor(out, in0, scalar, in1, op0, op1), r, w)

    def cp(self, out, in_, r, w, eng="dve"):
        if eng == "act":
            self.S.op("act", lambda e: e.activation(out, in_, AF.Copy), r, w)
        else:
            self.S.op(eng, lambda e: e.tensor_copy(out, in_), r, w)

    def red(self, out, in_, r, w):
        self.S.op("dve", lambda e: e.reduce_sum(out, in_, axis=AX.X), r, w)

    def recip(self, out, in_, r, w):
        self.S.op("dve", lambda e: e.reciprocal(out, in_), r, w)

    def scan(self, out, d0, d1, init, r, w):
        self.S.op("dve", lambda e: e.tensor_tensor_scan(out, d0, d1, init, ALU.mult, ALU.add), r, w)

    def mset(self, ap, val, w, eng="dve"):
        self.S.op(eng, lambda e: e.memset(ap, val), (), w)

    def dma(self, eng, out, in_, r, w, dsem="auto"):
        self.S.op(eng, lambda e: e.dma_start(out=out, in_=in_), r, w, dsem)

    def sb(self, st, name, shape, dt):
        self.nsb = getattr(self, "nsb", 0) + 1
        return st.enter_context(self.nc.sbuf_tensor("%s_%d" % (name, self.nsb), shape, dt))

    def fence(self, bufs):
        op = self.S.op("dve", lambda e: e.memset(self.fence_t[:, 0:1], 0.0), (), list(bufs) + [self.fence_b] + self.psb + self.ptb)
        self.S.fence = op

    def rsqrt(self, out, in_, scale, bias, r, w, tmp, tmpb):
        self.act(tmp, in_, AF.Ln, r, [tmpb], scale=scale, bias=bias)
        self.act(out, tmp, AF.Exp, [tmpb], w, scale=-0.5)

    def wget(self, srcs, kc, ncols):
        i = self.wn % len(self.wr_t)
        self.wn += 1
        view = self.wr_t[i][:, 0:kc * ncols].rearrange("p (k n) -> p k n", k=kc)
        for src, c0, n in srcs:
            self.dma("pool", view[:, :, c0:c0 + n], src, (), [self.wr_b[i]], self.wr_s[i])
        return view, self.wr_b[i]

    def build(self):
        nc = self.nc
        L = self.nlayers
        NS = self.nseg
        di = lambda name, shape: nc.dram_tensor(name, list(shape), F32, kind="ExternalInput").ap()
        do = lambda name, shape: nc.dram_tensor(name, list(shape), F32, kind="ExternalOutput").ap()
        self.d_xT = di("xT", [NS, D, T])
        self.d_cT = di("cT", [128, DC * NSEQ])
        self.d_rope = di("rope", [NS, 128, 9 * 16])
        self.d_ck = di("ck", [2, NS, 128, NSQ * 256])
        self.d_cv = di("cv", [2, NS, 128, NSQ * 256])
        self.d_st = di("st", [2, NS * NSQ, RH, 128, 128])
        self.d_hflag = di("hflag", [NS, 128, 1])
        self.d_masks = di("masks", [128, 1792])
        self.d_hmask = di("hmask", [128, 128 + T + 4 * 32 + 32])
        self.d_gainT = di("gainT", [128, DEPTH * 2 * DC])
        self.d_badaT = di("badaT", [128, DEPTH * 96])
        self.d_qkg = di("qkg", [2, 2, HD])
        self.d_sinks = di("sinks", [2, NQH])
        self.d_lbT = di("lbT", [128, 2 * RH])
        self.d_ogT = di("ogT", [128, 2 * RH])
        self.d_wada = di("w_ada", [DEPTH, D, 6 * D])
        self.d_wqkv = di("attn_w_qkv", [2, D, 2560])
        self.d_wo_a = di("attn_w_o", [2, D, D])
        self.d_win = di("rec_w_in", [2, D, 4 * D])
        self.d_wo_r = di("rec_w_o", [2, D, D])
        self.d_wup = di("mlp_w_up", [DEPTH, D, DFF])
        self.d_wdn = di("mlp_w_down", [DEPTH, DFF, D])
        self.o_yT = do("yT", [NS, D, T])
        self.o_kwp = do("kwp", [2, NS, 128, 256])
        self.o_vwp = do("vwp", [2, NS, 128, 256])
        self.o_kws = do("kws", [2, NS * NSQ, 128, 256])
        self.o_vws = do("vws", [2, NS * NSQ, 128, 256])
        self.o_hsp = do("hsp", [2, NS, RH, 128, 128])
        self.o_hss = do("hss", [2, NS * NSQ, RH, 128, 128])

        with ExitStack() as st:
            self.S = S = Sched(nc, st)
            sb = lambda name, shape, dt: self.sb(st, name, shape, dt)
            self.x_t = sb("x_t", [128, DC * T], F32)
            self.xv = self.x_t[:].rearrange("p (c t) -> p c t", c=DC)
            self.xb = [[Buf("x%d_%d" % (c, k)) for k in range(3)] for c in range(DC)]
            self.h_t = sb("h_t", [128, DC * T], BF16)
            self.hv = self.h_t[:].rearrange("p (c t) -> p c t", c=DC)
            self.hb = [Buf("h%d" % k) for k in range(3)]
            self.wr_t = [sb("wr%d" % i, [128, 8192], BF16) for i in range(2)]
            self.wr_b = [Buf("wr%d" % i) for i in range(2)]
            self.wr_s = [S.dma_sem() for i in range(2)]
            self.wn = 0
            self.mod_t = sb("mod_t", [128, DEPTH * 6 * DC * NSEQ], F32)
            self.modv = self.mod_t[:].rearrange("p (l j c s) -> p l j c s", l=DEPTH, j=6, c=DC)
            self.modb = [Buf("mod%d" % l_) for l_ in range(DEPTH)]
            self.fence_t = sb("fence_t", [128, 2], F32)
            self.fence_b = Buf("fence")
            self.ones_t = sb("ones_t", [128, 128], BF16)
            self.ident_t = sb("ident_t", [128, 128], BF16)
            self.masks_t = sb("masks_t", [128, 1792], BF16)
            self.hmask_t = sb("hmask_t", [128, 128 + T + 4 * 32 + 32], BF16)
            self.rope_t = sb("rope_t", [128, 9 * 16], F32)
            self.ropeb = Buf("rope")
            self.hflag_t = sb("hflag_t", [128, 1], F32)
            self.mp0_t = sb("mp0_t", [128, 512], BF16)
            self.mp0b = Buf("mp0")
            self.cb = Buf("consts")
            self.gain_t = sb("gain_t", [128, DEPTH * 2 * DC], F32)
            self.bada_t = sb("bada_t", [128, DEPTH * 96], F32)
            self.sc_t = sb("sc_t", [128, DC * NSEQ], BF16)
            self.gbb = Buf("gb")
            self.ada_s = [S.dma_sem() for i in range(2)]
            self.adan = 0
            self.lb_t = sb("lb_t", [128, 2 * RH], F32)
            self.og_t = sb("og_t", [128, 2 * RH], F32)
            self.halo_k = sb("halo_k", [128, 2 * NKV * 128], BF16)
            self.halo_v = sb("halo_v", [128, 2 * NKV * 65], BF16)
            self.halob = [[Buf("halo%d_%d" % (j, g)) for g in range(NKV)] for j in range(2)]
            self.ps = [st.enter_context(nc.psum_tensor("ps%d" % i, [128, 512], F32)) for i in range(6)]
            self.psb = [Buf("ps%d" % i) for i in range(6)]
            self.pt = [st.enter_context(nc.psum_tensor("pt%d" % i, [128, 1024], BF16)) for i in range(2)]
            self.ptb = [Buf("pt%d" % i) for i in range(2)]
            self.ptn = 0
            S.make_pool("sp", 20)
            S.make_pool("pool", 6)
            self.pn = 0

            self.load_consts(st)
            self.compute_mod(st)
            for seg in range(self.nseg):
                self.segment(seg)
            S.emit()
        return nc

    def load_consts(self, st):
        S = self.S
        self.dma("pool", self.masks_t[:], self.d_masks, (), [self.cb])
        self.dma("pool", self.hmask_t[:], self.d_hmask, (), [self.cb])
        self.dma("sp", self.og_t[:], self.d_ogT, (), [self.cb])
        self.dma("sp", self.lb_t[:], self.d_lbT, (), [self.cb])
        self.mset(self.ones_t[:], 1.0, [self.cb])
        self.cp(self.ident_t[:], self.masks_t[:, 1664:1792], [self.cb], [self.cb])
        lb = self.lb_t
        self.tt(lb[:, 0:16], lb[:, 16:32], lb[:, 0:16], ALU.subtract, [self.cb], [self.cb])
        self.act(lb[:, 0:16], lb[:, 0:16], AF.Sigmoid, [self.cb], [self.cb])
        self.ts(lb[:, 16:32], lb[:, 0:16], -1.0, 1.0, ALU.mult, ALU.add, [self.cb], [self.cb])
        self.mset(self.halo_k[:], 0.0, [b for row in self.halob for b in row])
        self.mset(self.halo_v[:], 0.0, [b for row in self.halob for b in row])

    def m_cur(self):
        return self.masks_t[:, 0:512].rearrange("p (j t) -> p j t", j=4)

    def m_prev(self):
        return self.masks_t[:, 512:1024].rearrange("p (j t) -> p j t", j=4)

    def m_sprev(self, b):
        return self.masks_t[:, 1024 + b * 128:1024 + (b + 1) * 128].rearrange("p (j t) -> p j t", j=4)

    def m_scur(self):
        return self.masks_t[0:32, 1536:1664].rearrange("p (j t) -> p j t", j=4)

    def compute_mod(self, st):
        S = self.S
        with ExitStack() as ph:
            cT = self.sb(ph, "cT", [128, DC * NSEQ], F32)
            cTb = S.buf("cT")
            self.dma("sp", cT[:], self.d_cT, (), [cTb])
            self.dma("sp", self.gain_t[:], self.d_gainT, (), [self.gbb])
            self.dma("sp", self.bada_t[:], self.d_badaT, (), [self.gbb])
            self.act(self.sc_t[:], cT[:], AF.Silu, [cTb], [self.gbb])
            wt = [self.sb(ph, "adaw%d" % i, [128, 4096], BF16) for i in range(2)]
            wtb = [S.buf() for i in range(2)]
            for _ in self.ada_gen(self.layers[0], wt, wtb, [4, 5]):
                pass
            self.fence([cTb] + wtb)

    def ada_gen(self, l, wt, wtb, banks):
        S = self.S
        scv = self.sc_t[:].rearrange("p (c s) -> p c s", c=DC)
        wv = self.d_wada[l].rearrange("(kc p) n -> p kc n", p=128)
        gbb = self.gbb
        for nt in range(48):
            i = self.adan % 2
            self.adan += 1
            w = wt[i][:].rearrange("p (k n) -> p k n", k=DC)
            self.dma("pool", w, wv[:, :, nt * 256:(nt + 1) * 256], (), [wtb[i]], self.ada_s[i])
            for q in range(2):
                nch = nt * 2 + q
                j, c = divmod(nch, DC)
                pi = banks[j % len(banks)]
                pso = self.ps[pi][:, c * NSEQ:(c + 1) * NSEQ]
                for kc in range(DC):
                    self.mm(pso, w[:, kc, q * 128:(q + 1) * 128], scv[:, kc, :], kc == 0, kc == DC - 1,
                            [wtb[i], gbb], [self.psb[pi]])
                if c == DC - 1:
                    bias = self.bada_t[:, l * 96 + j * 16:l * 96 + (j + 1) * 16].unsqueeze(2).to_broadcast([128, DC, NSEQ])
                    self.tt(self.modv[:, l, j], self.ps[pi][:, 0:DC * NSEQ].rearrange("p (c s) -> p c s", c=DC),
                            bias, ALU.add, [self.psb[pi], gbb], [self.modb[l]])
            yield
        for jn, j in ((0, 1), (1, 4)):
            g = self.gain_t[:, (l * 2 + jn) * DC:(l * 2 + jn + 1) * DC].unsqueeze(2).to_broadcast([128, DC, NSEQ])
            self.ts(self.modv[:, l, j], self.modv[:, l, j], 1.0, math.sqrt(D), ALU.add, ALU.mult,
                    [self.modb[l]], [self.modb[l]])
            self.tt(self.modv[:, l, j], self.modv[:, l, j], g, ALU.mult, [self.modb[l], gbb], [self.modb[l]])
        yield

    def tokmod(self, dst, dstb, l, j, seg):
        src = self.modv[:, l, j, :, 1 + seg * NSQ:1 + (seg + 1) * NSQ].unsqueeze(3).to_broadcast([128, DC, NSQ, DEC_SEQ])
        self.cp(dst.rearrange("p c (b i) -> p c b i", b=NSQ), src, [self.modb[l]], [dstb])

    def norm_mod(self, l, which, seg):
        S = self.S
        js, jh = (1, 0) if which == 0 else (4, 3)
        with ExitStack() as ph:
            sq = self.sb(ph, "sq", [128, DC * 512], BF16)
            sqb = S.buf("sq")
            rs = self.sb(ph, "rs", [128, 512], F32)
            rsb = S.buf("rs")
            ln = self.sb(ph, "ln", [128, 512], F32)
            lnb = S.buf("ln")
            tmp = [self.sb(ph, "ntmp%d" % i, [128, 512], F32) for i in range(2)]
            tmpb = [S.buf("ntmp%d" % i) for i in range(2)]
            gst = self.sb(ph, "gst", [128, DC * TS], F32)
            sht = self.sb(ph, "sht", [128, DC * TS], F32)
            big = self.sb(ph, "nbig", [128, DC * TS], F32)
            gsb, shb, bigb = S.buf(), S.buf(), S.buf()
            gsv = gst[:].rearrange("p (c t) -> p c t", c=DC)
            shv = sht[:].rearrange("p (c t) -> p c t", c=DC)
            bigv = big[:].rearrange("p (c t) -> p c t", c=DC)
            self.tokmod(gsv, gsb, l, js, seg)
            self.tokmod(shv, shb, l, jh, seg)
            n = 0
            for k, (t0, t1) in enumerate(TCH):
                W = t1 - t0
                xr = [self.xb[c][k] for c in range(DC)]
                sqv = sq[:, 0:DC * W].rearrange("p (c t) -> p c t", c=DC)
                self.act(sqv, self.xv[:, :, t0:t1], AF.Square, xr, [sqb])
                pi = 5
                for c in range(DC):
                    self.mm(self.ps[pi][:, 0:W], self.ones_t[:], sqv[:, c, :], c == 0, c == DC - 1,
                            [sqb, self.cb], [self.psb[pi]])
                self.rsqrt(rs[:, 0:W], self.ps[pi][:, 0:W], 1.0, D * EPS, [self.psb[pi]], [rsb], ln[:, 0:W], lnb)
                if k < 2:
                    for c in range(DC):
                        i = n % 2
                        n += 1
                        self.stt(tmp[i][:, 0:W], self.xv[:, c, t0:t1], self.modv[:, l, js, c, 0:1], rs[:, 0:W],
                                 ALU.mult, ALU.mult, [self.xb[c][k], self.modb[l], rsb], [tmpb[i]])
                        self.act(self.hv[:, c, t0:t1], tmp[i][:, 0:W], AF.Identity, [tmpb[i], self.modb[l]], [self.hb[k]],
                                 bias=self.modv[:, l, jh, c, 0:1])
                else:
                    rbc = rs[:, 0:W].unsqueeze(1).to_broadcast([128, DC, W])
                    self.tt(bigv, self.xv[:, :, t0:t1], rbc, ALU.mult, xr + [rsb], [bigb])
                    self.tt(bigv, bigv, gsv, ALU.mult, [bigb, gsb], [bigb])
                    self.tt(self.hv[:, :, t0:t1], bigv, shv, ALU.add, [bigb, shb], [self.hb[k]])
            self.fence([sqb, rsb, lnb, gsb, shb, bigb] + tmpb)

    def resid_add(self, pso, psb, l, jg, nch, k, seg, gtok, gtokb, stmp, stmpb):
        t0, t1 = TCH[k]
        if k < 2:
            self.stt(self.xv[:, nch, t0:t1], pso, self.modv[:, l, jg, nch, 0:1], self.xv[:, nch, t0:t1],
                     ALU.mult, ALU.add, [psb, self.modb[l], self.xb[nch][k]], [self.xb[nch][k]])
        else:
            self.tt(stmp, pso, gtok[:, nch, :], ALU.mult, [psb, gtokb], [stmpb])
            self.tt(self.xv[:, nch, t0:t1], self.xv[:, nch, t0:t1], stmp, ALU.add, [stmpb, self.xb[nch][k]], [self.xb[nch][k]])

    def proj_resid(self, w, wb, nkc, rhs_v, rhs_b, l, jg, seg, gtok, gtokb, stmp, stmpb):
        for nch in range(DC):
            for k, (t0, t1) in enumerate(TCH):
                pi = self.pn % 6
                self.pn += 1
                pso = self.ps[pi][:, 0:t1 - t0]
                for kc in range(nkc):
                    self.mm(pso, w[:, kc, nch * 128:(nch + 1) * 128], rhs_v[:, kc, t0:t1], kc == 0, kc == nkc - 1,
                            [wb] + rhs_b(kc, k), [self.psb[pi]])
                self.resid_add(pso, self.psb[pi], l, jg, nch, k, seg, gtok, gtokb, stmp, stmpb)

    def mlp(self, l, seg):
        S = self.S
        with ExitStack() as ph:
            a_t = self.sb(ph, "a_t", [128, 8 * T], BF16)
            av = a_t[:].rearrange("p (c t) -> p c t", c=8)
            ab = [[S.buf("a%d_%d" % (c, k)) for k in range(3)] for c in range(8)]
            rl = [self.sb(ph, "rl%d" % i, [128, 512], BF16) for i in range(2)]
            rlb = [S.buf() for i in range(2)]
            gtok = self.sb(ph, "g2tok", [128, DC * TS], F32)
            gtokv = gtok[:].rearrange("p (c t) -> p c t", c=DC)
            gtokb = S.buf()
            stmp = self.sb(ph, "stmp", [128, TS], F32)
            stmpb = S.buf()
            self.tokmod(gtokv, gtokb, l, 5, seg)
            li = self.layers.index(l)
            adag = None
            nbk = 6
            adab = []
            if seg == 0 and li + 1 < len(self.layers):
                wt = [self.sb(ph, "adaw%d" % i, [128, 4096], BF16) for i in range(2)]
                adab = [S.buf() for i in range(2)]
                adag = self.ada_gen(self.layers[li + 1], wt, adab, [5])
                nbk = 5

            def ada_step(n):
                nonlocal adag
                for _ in range(n):
                    if adag is None:
                        return
                    try:
                        next(adag)
                    except StopIteration:
                        adag = None
            nstep = 0
            wu = self.d_wup[l].rearrange("(kc p) n -> p kc n", p=128)
            wd = self.d_wdn[l].rearrange("(kc p) n -> p kc n", p=128)
            nr = 0
            for b in range(DFF // FB):
                for half in range(2):
                    c0 = b * FB + half * 512
                    w, wb = self.wget([(wu[:, :, c0:c0 + 512], 0, 512)], 16, 512)
                    for q in range(4):
                        fc = half * 4 + q
                        for k, (t0, t1) in enumerate(TCH):
                            W = t1 - t0
                            pi = self.pn % nbk
                            self.pn += 1
                            pso = self.ps[pi][:, 0:W]
                            for kc in range(DC):
                                self.mm(pso, w[:, kc, q * 128:(q + 1) * 128], self.hv[:, kc, t0:t1], kc == 0, kc == DC - 1,
                                        [wb, self.hb[k]], [self.psb[pi]])
                            i = nr % 2
                            nr += 1
                            self.act(rl[i][:, 0:W], pso, AF.Relu, [self.psb[pi]], [rlb[i]])
                            self.tt(av[:, fc, t0:t1], rl[i][:, 0:W], rl[i][:, 0:W], ALU.mult, [rlb[i]], [ab[fc][k]])
                    nstep += 1
                    ada_step(2 if nstep % 2 == 0 else 1)
                for half in range(2):
                    w, wb = self.wget([(wd[:, b * 8:(b + 1) * 8, half * 1024:(half + 1) * 1024], 0, 1024)], 8, 1024)
                    for q in range(8):
                        nch = half * 8 + q
                        for k, (t0, t1) in enumerate(TCH):
                            pi = self.pn % nbk
                            self.pn += 1
                            pso = self.ps[pi][:, 0:t1 - t0]
                            for kc in range(8):
                                self.mm(pso, w[:, kc, q * 128:(q + 1) * 128], av[:, kc, t0:t1], kc == 0, kc == 7,
                                        [wb, ab[kc][k]], [self.psb[pi]])
                            self.resid_add(pso, self.psb[pi], l, 5, nch, k, seg, gtokv, gtokb, stmp[:], stmpb)
                    nstep += 1
                    ada_step(2 if nstep % 2 == 0 else 1)
            ada_step(1000)
            self.fence([b_ for row in ab for b_ in row] + rlb + [gtokb, stmpb] + adab)

    def qk_norm_rope(self, P, nh, src, srcb, gain_bc, tt_, sc, scb, out_bf, outb, out_f32=None, out_f32b=None, dup=False):
        S = self.S
        sqf, ss, ssl, rs_, qn = sc["sqf"], sc["ss"], sc["ssl"], sc["rs"], sc["qn"]
        srcv = src.rearrange("p (h d) -> p h d", h=nh)
        sqv = sqf[0:P, 0:nh * 64].rearrange("p (h d) -> p h d", h=nh)
        qnv = qn[0:P, 0:nh * 64].rearrange("p (h d) -> p h d", h=nh)
        self.act(sqv, srcv, AF.Square, [srcb], [scb])
        self.red(ss[0:P, 0:nh], sqv, [scb], [scb])
        self.rsqrt(rs_[0:P, 0:nh], ss[0:P, 0:nh], 1.0 / HD, EPS, [scb], [scb], ssl[0:P, 0:nh], scb)
        self.tt(qnv, srcv, rs_[0:P, 0:nh].unsqueeze(2).to_broadcast([P, nh, HD]), ALU.mult, [srcb, scb], [scb])
        self.tt(qnv, qnv, gain_bc[0:P].unsqueeze(1).to_broadcast([P, nh, HD]), ALU.mult, [scb, self.attb], [scb])
        cos = self.rope_t[0:P, tt_ * 16:tt_ * 16 + 8].unsqueeze(1).to_broadcast([P, nh, 8])
        sin = self.rope_t[0:P, tt_ * 16 + 8:tt_ * 16 + 16].unsqueeze(1).to_broadcast([P, nh, 8])
        t1 = sc["t1"][0:P, 0:nh * 8].rearrange("p (h d) -> p h d", h=nh)
        t2 = sc["t2"][0:P, 0:nh * 8].rearrange("p (h d) -> p h d", h=nh)
        t3 = sc["t3"][0:P, 0:nh * 8].rearrange("p (h d) -> p h d", h=nh)
        t4 = sc["t4"][0:P, 0:nh * 8].rearrange("p (h d) -> p h d", h=nh)
        x1 = qnv[:, :, 0:8]
        x2 = qnv[:, :, 8:16]
        rb = [scb, self.ropeb]
        self.tt(t1, x1, cos, ALU.mult, rb, [scb])
        self.tt(t2, x2, sin, ALU.mult, rb, [scb])
        self.tt(t3, x2, cos, ALU.mult, rb, [scb])
        self.tt(t4, x1, sin, ALU.mult, rb, [scb])
        self.tt(x1, t1, t2, ALU.subtract, [scb], [scb])
        self.tt(x2, t3, t4, ALU.add, [scb], [scb])
        if dup:
            self.cp(out_bf, qn[0:P, 0:HD].unsqueeze(1).to_broadcast([P, 2, HD]), [scb], [outb], eng="act")
        else:
            self.cp(out_bf, qnv, [scb], [outb], eng="act")
        if out_f32 is not None:
            self.cp(out_f32, qn[0:P, 0:nh * 64], [scb], [out_f32b])

    def ptile(self):
        i = self.ptn % 2
        self.ptn += 1
        return self.pt[i], self.ptb[i]

    def attn_layer(self, l, seg):
        S = self.S
        j = l // 2
        last_seg = (seg == self.nseg - 1)
        with ExitStack() as ph:
            sb = lambda name, shape, dt: self.sb(ph, name, shape, dt)
            bufs = []

            def nb(name=""):
                b = S.buf(name)
                bufs.append(b)
                return b
            self.attb = nb("attc")
            qg_t = sb("qg_t", [128, HD], F32)
            kg_t = sb("kg_t", [128, HD], F32)
            es_t = sb("es_t", [128, NQH], F32)
            self.dma("sp", qg_t[:], self.d_qkg[j, 0].partition_broadcast(128), (), [self.attb])
            self.dma("sp", kg_t[:], self.d_qkg[j, 1].partition_broadcast(128), (), [self.attb])
            self.dma("sp", es_t[:], self.d_sinks[j].partition_broadcast(128), (), [self.attb])
            self.act(es_t[:], es_t[:], AF.Exp, [self.attb], [self.attb])
            ck_t = sb("ck_t", [128, NSQ * HD], F32)
            cv_t = sb("cv_t", [128, NSQ * HD], F32)
            ckb = nb("ck")
            for b in range(NSQ):
                sq_ = seg * NSQ + b
                self.dma("sp", self.o_kws[j, sq_, 0:120, :], self.d_ck[j, seg][8:128, b * 256:(b + 1) * 256], (), ())
                self.dma("sp", self.o_vws[j, sq_, 0:120, :], self.d_cv[j, seg][8:128, b * 256:(b + 1) * 256], (), ())
            gtok = sb("g1tok", [128, DC * TS], F32)
            gtokv = gtok[:].rearrange("p (c t) -> p c t", c=DC)
            gtokb = nb()
            stmp = sb("stmp", [128, TS], F32)
            stmpb = nb()
            self.tokmod(gtokv, gtokb, l, 2, seg)
            qT = sb("qT", [128, 4 * T], BF16)
            qTv = qT[:].rearrange("p (c t) -> p c t", c=4)
            qTb = [nb("qT%d" % i) for i in range(9)]
            oT = sb("oT", [128, 4 * T], BF16)
            oTv = oT[:].rearrange("p (c t) -> p c t", c=4)
            oTb = [nb("oT%d" % k) for k in range(3)]
            kT = sb("kT", [128, 9 * 128 + TS], BF16)
            kTb = [nb("kT%d" % i) for i in range(10)]
            va = sb("va", [128, 10 * 65], BF16)
            vav = va[:].rearrange("p (b d) -> p b d", d=65)
            vab = [nb("va%d" % i) for i in range(10)]
            ckT = sb("ckT", [128, NSQ * 128], BF16)
            ckTb = nb("ckT")
            cva = sb("cva", [128, NSQ * 65], BF16)
            cvav = cva[:].rearrange("p (b d) -> p b d", d=65)
            cvab = nb("cva")
            ckd = sb("ckd", [128, 128], BF16)
            ckdb = nb()
            PT = [sb("PT%d" % i, [128, 512], BF16) for i in range(6)]
            PTb = [nb() for i in range(6)]
            PTs = [sb("PTs%d" % i, [128, 128], BF16) for i in range(5)]
            PTsb = [nb() for i in range(5)]
            sc = {"sqf": sb("sqf", [128, 512], F32), "ss": sb("ss", [128, 8], F32), "ssl": sb("ssl", [128, 8], F32),
                  "rs": sb("rs", [128, 8], F32), "qn": sb("qn", [128, 512], F32),
                  "t1": sb("t1", [128, 64], F32), "t2": sb("t2", [128, 64], F32),
                  "t3": sb("t3", [128, 64], F32), "t4": sb("t4", [128, 64], F32)}
            scb = nb("qsc")
            qr = [sb("qr%d" % i, [128, 512], BF16) for i in range(2)]
            qrb = [nb() for i in range(2)]
            kf = [sb("kf%d" % i, [128, HD], F32) for i in range(2)]
            kfb = [nb() for i in range(2)]
            vf = [sb("vf%d" % i, [128, HD], F32) for i in range(2)]
            vfb = [nb() for i in range(2)]
            kr = sb("kr", [128, HD], BF16)
            krb = nb()
            krd = sb("krd", [128, 128], BF16)
            krdb = nb()
            den = sb("den", [128, 4], F32)
            denb = nb()
            otm = [sb("otm%d" % i, [128, 512], BF16) for i in range(2)]
            otmb = [nb() for i in range(2)]
            self.dma("sp", self.rope_t[:], self.d_rope[seg], (), [self.ropeb])
            self.dma("sp", self.hflag_t[:], self.d_hflag[seg], (), [self.mp0b])
            self.ts(self.mp0_t[:], self.masks_t[:, 512:1024], self.hflag_t[:, 0:1], None, ALU.mult, None,
                    [self.cb, self.mp0b], [self.mp0b])
            mp0 = self.mp0_t[:].rearrange("p (j t) -> p j t", j=4)
            wq_all = self.d_wqkv[j].rearrange("(kc p) n -> p kc n", p=128)
            wo_all = self.d_wo_a[j].rearrange("(kc p) n -> p kc n", p=128)
            nq = 0
            nkf = 0
            self.mset(vav[:, :, 64:65], 1.0, vab)
            self.mset(cvav[:, :, 64:65], 1.0, [cvab])
            for g in range(NKV):
                self.cp(kT[:, 0:128], self.halo_k[:, (j * NKV + g) * 128:(j * NKV + g + 1) * 128], [self.halob[j][g]], [kTb[0]])
                self.cp(vav[:, 0, 0:64], self.halo_v[:, (j * NKV + g) * 65:(j * NKV + g) * 65 + 64], [self.halob[j][g]], [vab[0]])
                w, wb = self.wget([(wq_all[:, :, g * 512:(g + 1) * 512], 0, 512)], 16, 512)
                dq = Defer(1)
                for tt_ in range(9):
                    P = 128 if tt_ < 8 else TS
                    t0 = tt_ * 128
                    k = min(tt_ // 4, 2)
                    pi = self.pn % 2
                    self.pn += 1
                    pso = self.ps[pi][0:P, :]
                    for kc in range(DC):
                        self.mm(pso, self.hv[:, kc, t0:t0 + P], w[:, kc, :], kc == 0, kc == DC - 1,
                                [wb, self.hb[k]], [self.psb[pi]])
                    i = nq % 2
                    nq += 1

                    def q_post(P=P, t0=t0, pi=pi, i=i, tt_=tt_, pso=pso):
                        self.qk_norm_rope(P, 8, pso, self.psb[pi], qg_t, tt_, sc, scb,
                                          qr[i][0:P, :].rearrange("p (h d) -> p h d", h=8), qrb[i])
                        ptile, ptb = self.ptile()
                        for c in range(4):
                            self.tr(ptile[:, c * 128:c * 128 + P], qr[i][0:P, c * 128:(c + 1) * 128], self.ident_t[0:P, 0:P],
                                    [qrb[i], self.cb], [ptb])
                        self.cp(qTv[:, :, t0:t0 + P], ptile[:, 0:512].rearrange("p (c t) -> p c t", c=4)[:, :, 0:P],
                                [ptb], [qTb[tt_]], eng="act")
                    dq.push(q_post)
                wkv = self.d_wqkv[j].rearrange("(kc p) n -> p kc n", p=128)
                w, wb = self.wget([(wkv[:, :, 2048 + g * 64:2048 + (g + 1) * 64], 0, 64),
                                   (wkv[:, :, 2304 + g * 64:2304 + (g + 1) * 64], 64, 64)], 16, 128)
                for tt_ in range(9):
                    P = 128 if tt_ < 8 else TS
                    t0 = tt_ * 128
                    k = min(tt_ // 4, 2)
                    pi = self.pn % 2
                    self.pn += 1
                    pso = self.ps[pi][0:P, 0:128]
                    for kc in range(DC):
                        self.mm(pso, self.hv[:, kc, t0:t0 + P], w[:, kc, :], kc == 0, kc == DC - 1,
                                [wb, self.hb[k]], [self.psb[pi]])
                    i = nkf % 2
                    nkf += 1

                    def kv_post(P=P, t0=t0, pi=pi, i=i, tt_=tt_, g=g):
                        wantf = tt_ >= 7
                        self.qk_norm_rope(P, 1, self.ps[pi][0:P, 0:64], self.psb[pi], kg_t, tt_, sc, scb,
                                          krd[0:P, :].rearrange("p (r d) -> p r d", r=2), krdb,
                                          kf[i][0:P, :] if wantf else None, kfb[i], dup=True)
                        if wantf:
                            self.cp(vf[i][0:P, :], self.ps[pi][0:P, 64:128], [self.psb[pi]], [vfb[i]], eng="act")
                        blk = tt_ + 1
                        self.cp(vav[0:P, blk, 0:64], self.ps[pi][0:P, 64:128], [self.psb[pi]], [vab[blk]])
                        ptile, ptb = self.ptile()
                        self.tr(ptile[:, 0:P], krd[0:P, :], self.ident_t[0:P, 0:P], [krdb, self.cb], [ptb])
                        self.cp(kT[:, blk * 128:blk * 128 + P], ptile[:, 0:P], [ptb], [kTb[blk]])
                        if tt_ == 7:
                            self.dma("sp", self.o_kwp[j, seg][:, g * 64:(g + 1) * 64], kf[i][:, :], [kfb[i]], ())
                            self.dma("sp", self.o_vwp[j, seg][:, g * 64:(g + 1) * 64], vf[i][:, :], [vfb[i]], ())
                            self.cp(self.halo_k[:, (j * NKV + g) * 128:(j * NKV + g + 1) * 128], kT[:, 8 * 128:9 * 128],
                                    [kTb[8]], [self.halob[j][g]])
                            self.cp(self.halo_v[:, (j * NKV + g) * 65:(j * NKV + g) * 65 + 64], vav[:, 8, 0:64],
                                    [vab[8]], [self.halob[j][g]])
                        if tt_ == 8:
                            for b in range(NSQ):
                                sq_ = seg * NSQ + b
                                self.dma("sp", self.o_kws[j, sq_, 120:128, g * 64:(g + 1) * 64], kf[i][b * 8:(b + 1) * 8, :], [kfb[i]], ())
                                self.dma("sp", self.o_vws[j, sq_, 120:128, g * 64:(g + 1) * 64], vf[i][b * 8:(b + 1) * 8, :], [vfb[i]], ())
                    dq.push(kv_post)
                dq.flush()
                ckv = ck_t[:].rearrange("p (b f) -> p b f", b=NSQ)
                cvv = cv_t[:].rearrange("p (b f) -> p b f", b=NSQ)
                self.dma("sp", ckv, self.d_ck[j, seg].rearrange("p (b f) -> p b f", b=NSQ)[:, :, g * 64:(g + 1) * 64], (), [ckb])
                self.dma("sp", cvv, self.d_cv[j, seg].rearrange("p (b f) -> p b f", b=NSQ)[:, :, g * 64:(g + 1) * 64], (), [ckb])
                for b in range(NSQ):
                    self.cp(ckd[:].rearrange("p (r d) -> p r d", r=2),
                            ckv[:, b, :].unsqueeze(1).to_broadcast([128, 2, HD]), [ckb], [ckdb])
                    ptile, ptb = self.ptile()
                    self.tr(ptile[:, 0:128], ckd[:], self.ident_t[:], [ckdb, self.cb], [ptb])
                    self.cp(ckT[:, b * 128:(b + 1) * 128], ptile[:, 0:128], [ptb], [ckTb], eng="act")
                self.cp(cvav[:, :, 0:64], cvv, [ckb], [cvab])
                NPT = len(PT)
                d1, d2, d3 = Defer(1), Defer(1), Defer(1)
                un = 0
                for i in range(8):
                    for par in range(2):
                        pp = slice(par * 64, (par + 1) * 64)
                        rhs = qTv[pp, :, i * 128:(i + 1) * 128]
                        bp = (un % 2) * 2
                        pts = [(2 * un) % NPT, (2 * un + 1) % NPT]
                        un += 1
                        for kbi, blk in enumerate((i, i + 1)):
                            self.mm(self.ps[bp + kbi][:, :], kT[pp, blk * 128:(blk + 1) * 128], rhs, True, True,
                                    [kTb[blk], qTb[i]], [self.psb[bp + kbi]])

                        def st1(i=i, par=par, bp=bp, pts=pts, g=g):
                            for kbi in range(2):
                                pb_ = pts[kbi]
                                self.act(PT[pb_][:], self.ps[bp + kbi][:, :], AF.Exp, [self.psb[bp + kbi]], [PTb[pb_]],
                                         scale=HD ** -0.5)
                                m = (mp0 if i == 0 else self.m_prev()) if kbi == 0 else self.m_cur()
                                mb = [self.mp0b] if (kbi == 0 and i == 0) else [self.cb]
                                self.tt(PT[pb_][:].rearrange("p (j t) -> p j t", j=4), PT[pb_][:].rearrange("p (j t) -> p j t", j=4),
                                        m, ALU.mult, [PTb[pb_]] + mb, [PTb[pb_]])

                            def st2():
                                po = 4 + par
                                for jj in range(4):
                                    for kbi, blk in enumerate((i, i + 1)):
                                        self.mm(self.ps[po][:, jj * 65:(jj + 1) * 65], PT[pts[kbi]][:, jj * 128:(jj + 1) * 128],
                                                vav[:, blk, :], kbi == 0, kbi == 1, [PTb[pts[kbi]], vab[blk]], [self.psb[po]])
                                pov = self.ps[po][:, 0:260].rearrange("p (j d) -> p j d", d=65)
                                self.tt(den[:, :], pov[:, :, 64], es_t[:, g * 8 + par:g * 8 + 8:2], ALU.add,
                                        [self.psb[po], self.attb], [denb])
                                self.recip(den[:, :], den[:, :], [denb], [denb])
                                oi = i % 2
                                ov = otm[oi][:].rearrange("p (j r d) -> p j r d", j=4, r=2)[:, :, par, :]
                                self.tt(ov, pov[:, :, 0:64], den[:, :].unsqueeze(2).to_broadcast([128, 4, HD]), ALU.mult,
                                        [self.psb[po], denb], [otmb[oi]])
                                if par == 1:
                                    def st3():
                                        ptile, ptb = self.ptile()
                                        for c in range(4):
                                            self.tr(ptile[:, c * 128:(c + 1) * 128], otm[oi][:, c * 128:(c + 1) * 128], self.ident_t[:],
                                                    [otmb[oi], self.cb], [ptb])
                                        self.cp(oTv[:, :, i * 128:(i + 1) * 128], ptile[:, 0:512].rearrange("p (c t) -> p c t", c=4),
                                                [ptb], [oTb[i // 4]], eng="act")
                                    d3.push(st3)
                            d2.push(st2)
                        d1.push(st1)
                d1.flush()
                d2.flush()
                d3.flush()
                for par in range(2):
                    pp = slice(par * 64, (par + 1) * 64)
                    rhs = qTv[pp, :, TP:T]
                    for b in range(NSQ + 1):
                        pi = 2 + (b % 2)
                        if b < NSQ:
                            Pk = 128
                            self.mm(self.ps[pi][:, 0:128], ckT[pp, b * 128:(b + 1) * 128], rhs, True, True,
                                    [ckTb, qTb[8]], [self.psb[pi]])
                            m = self.m_sprev(b)
                        else:
                            Pk = TS
                            self.mm(self.ps[pi][0:Pk, 0:128], kT[pp, 9 * 128:9 * 128 + TS], rhs, True, True,
                                    [kTb[9], qTb[8]], [self.psb[pi]])
                            m = self.m_scur()
                        self.act(PTs[b][0:Pk, :], self.ps[pi][0:Pk, 0:128], AF.Exp, [self.psb[pi]], [PTsb[b]], scale=HD ** -0.5)
                        self.tt(PTs[b][0:Pk, :].rearrange("p (j t) -> p j t", j=4),
                                PTs[b][0:Pk, :].rearrange("p (j t) -> p j t", j=4), m, ALU.mult,
                                [PTsb[b], self.cb], [PTsb[b]])
                    po = 4 + par
                    for jj in range(4):
                        for b in range(NSQ + 1):
                            if b < NSQ:
                                self.mm(self.ps[po][0:TS, jj * 65:(jj + 1) * 65], PTs[b][:, jj * 32:(jj + 1) * 32],
                                        cvav[:, b, :], b == 0, False, [PTsb[b], cvab], [self.psb[po]])
                            else:
                                self.mm(self.ps[po][0:TS, jj * 65:(jj + 1) * 65], PTs[b][0:TS, jj * 32:(jj + 1) * 32],
                                        vav[0:TS, 9, :], False, True, [PTsb[b], vab[9]], [self.psb[po]])
                    pov = self.ps[po][0:TS, 0:260].rearrange("p (j d) -> p j d", d=65)
                    self.tt(den[0:TS, :], pov[:, :, 64], es_t[0:TS, g * 8 + par:g * 8 + 8:2], ALU.add,
                            [self.psb[po], self.attb], [denb])
                    self.recip(den[0:TS, :], den[0:TS, :], [denb], [denb])
                    ov = otm[0][0:TS, :].rearrange("p (j r d) -> p j r d", j=4, r=2)[:, :, par, :]
                    self.tt(ov, pov[:, :, 0:64], den[0:TS, :].unsqueeze(2).to_broadcast([TS, 4, HD]), ALU.mult,
                            [self.psb[po], denb], [otmb[0]])
                ptile, ptb = self.ptile()
                for c in range(4):
                    self.tr(ptile[:, c * 128:c * 128 + TS], otm[0][0:TS, c * 128:(c + 1) * 128], self.ident_t[0:TS, 0:TS],
                            [otmb[0], self.cb], [ptb])
                self.cp(oTv[:, :, TP:T], ptile[:, 0:512].rearrange("p (c t) -> p c t", c=4)[:, :, 0:TS],
                        [ptb], [oTb[2]], eng="act")
                w, wb = self.wget([(wo_all[:, g * 4:(g + 1) * 4, :], 0, 2048)], 4, 2048)
                self.proj_resid(w, wb, 4, oTv, lambda kc, k: [oTb[k]], l, 2, seg, gtokv, gtokb, stmp[:], stmpb)
            self.fence(bufs)

    def hgrn_layer(self, l, seg):
        S = self.S
        j = l // 2
        tiles = [(i * HT, min(HT, TP - i * HT)) for i in range((TP + HT - 1) // HT)]
        NT = len(tiles)
        NCK = TP // CH
        with ExitStack() as ph:
            sb = lambda name, shape, dt: self.sb(ph, name, shape, dt)
            bufs = []

            def nb(name=""):
                b = S.buf(name)
                bufs.append(b)
                return b
            gtok = sb("g1tok", [128, DC * TS], F32)
            gtokv = gtok[:].rearrange("p (c t) -> p c t", c=DC)
            gtokb = nb()
            stmp = sb("stmp", [128, TS], F32)
            stmpb = nb()
            self.tokmod(gtokv, gtokb, l, 2, seg)
            qs = sb("qs", [128, T], BF16)
            A = sb("Aa", [128, T], F32)
            B = sb("Bb", [128, T], F32)
            C = sb("Cc", [128, T], F32)
            qsb, Ab_, Bb_, Cb_ = nb("qs"), nb("A"), nb("B"), nb("C")
            qt = sb("qt", [128, T], BF16)
            kt = sb("kt", [128, T], BF16)
            kh = sb("kh", [128, T], BF16)
            qtb, ktb, khb = nb("qt"), nb("kt"), nb("kh")
            qts = sb("qts", [128, NSQ * TS], BF16)
            khs = sb("khs", [128, NSQ * TS], BF16)
            qtsb, khsb = nb(), nb()
            Dch = sb("Dch", [128, NCK + NSQ], F32)
            Dchb = nb("Dch")
            vt = sb("vt", [128, (NT + 1) * 128], BF16)
            vtv = vt[:].rearrange("p (i d) -> p i d", d=128)
            vtb = nb("vt")
            sg = sb("sg", [128, (NT + 1) * 128], BF16)
            sgv = sg[:].rearrange("p (i d) -> p i d", d=128)
            sgb = nb("sg")
            kht = sb("kht", [128, NT * 128], BF16)
            khtv = kht[:].rearrange("p (i d) -> p i d", d=128)
            khtb = nb("kht")
            khts = sb("khts", [128, NSQ * 128], BF16)
            khtsv = khts[:].rearrange("p (i d) -> p i d", d=128)
            khtsb = nb("khts")
            NSB = 6
            Sb = sb("Sb", [128, NSB * 128], BF16)
            Sbv = Sb[:].rearrange("p (i d) -> p i d", d=128)
            Sbb = [nb("Sb%d" % i) for i in range(NSB)]
            St = sb("St", [128, 128], F32)
            Stb = nb("St")
            S0 = sb("S0", [128, NSQ * 128], F32)
            S0v = S0[:].rearrange("p (i d) -> p i d", d=128)
            S0b = [nb("S0_%d" % i) for i in range(NSQ)]
            S0h = sb("S0h", [128, NSQ * 128], BF16)
            S0hv = S0h[:].rearrange("p (i d) -> p i d", d=128)
            S0hb = nb("S0h")
            Am = [sb("Am%d" % i, [128, HT], BF16) for i in range(2)]
            Amb = [nb() for i in range(2)]
            og = [sb("og%d" % i, [128, 128], BF16) for i in range(2)]
            ogb = [nb() for i in range(2)]
            ss = sb("hss_", [128, 4], F32)
            ssb = nb()
            junk = sb("junk", [128, 128], F32)
            junkb = nb()
            oT = sb("oTr", [128, 4 * T], BF16)
            oTv = oT[:].rearrange("p (c t) -> p c t", c=4)
            oTb = [nb("oT%d" % k) for k in range(3)]
            ub = [nb("u%d" % i) for i in range(4)]
            ob4 = [nb("o%d" % i) for i in range(4)]
            bd = self.hmask_t[:, 0:128]
            reset = self.hmask_t[:, 128:128 + T]
            seqm = self.hmask_t[:, 128 + T:128 + T + 128].rearrange("p (b t) -> p b t", b=NSQ)
            bds = self.hmask_t[0:TS, 128 + T + 128:128 + T + 128 + TS]
            win = self.d_win[j].rearrange("(kc p) (g n) -> p kc g n", p=128, g=4)
            wo_all = self.d_wo_r[j].rearrange("(kc p) n -> p kc n", p=128)
            hspb = self.hspb
            nA = 0
            nO = 0
            nU = 0
            nog = 0
            dO = Defer(1)
            def fm_gen(w, wb, banks):
                n_ = 0
                for which, dst, dstb, func in ((0, qs, qsb, AF.Silu), (1, A, Ab_, AF.Sigmoid)):
                    for k, (t0, t1) in enumerate(TCH):
                        pi = banks[n_ % len(banks)]
                        n_ += 1
                        pso = self.ps[pi][:, 0:t1 - t0]
                        for kc in range(DC):
                            self.mm(pso, w[:, kc, which * 128:(which + 1) * 128], self.hv[:, kc, t0:t1], kc == 0, kc == DC - 1,
                                    [wb, self.hb[k]], [self.psb[pi]])
                        self.act(dst[:, t0:t1], pso, func, [self.psb[pi]], [dstb])
                        yield

            def get_w(hd_):
                return self.wget([(win[:, :, gi, hd_ * 128:(hd_ + 1) * 128], gi * 128, 128) for gi in range(4)], 16, 512)
            wnext = get_w(0)
            for _ in fm_gen(wnext[0], wnext[1], [0, 1, 2, 3, 4, 5]):
                pass
            for hd in range(self.nheads):
                w, wb = wnext
                def gates():
                    if j == 1:
                        self.ts(A[:], A[:], self.lb_t[:, 16 + hd:17 + hd], self.lb_t[:, hd:hd + 1], ALU.mult, ALU.add,
                                [Ab_, self.cb], [Ab_])
                        yield
                    self.act(B[:], A[:], AF.Ln, [Ab_], [Bb_])
                    yield
                    self.scan(C[:], reset, B[:], 0.0, [Bb_, self.cb], [Cb_])
                    yield
                    self.ts(A[:], A[:], -1.0, 1.0, ALU.mult, ALU.add, [Ab_], [Ab_])
                    yield
                    self.act(B[:], C[:], AF.Exp, [Cb_], [Bb_])
                    yield
                    self.tt(qt[:], qs[:], B[:], ALU.mult, [qsb, Bb_], [qtb])
                    yield
                    self.ts(B[:], C[:], -1.0, CLAMP, ALU.mult, ALU.min, [Cb_], [Bb_])
                    yield
                    self.act(B[:], B[:], AF.Exp, [Bb_], [Bb_])
                    yield
                    self.tt(kt[:], A[:], B[:], ALU.mult, [Ab_, Bb_], [ktb])
                    yield
                    Cp = C[:, 0:TP].rearrange("p (c i) -> p c i", i=CH)
                    Bp = B[:, 0:TP].rearrange("p (c i) -> p c i", i=CH)
                    self.tt(Bp, C[:, CH - 1:TP:CH].unsqueeze(2).to_broadcast([128, NCK, CH]), Cp, ALU.subtract, [Cb_], [Bb_])
                    yield
                    Cs = C[:, TP:T].rearrange("p (c i) -> p c i", i=DEC_SEQ)
                    Bs = B[:, TP:T].rearrange("p (c i) -> p c i", i=DEC_SEQ)
                    self.tt(Bs, C[:, TP + DEC_SEQ - 1:T:DEC_SEQ].unsqueeze(2).to_broadcast([128, NSQ, DEC_SEQ]), Cs, ALU.subtract, [Cb_], [Bb_])
                    yield
                    self.act(B[:], B[:], AF.Exp, [Bb_], [Bb_])
                    yield
                    self.tt(kh[:], A[:], B[:], ALU.mult, [Ab_, Bb_], [khb])
                    yield
                    self.act(Dch[:, 0:NCK], C[:, CH - 1:TP:CH], AF.Exp, [Cb_], [Dchb])
                    yield
                    self.act(Dch[:, NCK:NCK + NSQ], C[:, TP + DEC_SEQ - 1:T:DEC_SEQ], AF.Exp, [Cb_], [Dchb])
                    yield
                    self.tt(qts[:].rearrange("p (b t) -> p b t", b=NSQ), qt[:, TP:T].unsqueeze(1).to_broadcast([128, NSQ, TS]), seqm,
                            ALU.mult, [qtb, self.cb], [qtsb])
                    yield
                    self.tt(khs[:].rearrange("p (b t) -> p b t", b=NSQ), kh[:, TP:T].unsqueeze(1).to_broadcast([128, NSQ, TS]), seqm,
                            ALU.mult, [khb, self.cb], [khsb])
                    yield
                def tmtiles():
                    for which, dstv, dstb in ((2, vtv, vtb), (3, sgv, sgb)):
                        for ti in range(NT + 1):
                            t0, P = tiles[ti] if ti < NT else (TP, TS)
                            k = 2 if ti == NT else (0 if t0 + P <= 512 else 1)
                            hbs = [self.hb[k]] if (ti == NT or t0 >= 512 or t0 + P <= 512) else [self.hb[0], self.hb[1]]
                            pi = self.pn % 6
                            self.pn += 1
                            pso = self.ps[pi][0:P, 0:128]
                            for kc in range(DC):
                                self.mm(pso, self.hv[:, kc, t0:t0 + P], w[:, kc, which * 128:(which + 1) * 128], kc == 0, kc == DC - 1,
                                        [wb] + hbs, [self.psb[pi]])
                            if which == 2:
                                self.cp(dstv[0:P, ti, :], pso, [self.psb[pi]], [dstb])
                            else:
                                self.act(dstv[0:P, ti, :], pso, AF.Silu, [self.psb[pi]], [dstb])
                            yield
                ga, tmg = gates(), tmtiles()
                live = [ga, tmg]
                while live:
                    for gen in list(live):
                        try:
                            next(gen)
                        except StopIteration:
                            live.remove(gen)
                for ti in range(NT):
                    t0, P = tiles[ti]
                    ptile, ptb = self.ptile()
                    self.tr(ptile[0:P, 0:128], kh[:, t0:t0 + P], self.ident_t[:], [khb, self.cb], [ptb])
                    self.cp(khtv[0:P, ti, :], ptile[0:P, 0:128], [ptb], [khtb], eng="act")
                ptile, ptb = self.ptile()
                for b in range(NSQ):
                    self.tr(ptile[0:TS, b * 128:(b + 1) * 128], khs[:, b * TS:(b + 1) * TS], self.ident_t[:], [khsb, self.cb], [ptb])
                self.cp(khts[0:TS, :], ptile[0:TS, 0:NSQ * 128], [ptb], [khtsb], eng="act")
                fmg = None
                if hd + 1 < self.nheads:
                    wnext = get_w(hd + 1)
                    fmg = fm_gen(wnext[0], wnext[1], [3])

                def fm_step():
                    nonlocal fmg
                    if fmg is not None:
                        try:
                            next(fmg)
                        except StopIteration:
                            fmg = None
                if seg == 0:
                    self.mset(St[:], 0.0, [Stb])
                else:
                    self.dma("sp", St[:], self.o_hsp[j, seg - 1, hd], [hspb[j][hd]], [Stb])
                self.cp(Sbv[:, 0, :], St[:], [Stb], [Sbb[0]], eng="act")
                for c in range(NCK):
                    ti, a = divmod(c, HT // CH)
                    u = 4 + nU % 2
                    nU += 1
                    uo = self.ps[u][:, 0:128]
                    self.mm(uo, khtv[a * CH:(a + 1) * CH, ti, :], vtv[a * CH:(a + 1) * CH, ti, :], True, True,
                            [khtb, vtb], [self.psb[u]])
                    self.stt(St[:], St[:], Dch[:, c:c + 1], uo, ALU.mult, ALU.add, [Stb, Dchb, self.psb[u]], [Stb])
                    if c < NCK - 1:
                        self.cp(Sbv[:, (c + 1) % NSB, :], St[:], [Stb], [Sbb[(c + 1) % NSB]], eng="act")
                    if c % 5 == 1:
                        fm_step()
                    if a == HT // CH - 1 or c == NCK - 1:
                        t0, P = tiles[ti]
                        ai = nA % 2
                        nA += 1
                        pa = self.ps[0]
                        self.mm(pa[0:P, 0:P], kt[:, t0:t0 + P], qt[:, t0:t0 + P], True, True, [ktb, qtb], [self.psb[0]])
                        self.tt(Am[ai][0:P, 0:P], pa[0:P, 0:P], bd[0:P, 0:P], ALU.mult, [self.psb[0], self.cb], [Amb[ai]])
                        ob_ = 1 + ai
                        po = self.ps[ob_][:, 0:128]
                        nchk = P // CH
                        self.mm(po[0:P, :], Am[ai][0:P, 0:P], vtv[0:P, ti, :], True, False, [Amb[ai], vtb], [self.psb[ob_]])
                        for a2 in range(nchk):
                            c2 = ti * (HT // CH) + a2
                            self.mm(po[a2 * CH:(a2 + 1) * CH, :], qt[:, t0 + a2 * CH:t0 + (a2 + 1) * CH], Sbv[:, c2 % NSB, :],
                                    False, True, [qtb, Sbb[c2 % NSB]], [self.psb[ob_]])

                        def o_post(po=po, ob_=ob_, P=P, ti=ti, t0=t0, nog=nog, hd=hd):
                            self.hgrn_out(po[0:P, :], self.psb[ob_], P, sgv[0:P, ti, :], sgb, ss, ssb, junk, junkb, og, ogb, nog,
                                          oTv[:, hd % 4, t0:t0 + P], oTb, t0, P, j, hd)
                        dO.push(o_post)
                        nog += 1
                self.dma("sp", self.o_hsp[j, seg, hd], St[:], [Stb], [hspb[j][hd]])
                for b in range(NSQ):
                    self.dma("sp", S0v[:, b, :], self.d_st[j, seg * NSQ + b, hd], (), [S0b[b]])
                self.cp(S0h[:], S0[:], S0b, [S0hb], eng="act")
                dO.flush()
                while fmg is not None:
                    fm_step()
                ai = nA % 2
                nA += 1
                pa = self.ps[0]
                self.mm(pa[0:TS, 0:TS], kt[:, TP:T], qt[:, TP:T], True, True, [ktb, qtb], [self.psb[0]])
                self.tt(Am[ai][0:TS, 0:TS], pa[0:TS, 0:TS], bds, ALU.mult, [self.psb[0], self.cb], [Amb[ai]])
                ob_ = 1
                po = self.ps[ob_][:, 0:128]
                self.mm(po[0:TS, :], Am[ai][0:TS, 0:TS], vtv[0:TS, NT, :], True, False, [Amb[ai], vtb], [self.psb[ob_]])
                for b in range(NSQ):
                    self.mm(po[0:TS, :], qts[:, b * TS:(b + 1) * TS], S0hv[:, b, :], False, b == NSQ - 1, [qtsb, S0hb], [self.psb[ob_]])
                self.hgrn_out(po[0:TS, :], self.psb[ob_], TS, sgv[0:TS, NT, :], sgb, ss, ssb, junk, junkb, og, ogb, nog,
                              oTv[:, hd % 4, TP:T], oTb, TP, TS, j, hd)
                nog += 1
                for b in range(NSQ):
                    u = 4 + nU % 2
                    nU += 1
                    uo = self.ps[u][:, 0:128]
                    self.mm(uo, khtsv[0:TS, b, :], vtv[0:TS, NT, :], True, True, [khtsb, vtb], [self.psb[u]])
                    self.stt(S0v[:, b, :], S0v[:, b, :], Dch[:, NCK + b:NCK + b + 1], uo, ALU.mult, ALU.add,
                             [S0b[b], Dchb, self.psb[u], S0hb], [S0b[b]])
                    self.dma("sp", self.o_hss[j, seg * NSQ + b, hd], S0v[:, b, :], [S0b[b]], ())
                if hd % 4 == 3:
                    g = hd // 4
                    w2, wb2 = self.wget([(wo_all[:, g * 4:(g + 1) * 4, :], 0, 2048)], 4, 2048)
                    self.proj_resid(w2, wb2, 4, oTv, lambda kc, k: [oTb[k]], l, 2, seg, gtokv, gtokb, stmp[:], stmpb)
            self.fence(bufs)

    def hgrn_out(self, po, pob, P, sgt, sgb, ss, ssb, junk, junkb, og, ogb, n, oT_dst, oTb, t0, P2, j, hd):
        i = n % 2
        col = n % 2
        self.act(junk[0:P, :], po, AF.Square, [pob], [junkb], accum_out=ss[0:P, col:col + 1])
        self.rsqrt(ss[0:P, 2 + col:3 + col], ss[0:P, col:col + 1], 1.0 / 128.0, EPS, [junkb], [ssb], junk[0:P, 0:1], junkb)
        self.stt(og[i][0:P, :], po, ss[0:P, 2 + col:3 + col], sgt, ALU.mult, ALU.mult, [pob, ssb, sgb], [ogb[i]])
        ptile, ptb = self.ptile()
        self.tr(ptile[:, 0:P], og[i][0:P, :], self.ident_t[0:P, 0:P], [ogb[i], self.cb], [ptb])
        ks = sorted(set([min(t0 // 512, 2), min((t0 + P - 1) // 512, 2)]))
        self.ts(oT_dst, ptile[:, 0:P], self.og_t[:, j * RH + hd:j * RH + hd + 1], None, ALU.mult, None,
                [ptb, self.cb], [oTb[k] for k in ks])

    def segment(self, seg):
        xT = self.d_xT[seg].rearrange("(c p) t -> p c t", p=128)
        for c in range(DC):
            self.dma("sp", self.xv[:, c, :], xT[:, c, :], (), self.xb[c])
        for l in self.layers:
            self.norm_mod(l, 0, seg)
            if l % 2 == 0:
                self.attn_layer(l, seg)
            else:
                self.hgrn_layer(l, seg)
            if self.do_mlp:
                self.norm_mod(l, 1, seg)
                self.mlp(l, seg)
        yT = self.o_yT[seg].rearrange("(c p) t -> p c t", p=128)
        for c in range(DC):
            self.dma("sp", yT[:, c, :], self.xv[:, c, :], self.xb[c], ())


def _rope_tables():
    half = ROT // 2
    inv = (np.float32(THETA) ** (-np.arange(half, dtype=np.float32) / np.float32(half))).astype(np.float32)
    out = np.zeros((NSEG, 128, 9, 16), np.float32)
    for seg in range(NSEG):
        for tt in range(9):
            if tt < 8:
                pos = (seg * TP + tt * 128 + np.arange(128)).astype(np.float32)
            else:
                pos = (PAST_LEN + (np.arange(128) % DEC_SEQ)).astype(np.float32)
            ang = pos[:, None] * inv[None, :]
            out[seg, :, tt, 0:8] = np.cos(ang)
            out[seg, :, tt, 8:16] = np.sin(ang)
    return out.reshape(NSEG, 128, 9 * 16)


def _masks():
    m = np.zeros((128, 1792), np.float32)
    s = np.arange(128)[:, None]
    t = np.arange(128)[None, :]
    cur = (s <= t).astype(np.float32)
    prev = (s > t).astype(np.float32)
    m[:, 0:512] = np.tile(cur, (1, 4))
    m[:, 512:1024] = np.tile(prev, (1, 4))
    ts_ = np.arange(TS)[None, :]
    for b in range(NSQ):
        mm_ = ((ts_ // DEC_SEQ == b) & (s > (ts_ % DEC_SEQ))).astype(np.float32)
        m[:, 1024 + b * 128:1024 + (b + 1) * 128] = np.tile(mm_, (1, 4))
    s32 = np.arange(TS)[:, None]
    sc = ((s32 // DEC_SEQ == ts_ // DEC_SEQ) & (s32 % DEC_SEQ <= ts_ % DEC_SEQ)).astype(np.float32)
    m[0:TS, 1536:1664] = np.tile(sc, (1, 4))
    m[:, 1664:1792] = np.eye(128, dtype=np.float32)
    hm = np.zeros((128, 128 + T + 128 + TS), np.float32)
    hm[:, 0:128] = ((s // CH == t // CH) & (s <= t)).astype(np.float32)
    reset = np.ones(T, np.float32)
    reset[0:TP:CH] = 0.0
    reset[TP:T:DEC_SEQ] = 0.0
    hm[:, 128:128 + T] = reset[None, :]
    sq = np.zeros((NSQ, TS), np.float32)
    for b in range(NSQ):
        sq[b, b * DEC_SEQ:(b + 1) * DEC_SEQ] = 1.0
    hm[:, 128 + T:128 + T + 128] = sq.reshape(1, -1)
    hm[0:TS, 128 + T + 128:] = ((s32 // DEC_SEQ == ts_ // DEC_SEQ) & (s32 <= ts_)).astype(np.float32)
    return m, hm


def _core_assign(core):
    b = core % BATCH
    return b, [b * 8 + i for i in range(NSEG * NSQ)]


_PROG_CACHE = {}


def _get_prog(nlayers):
    if nlayers not in _PROG_CACHE:
        p = Prog(nlayers=nlayers)
        p.hspb = [[Buf("hsp%d_%d" % (j, h)) for h in range(RH)] for j in range(2)]
        p.build()
        _PROG_CACHE[nlayers] = p
    return _PROG_CACHE[nlayers]


def kernel(x_prompt, x_sample, cache_k_win, cache_v_win, state_hgrn, c_prompt, c_sample,
           norm_gain, w_ada, b_ada, attn_w_qkv, attn_q_gain, attn_k_gain, attn_sinks, attn_w_o,
           rec_w_in, rec_lb_logits, rec_o_gain, rec_w_o, mlp_w_up, mlp_w_down, _nlayers=DEPTH, _trace=False):
    f = lambda a: np.ascontiguousarray(np.asarray(a, dtype=np.float32))
    x_prompt, x_sample = f(x_prompt), f(x_sample)
    cache_k_win, cache_v_win, state_hgrn = f(cache_k_win), f(cache_v_win), f(state_hgrn)
    c_prompt, c_sample = f(c_prompt), f(c_sample)
    masks, hmask = _masks()
    rope = _rope_tables()
    shared = {
        "masks": masks, "hmask": hmask, "rope": rope,
        "gainT": f(np.asarray(norm_gain).reshape(DEPTH, 2, DC, 128).transpose(3, 0, 1, 2).reshape(128, -1)),
        "badaT": f(np.asarray(b_ada).reshape(DEPTH, 96, 128).transpose(2, 0, 1).reshape(128, -1)),
        "qkg": f(np.stack([np.asarray(attn_q_gain), np.asarray(attn_k_gain)], axis=1)),
        "sinks": f(attn_sinks),
        "lbT": f(np.asarray(rec_lb_logits).reshape(2, RH, 128).transpose(2, 0, 1).reshape(128, -1)),
        "ogT": f(np.asarray(rec_o_gain).reshape(2, RH, 128).transpose(2, 0, 1).reshape(128, -1)),
        "w_ada": f(w_ada), "attn_w_qkv": f(attn_w_qkv), "attn_w_o": f(attn_w_o), "rec_w_in": f(rec_w_in),
        "rec_w_o": f(rec_w_o), "mlp_w_up": f(mlp_w_up), "mlp_w_down": f(mlp_w_down),
    }
    hflag = np.zeros((NSEG, 128, 1), np.float32)
    hflag[1:] = 1.0
    in_maps = []
    for core in range(NCORES):
        b, sq = _core_assign(core)
        xT = np.empty((NSEG, D, T), np.float32)
        for seg in range(NSEG):
            xT[seg, :, 0:TP] = x_prompt[b, seg * TP:(seg + 1) * TP, :].T
            xs = x_sample[sq[seg * NSQ:(seg + 1) * NSQ]].reshape(TS, D)
            xT[seg, :, TP:T] = xs.T
        cs = np.concatenate([c_prompt[b:b + 1], c_sample[sq]], axis=0)
        cT = cs.reshape(NSEQ, DC, 128).transpose(2, 1, 0).reshape(128, DC * NSEQ)
        ck = cache_k_win[:, sq].reshape(2, NSEG, NSQ, 128, 256).transpose(0, 1, 3, 2, 4).reshape(2, NSEG, 128, NSQ * 256)
        cv = cache_v_win[:, sq].reshape(2, NSEG, NSQ, 128, 256).transpose(0, 1, 3, 2, 4).reshape(2, NSEG, 128, NSQ * 256)
        m = dict(shared)
        m.update({"xT": xT, "cT": f(cT), "ck": f(ck), "cv": f(cv), "st": f(state_hgrn[:, sq]), "hflag": hflag})
        in_maps.append(m)
    prog = _get_prog(_nlayers)
    res = run_bass_kernel_spmd(prog.nc, in_maps, core_ids=list(range(NCORES)), trace=_trace)
    R = res.results
    y_prompt = np.empty((BATCH, SEQ, D), np.float32)
    y_sample = np.empty((DEC_BATCH, DEC_SEQ, D), np.float32)
    kwp = np.empty((2, BATCH, 128, NKV, HD), np.float32)
    vwp = np.empty_like(kwp)
    kws = np.empty((2, DEC_BATCH, 128, NKV, HD), np.float32)
    vws = np.empty_like(kws)
    hsp = np.empty((2, BATCH, RH, 128, 128), np.float32)
    hss = np.empty((2, DEC_BATCH, RH, 128, 128), np.float32)
    for core in range(BATCH):
        b, sq = _core_assign(core)
        r = R[core]
        for seg in range(NSEG):
            y_prompt[b, seg * TP:(seg + 1) * TP, :] = r["yT"][seg][:, 0:TP].T
            ys = r["yT"][seg][:, TP:T].T.reshape(NSQ, DEC_SEQ, D)
            y_sample[sq[seg * NSQ:(seg + 1) * NSQ]] = ys
        kwp[:, b] = r["kwp"][:, NSEG - 1].reshape(2, 128, NKV, HD)
        vwp[:, b] = r["vwp"][:, NSEG - 1].reshape(2, 128, NKV, HD)
        kws[:, sq] = r["kws"].reshape(2, NSEG * NSQ, 128, NKV, HD)
        vws[:, sq] = r["vws"].reshape(2, NSEG * NSQ, 128, NKV, HD)
        hsp[:, b] = r["hsp"][:, NSEG - 1]
        hss[:, sq] = r["hss"]
    if _trace:
        kernel.last_exec_ns = res.exec_time_ns
    return (y_prompt, y_sample, kwp, vwp, kws, vws, hsp, hss)
```

```python
import math
import numpy as np
from contextlib import ExitStack
import concourse.bass as bass
import concourse.mybir as mybir
from concourse.bass_utils import run_bass_kernel_spmd

F32 = mybir.dt.float32
BF16 = mybir.dt.bfloat16
AF = mybir.ActivationFunctionType
ALU = mybir.AluOpType
AX = mybir.AxisListType

D = 2048
DC = 16
DEPTH = 4
BATCH = 4
SEQ = 2048
DEC_BATCH = 32
DEC_SEQ = 8
PAST_LEN = 16384
HD = 64
NQH = 32
NKV = 4
WIN = 128
ROT = 16
THETA = 500000.0
RH = 16
DFF = 8192
EPS = 1e-6

NCORES = 8
NSEG = 2
TP = 1024
NSQ = 4
TS = NSQ * DEC_SEQ
T = TP + TS
NSEQ = 1 + NSEG * NSQ
TCH = [(0, 512), (512, 1024), (1024, T)]
FB = 1024
CH = 32
HT = 96
CLAMP = 70.0

ENGS = ("pe", "act", "dve", "pool", "sp")


class Buf:
    __slots__ = ("name", "w", "r")

    def __init__(self, name="", w=None):
        self.name = name
        self.w = w
        self.r = []


class DmaSem:
    __slots__ = ("sem", "count")

    def __init__(self, sem):
        self.sem = sem
        self.count = 0


class Op:
    __slots__ = ("eng", "idx", "fn", "waits", "signal", "count", "dsem", "dcount")

    def __init__(self, eng, idx, fn):
        self.eng = eng
        self.idx = idx
        self.fn = fn
        self.waits = []
        self.signal = False
        self.count = 0
        self.dsem = None
        self.dcount = 0


class Sched:
    def __init__(self, nc, stack):
        self.nc = nc
        self.stack = stack
        self.ops = {e: [] for e in ENGS}
        self.sems = {e: stack.enter_context(nc.semaphore("s_" + e)) for e in ENGS}
        self.wm = {a: {b: -1 for b in ENGS} for a in ENGS}
        self.dwm = {a: {} for a in ENGS}
        self.nsem = 0
        self.fence = None
        self.pools = {}
        self.pool_i = {}
        self.all_dsems = []

    def make_pool(self, eng, n):
        self.pools[eng] = [self.dma_sem() for _ in range(n)]
        self.pool_i[eng] = 0

    def dma_sem(self, name=None):
        self.nsem += 1
        s = self.stack.enter_context(self.nc.semaphore(name or ("dq%d" % self.nsem)))
        d = DmaSem(s)
        self.all_dsems.append(d)
        return d

    def buf(self, name=""):
        return Buf(name, self.fence)

    def op(self, eng, fn, reads=(), writes=(), dsem=None):
        y = Op(eng, len(self.ops[eng]), fn)
        a = eng
        best = {}
        dbest = {}
        cands = []
        if dsem == "auto":
            pl = self.pools[eng]
            dsem = pl[self.pool_i[eng] % len(pl)]
            self.pool_i[eng] += 1
            if dsem.count > 0 and self.dwm[a].get(id(dsem), 0) < dsem.count:
                self.dwm[a][id(dsem)] = dsem.count
                y.waits.append((dsem, dsem.count))
        for b in reads:
            cands.append(b.w)
        for b in writes:
            cands.append(b.w)
            cands.extend(b.r)
        for x in cands:
            if x is None:
                continue
            if x.dsem is not None:
                if x.dsem is dsem:
                    continue
                key = id(x.dsem)
                if key not in dbest or dbest[key].dcount < x.dcount:
                    dbest[key] = x
            else:
                if x.eng == "pe" and a == "pe":
                    continue
                if x.eng not in best or best[x.eng].idx < x.idx:
                    best[x.eng] = x
        for key, x in dbest.items():
            if self.dwm[a].get(key, 0) >= x.dcount:
                continue
            self.dwm[a][key] = x.dcount
            y.waits.append((x.dsem, x.dcount))
        for b_, x in best.items():
            if self.wm[a][b_] >= x.idx:
                continue
            self.wm[a][b_] = x.idx
            y.waits.append(x)
        for b in reads:
            b.r.append(y)
        for b in writes:
            b.w = y
            b.r = []
        if dsem is not None:
            dsem.count += 16
            y.dsem = dsem
            y.dcount = dsem.count
        self.ops[eng].append(y)
        return y

    def emit(self, final_waits=()):
        nc = self.nc
        for e in ENGS:
            for y in self.ops[e]:
                for w in y.waits:
                    if isinstance(w, Op):
                        w.signal = True
        for e in ENGS:
            c = 0
            for y in self.ops[e]:
                if y.signal:
                    c += 1
                    y.count = c
        sems = self.sems
        ops = self.ops
        self.stats = {e: (len(ops[e]), sum(1 for y in ops[e] if y.signal),
                          sum(len(y.waits) for y in ops[e])) for e in ENGS}

        def run(e, h):
            for y in ops[e]:
                for w in y.waits:
                    if isinstance(w, Op):
                        h.wait_ge(sems[w.eng], w.count)
                    else:
                        h.wait_ge(w[0].sem, w[1])
                ins = y.fn(h)
                if y.dsem is not None:
                    ins.then_inc(y.dsem.sem, 16)
                elif y.signal:
                    ins.then_inc(sems[e], 1)
            if e == "sp":
                for d in self.all_dsems:
                    if d.count > 0:
                        h.wait_ge(d.sem, d.count)

        with nc.Block() as block:
            @block.tensor
            def _(h):
                run("pe", h)

            @block.scalar
            def _(h):
                run("act", h)

            @block.vector
            def _(h):
                run("dve", h)

            @block.gpsimd
            def _(h):
                run("pool", h)

            @block.sync
            def _(h):
                run("sp", h)


class Defer:
    def __init__(self, lag):
        self.q = []
        self.lag = lag

    def push(self, fn):
        self.q.append(fn)
        while len(self.q) > self.lag:
            self.q.pop(0)()

    def flush(self):
        while self.q:
            self.q.pop(0)()


class Prog:
    def __init__(self, nlayers=DEPTH, nseg=NSEG, layers=None, do_mlp=True, nheads=RH):
        self.nlayers = nlayers
        self.nseg = nseg
        self.layers = list(range(nlayers)) if layers is None else layers
        self.do_mlp = do_mlp
        self.nheads = nheads
        self.nc = bass.Bass("TRN2", target_bir_lowering=False)
        self.out_sems = []

    def mm(self, out, lhsT, rhs, start, stop, r, w):
        self.S.op("pe", lambda e: e.matmul(out, lhsT, rhs, start=start, stop=stop), r, w)

    def tr(self, out, in_, ident, r, w):
        self.S.op("pe", lambda e: e.transpose(out, in_, ident), r, w)

    def act(self, out, in_, func, r, w, **kw):
        self.S.op("act", lambda e: e.activation(out, in_, func, **kw), r, w)

    def tt(self, out, in0, in1, op, r, w, eng="dve"):
        self.S.op(eng, lambda e: e.tensor_tensor(out, in0, in1, op), r, w)

    def ts(self, out, in0, s1, s2, op0, op1, r, w, eng="dve"):
        if op1 is None:
            self.S.op(eng, lambda e: e.tensor_scalar(out, in0, s1, None, op0), r, w)
        else:
            self.S.op(eng, lambda e: e.tensor_scalar(out, in0, s1, s2, op0, op1), r, w)

    def stt(self, out, in0, scalar, in1, op0, op1, r, w, eng="dve"):
        self.S.op(eng, lambda e: e.scalar_tensor_tensor(out, in0, scalar, in1, op0, op1), r, w)

    def cp(self, out, in_, r, w, eng="dve"):
        if eng == "act":
            self.S.op("act", lambda e: e.activation(out, in_, AF.Copy), r, w)
        else:
            self.S.op(eng, lambda e: e.tensor_copy(out, in_), r, w)

    def red(self, out, in_, r, w):
        self.S.op("dve", lambda e: e.reduce_sum(out, in_, axis=AX.X), r, w)

    def recip(self, out, in_, r, w):
        self.S.op("dve", lambda e: e.reciprocal(out, in_), r, w)

    def scan(self, out, d0, d1, init, r, w):
        self.S.op("dve", lambda e: e.tensor_tensor_scan(out, d0, d1, init, ALU.mult, ALU.add), r, w)

    def mset(self, ap, val, w, eng="dve"):
        self.S.op(eng, lambda e: e.memset(ap, val), (), w)

    def dma(self, eng, out, in_, r, w, dsem="auto"):
        self.S.op(eng, lambda e: e.dma_start(out=out, in_=in_), r, w, dsem)

    def sb(self, st, name, shape, dt):
        self.nsb = getattr(self, "nsb", 0) + 1
        return st.enter_context(self.nc.sbuf_tensor("%s_%d" % (name, self.nsb), shape, dt))

    def fence(self, bufs):
        op = self.S.op("dve", lambda e: e.memset(self.fence_t[:, 0:1], 0.0), (), list(bufs) + [self.fence_b] + self.psb + self.ptb)
        self.S.fence = op

    def rsqrt(self, out, in_, scale, bias, r, w, tmp, tmpb):
        self.act(tmp, in_, AF.Ln, r, [tmpb], scale=scale, bias=bias)
        self.act(out, tmp, AF.Exp, [tmpb], w, scale=-0.5)

    def wget(self, srcs, kc, ncols):
        i = self.wn % len(self.wr_t)
        self.wn += 1
        view = self.wr_t[i][:, 0:kc * ncols].rearrange("p (k n) -> p k n", k=kc)
        for src, c0, n in srcs:
            self.dma("pool", view[:, :, c0:c0 + n], src, (), [self.wr_b[i]], self.wr_s[i])
        return view, self.wr_b[i]

    def build(self):
        nc = self.nc
        L = self.nlayers
        NS = self.nseg
        di = lambda name, shape: nc.dram_tensor(name, list(shape), F32, kind="ExternalInput").ap()
        do = lambda name, shape: nc.dram_tensor(name, list(shape), F32, kind="ExternalOutput").ap()
        self.d_xT = di("xT", [NS, D, T])
        self.d_cT = di("cT", [128, DC * NSEQ])
        self.d_rope = di("rope", [NS, 128, 9 * 16])
        self.d_ck = di("ck", [2, NS, 128, NSQ * 256])
        self.d_cv = di("cv", [2, NS, 128, NSQ * 256])
        self.d_st = di("st", [2, NS * NSQ, RH, 128, 128])
        self.d_hflag = di("hflag", [NS, 128, 1])
        self.d_masks = di("masks", [128, 1792])
        self.d_hmask = di("hmask", [128, 128 + T + 4 * 32 + 32])
        self.d_gainT = di("gainT", [128, DEPTH * 2 * DC])
        self.d_badaT = di("badaT", [128, DEPTH * 96])
        self.d_qkg = di("qkg", [2, 2, HD])
        self.d_sinks = di("sinks", [2, NQH])
        self.d_lbT = di("lbT", [128, 2 * RH])
        self.d_ogT = di("ogT", [128, 2 * RH])
        self.d_wada = di("w_ada", [DEPTH, D, 6 * D])
        self.d_wqkv = di("attn_w_qkv", [2, D, 2560])
        self.d_wo_a = di("attn_w_o", [2, D, D])
        self.d_win = di("rec_w_in", [2, D, 4 * D])
        self.d_wo_r = di("rec_w_o", [2, D, D])
        self.d_wup = di("mlp_w_up", [DEPTH, D, DFF])
        self.d_wdn = di("mlp_w_down", [DEPTH, DFF, D])
        self.o_yT = do("yT", [NS, D, T])
        self.o_kwp = do("kwp", [2, NS, 128, 256])
        self.o_vwp = do("vwp", [2, NS, 128, 256])
        self.o_kws = do("kws", [2, NS * NSQ, 128, 256])
        self.o_vws = do("vws", [2, NS * NSQ, 128, 256])
        self.o_hsp = do("hsp", [2, NS, RH, 128, 128])
        self.o_hss = do("hss", [2, NS * NSQ, RH, 128, 128])

        with ExitStack() as st:
            self.S = S = Sched(nc, st)
            sb = lambda name, shape, dt: self.sb(st, name, shape, dt)
            self.x_t = sb("x_t", [128, DC * T], F32)
            self.xv = self.x_t[:].rearrange("p (c t) -> p c t", c=DC)
            self.xb = [[Buf("x%d_%d" % (c, k)) for k in range(3)] for c in range(DC)]
            self.h_t = sb("h_t", [128, DC * T], BF16)
            self.hv = self.h_t[:].rearrange("p (c t) -> p c t", c=DC)
            self.hb = [Buf("h%d" % k) for k in range(3)]
            self.wr_t = [sb("wr%d" % i, [128, 8192], BF16) for i in range(2)]
            self.wr_b = [Buf("wr%d" % i) for i in range(2)]
            self.wr_s = [S.dma_sem() for i in range(2)]
            self.wn = 0
            self.mod_t = sb("mod_t", [128, DEPTH * 6 * DC * NSEQ], F32)
            self.modv = self.mod_t[:].rearrange("p (l j c s) -> p l j c s", l=DEPTH, j=6, c=DC)
            self.modb = [Buf("mod%d" % l_) for l_ in range(DEPTH)]
            self.fence_t = sb("fence_t", [128, 2], F32)
            self.fence_b = Buf("fence")
            self.ones_t = sb("ones_t", [128, 128], BF16)
            self.ident_t = sb("ident_t", [128, 128], BF16)
            self.masks_t = sb("masks_t", [128, 1792], BF16)
            self.hmask_t = sb("hmask_t", [128, 128 + T + 4 * 32 + 32], BF16)
            self.rope_t = sb("rope_t", [128, 9 * 16], F32)
            self.ropeb = Buf("rope")
            self.hflag_t = sb("hflag_t", [128, 1], F32)
            self.mp0_t = sb("mp0_t", [128, 512], BF16)
            self.mp0b = Buf("mp0")
            self.cb = Buf("consts")
            self.gain_t = sb("gain_t", [128, DEPTH * 2 * DC], F32)
            self.bada_t = sb("bada_t", [128, DEPTH * 96], F32)
            self.sc_t = sb("sc_t", [128, DC * NSEQ], BF16)
            self.gbb = Buf("gb")
            self.ada_s = [S.dma_sem() for i in range(2)]
            self.adan = 0
            self.lb_t = sb("lb_t", [128, 2 * RH], F32)
            self.og_t = sb("og_t", [128, 2 * RH], F32)
            self.halo_k = sb("halo_k", [128, 2 * NKV * 128], BF16)
            self.halo_v = sb("halo_v", [128, 2 * NKV * 65], BF16)
            self.halob = [[Buf("halo%d_%d" % (j, g)) for g in range(NKV)] for j in range(2)]
            self.ps = [st.enter_context(nc.psum_tensor("ps%d" % i, [128, 512], F32)) for i in range(6)]
            self.psb = [Buf("ps%d" % i) for i in range(6)]
            self.pt = [st.enter_context(nc.psum_tensor("pt%d" % i, [128, 1024], BF16)) for i in range(2)]
            self.ptb = [Buf("pt%d" % i) for i in range(2)]
            self.ptn = 0
            S.make_pool("sp", 20)
            S.make_pool("pool", 6)
            self.pn = 0

            self.load_consts(st)
            self.compute_mod(st)
            for seg in range(self.nseg):
                self.segment(seg)
            S.emit()
        return nc

    def load_consts(self, st):
        S = self.S
        self.dma("pool", self.masks_t[:], self.d_masks, (), [self.cb])
        self.dma("pool", self.hmask_t[:], self.d_hmask, (), [self.cb])
        self.dma("sp", self.og_t[:], self.d_ogT, (), [self.cb])
        self.dma("sp", self.lb_t[:], self.d_lbT, (), [self.cb])
        self.mset(self.ones_t[:], 1.0, [self.cb])
        self.cp(self.ident_t[:], self.masks_t[:, 1664:1792], [self.cb], [self.cb])
        lb = self.lb_t
        self.tt(lb[:, 0:16], lb[:, 16:32], lb[:, 0:16], ALU.subtract, [self.cb], [self.cb])
        self.act(lb[:, 0:16], lb[:, 0:16], AF.Sigmoid, [self.cb], [self.cb])
        self.ts(lb[:, 16:32], lb[:, 0:16], -1.0, 1.0, ALU.mult, ALU.add, [self.cb], [self.cb])
        self.mset(self.halo_k[:], 0.0, [b for row in self.halob for b in row])
        self.mset(self.halo_v[:], 0.0, [b for row in self.halob for b in row])

    def m_cur(self):
        return self.masks_t[:, 0:512].rearrange("p (j t) -> p j t", j=4)

    def m_prev(self):
        return self.masks_t[:, 512:1024].rearrange("p (j t) -> p j t", j=4)

    def m_sprev(self, b):
        return self.masks_t[:, 1024 + b * 128:1024 + (b + 1) * 128].rearrange("p (j t) -> p j t", j=4)

    def m_scur(self):
        return self.masks_t[0:32, 1536:1664].rearrange("p (j t) -> p j t", j=4)

    def compute_mod(self, st):
        S = self.S
        with ExitStack() as ph:
            cT = self.sb(ph, "cT", [128, DC * NSEQ], F32)
            cTb = S.buf("cT")
            self.dma("sp", cT[:], self.d_cT, (), [cTb])
            self.dma("sp", self.gain_t[:], self.d_gainT, (), [self.gbb])
            self.dma("sp", self.bada_t[:], self.d_badaT, (), [self.gbb])
            self.act(self.sc_t[:], cT[:], AF.Silu, [cTb], [self.gbb])
            wt = [self.sb(ph, "adaw%d" % i, [128, 4096], BF16) for i in range(2)]
            wtb = [S.buf() for i in range(2)]
            for _ in self.ada_gen(self.layers[0], wt, wtb, [4, 5]):
                pass
            self.fence([cTb] + wtb)

    def ada_gen(self, l, wt, wtb, banks):
        S = self.S
        scv = self.sc_t[:].rearrange("p (c s) -> p c s", c=DC)
        wv = self.d_wada[l].rearrange("(kc p) n -> p kc n", p=128)
        gbb = self.gbb
        for nt in range(48):
            i = self.adan % 2
            self.adan += 1
            w = wt[i][:].rearrange("p (k n) -> p k n", k=DC)
            self.dma("pool", w, wv[:, :, nt * 256:(nt + 1) * 256], (), [wtb[i]], self.ada_s[i])
            for q in range(2):
                nch = nt * 2 + q
                j, c = divmod(nch, DC)
                pi = banks[j % len(banks)]
                pso = self.ps[pi][:, c * NSEQ:(c + 1) * NSEQ]
                for kc in range(DC):
                    self.mm(pso, w[:, kc, q * 128:(q + 1) * 128], scv[:, kc, :], kc == 0, kc == DC - 1,
                            [wtb[i], gbb], [self.psb[pi]])
                if c == DC - 1:
                    bias = self.bada_t[:, l * 96 + j * 16:l * 96 + (j + 1) * 16].unsqueeze(2).to_broadcast([128, DC, NSEQ])
                    self.tt(self.modv[:, l, j], self.ps[pi][:, 0:DC * NSEQ].rearrange("p (c s) -> p c s", c=DC),
                            bias, ALU.add, [self.psb[pi], gbb], [self.modb[l]])
            yield
        for jn, j in ((0, 1), (1, 4)):
            g = self.gain_t[:, (l * 2 + jn) * DC:(l * 2 + jn + 1) * DC].unsqueeze(2).to_broadcast([128, DC, NSEQ])
            self.ts(self.modv[:, l, j], self.modv[:, l, j], 1.0, math.sqrt(D), ALU.add, ALU.mult,
                    [self.modb[l]], [self.modb[l]])
            self.tt(self.modv[:, l, j], self.modv[:, l, j], g, ALU.mult, [self.modb[l], gbb], [self.modb[l]])
        yield

    def tokmod(self, dst, dstb, l, j, seg):
        src = self.modv[:, l, j, :, 1 + seg * NSQ:1 + (seg + 1) * NSQ].unsqueeze(3).to_broadcast([128, DC, NSQ, DEC_SEQ])
        self.cp(dst.rearrange("p c (b i) -> p c b i", b=NSQ), src, [self.modb[l]], [dstb])

    def norm_mod(self, l, which, seg):
        S = self.S
        js, jh = (1, 0) if which == 0 else (4, 3)
        with ExitStack() as ph:
            sq = self.sb(ph, "sq", [128, DC * 512], BF16)
            sqb = S.buf("sq")
            rs = self.sb(ph, "rs", [128, 512], F32)
            rsb = S.buf("rs")
            ln = self.sb(ph, "ln", [128, 512], F32)
            lnb = S.buf("ln")
            tmp = [self.sb(ph, "ntmp%d" % i, [128, 512], F32) for i in range(2)]
            tmpb = [S.buf("ntmp%d" % i) for i in range(2)]
            gst = self.sb(ph, "gst", [128, DC * TS], F32)
            sht = self.sb(ph, "sht", [128, DC * TS], F32)
            big = self.sb(ph, "nbig", [128, DC * TS], F32)
            gsb, shb, bigb = S.buf(), S.buf(), S.buf()
            gsv = gst[:].rearrange("p (c t) -> p c t", c=DC)
            shv = sht[:].rearrange("p (c t) -> p c t", c=DC)
            bigv = big[:].rearrange("p (c t) -> p c t", c=DC)
            self.tokmod(gsv, gsb, l, js, seg)
            self.tokmod(shv, shb, l, jh, seg)
            n = 0
            for k, (t0, t1) in enumerate(TCH):
                W = t1 - t0
                xr = [self.xb[c][k] for c in range(DC)]
                sqv = sq[:, 0:DC * W].rearrange("p (c t) -> p c t", c=DC)
                self.act(sqv, self.xv[:, :, t0:t1], AF.Square, xr, [sqb])
                pi = 5
                for c in range(DC):
                    self.mm(self.ps[pi][:, 0:W], self.ones_t[:], sqv[:, c, :], c == 0, c == DC - 1,
                            [sqb, self.cb], [self.psb[pi]])
                self.rsqrt(rs[:, 0:W], self.ps[pi][:, 0:W], 1.0, D * EPS, [self.psb[pi]], [rsb], ln[:, 0:W], lnb)
                if k < 2:
                    for c in range(DC):
                        i = n % 2
                        n += 1
                        self.stt(tmp[i][:, 0:W], self.xv[:, c, t0:t1], self.modv[:, l, js, c, 0:1], rs[:, 0:W],
                                 ALU.mult, ALU.mult, [self.xb[c][k], self.modb[l], rsb], [tmpb[i]])
                        self.act(self.hv[:, c, t0:t1], tmp[i][:, 0:W], AF.Identity, [tmpb[i], self.modb[l]], [self.hb[k]],
                                 bias=self.modv[:, l, jh, c, 0:1])
                else:
                    rbc = rs[:, 0:W].unsqueeze(1).to_broadcast([128, DC, W])
                    self.tt(bigv, self.xv[:, :, t0:t1], rbc, ALU.mult, xr + [rsb], [bigb])
                    self.tt(bigv, bigv, gsv, ALU.mult, [bigb, gsb], [bigb])
                    self.tt(self.hv[:, :, t0:t1], bigv, shv, ALU.add, [bigb, shb], [self.hb[k]])
            self.fence([sqb, rsb, lnb, gsb, shb, bigb] + tmpb)

    def resid_add(self, pso, psb, l, jg, nch, k, seg, gtok, gtokb, stmp, stmpb):
        t0, t1 = TCH[k]
        if k < 2:
            self.stt(self.xv[:, nch, t0:t1], pso, self.modv[:, l, jg, nch, 0:1], self.xv[:, nch, t0:t1],
                     ALU.mult, ALU.add, [psb, self.modb[l], self.xb[nch][k]], [self.xb[nch][k]])
        else:
            self.tt(stmp, pso, gtok[:, nch, :], ALU.mult, [psb, gtokb], [stmpb])
            self.tt(self.xv[:, nch, t0:t1], self.xv[:, nch, t0:t1], stmp, ALU.add, [stmpb, self.xb[nch][k]], [self.xb[nch][k]])

    def proj_resid(self, w, wb, nkc, rhs_v, rhs_b, l, jg, seg, gtok, gtokb, stmp, stmpb):
        for nch in range(DC):
            for k, (t0, t1) in enumerate(TCH):
                pi = self.pn % 6
                self.pn += 1
                pso = self.ps[pi][:, 0:t1 - t0]
                for kc in range(nkc):
                    self.mm(pso, w[:, kc, nch * 128:(nch + 1) * 128], rhs_v[:, kc, t0:t1], kc == 0, kc == nkc - 1,
                            [wb] + rhs_b(kc, k), [self.psb[pi]])
                self.resid_add(pso, self.psb[pi], l, jg, nch, k, seg, gtok, gtokb, stmp, stmpb)

    def mlp(self, l, seg):
        S = self.S
        with ExitStack() as ph:
            a_t = self.sb(ph, "a_t", [128, 8 * T], BF16)
            av = a_t[:].rearrange("p (c t) -> p c t", c=8)
            ab = [[S.buf("a%d_%d" % (c, k)) for k in range(3)] for c in range(8)]
            rl = [self.sb(ph, "rl%d" % i, [128, 512], BF16) for i in range(2)]
            rlb = [S.buf() for i in range(2)]
            gtok = self.sb(ph, "g2tok", [128, DC * TS], F32)
            gtokv = gtok[:].rearrange("p (c t) -> p c t", c=DC)
            gtokb = S.buf()
            stmp = self.sb(ph, "stmp", [128, TS], F32)
            stmpb = S.buf()
            self.tokmod(gtokv, gtokb, l, 5, seg)
            li = self.layers.index(l)
            adag = None
            nbk = 6
            adab = []
            if seg == 0 and li + 1 < len(self.layers):
                wt = [self.sb(ph, "adaw%d" % i, [128, 4096], BF16) for i in range(2)]
                adab = [S.buf() for i in range(2)]
                adag = self.ada_gen(self.layers[li + 1], wt, adab, [5])
                nbk = 5

            def ada_step(n):
                nonlocal adag
                for _ in range(n):
                    if adag is None:
                        return
                    try:
                        next(adag)
                    except StopIteration:
                        adag = None
            nstep = 0
            wu = self.d_wup[l].rearrange("(kc p) n -> p kc n", p=128)
            wd = self.d_wdn[l].rearrange("(kc p) n -> p kc n", p=128)
            nr = 0
            for b in range(DFF // FB):
                for half in range(2):
                    c0 = b * FB + half * 512
                    w, wb = self.wget([(wu[:, :, c0:c0 + 512], 0, 512)], 16, 512)
                    for q in range(4):
                        fc = half * 4 + q
                        for k, (t0, t1) in enumerate(TCH):
                            W = t1 - t0
                            pi = self.pn % nbk
                            self.pn += 1
                            pso = self.ps[pi][:, 0:W]
                            for kc in range(DC):
                                self.mm(pso, w[:, kc, q * 128:(q + 1) * 128], self.hv[:, kc, t0:t1], kc == 0, kc == DC - 1,
                                        [wb, self.hb[k]], [self.psb[pi]])
                            i = nr % 2
                            nr += 1
                            self.act(rl[i][:, 0:W], pso, AF.Relu, [self.psb[pi]], [rlb[i]])
                            self.tt(av[:, fc, t0:t1], rl[i][:, 0:W], rl[i][:, 0:W], ALU.mult, [rlb[i]], [ab[fc][k]])
                    nstep += 1
                    ada_step(2 if nstep % 2 == 0 else 1)
                for half in range(2):
                    w, wb = self.wget([(wd[:, b * 8:(b + 1) * 8, half * 1024:(half + 1) * 1024], 0, 1024)], 8, 1024)
                    for q in range(8):
                        nch = half * 8 + q
                        for k, (t0, t1) in enumerate(TCH):
                            pi = self.pn % nbk
                            self.pn += 1
                            pso = self.ps[pi][:, 0:t1 - t0]
                            for kc in range(8):
                                self.mm(pso, w[:, kc, q * 128:(q + 1) * 128], av[:, kc, t0:t1], kc == 0, kc == 7,
                                        [wb, ab[kc][k]], [self.psb[pi]])
                            self.resid_add(pso, self.psb[pi], l, 5, nch, k, seg, gtokv, gtokb, stmp[:], stmpb)
                    nstep += 1
                    ada_step(2 if nstep % 2 == 0 else 1)
            ada_step(1000)
            self.fence([b_ for row in ab for b_ in row] + rlb + [gtokb, stmpb] + adab)

    def qk_norm_rope(self, P, nh, src, srcb, gain_bc, tt_, sc, scb, out_bf, outb, out_f32=None, out_f32b=None, dup=False):
        S = self.S
        sqf, ss, ssl, rs_, qn = sc["sqf"], sc["ss"], sc["ssl"], sc["rs"], sc["qn"]
        srcv = src.rearrange("p (h d) -> p h d", h=nh)
        sqv = sqf[0:P, 0:nh * 64].rearrange("p (h d) -> p h d", h=nh)
        qnv = qn[0:P, 0:nh * 64].rearrange("p (h d) -> p h d", h=nh)
        self.act(sqv, srcv, AF.Square, [srcb], [scb])
        self.red(ss[0:P, 0:nh], sqv, [scb], [scb])
        self.rsqrt(rs_[0:P, 0:nh], ss[0:P, 0:nh], 1.0 / HD, EPS, [scb], [scb], ssl[0:P, 0:nh], scb)
        self.tt(qnv, srcv, rs_[0:P, 0:nh].unsqueeze(2).to_broadcast([P, nh, HD]), ALU.mult, [srcb, scb], [scb])
        self.tt(qnv, qnv, gain_bc[0:P].unsqueeze(1).to_broadcast([P, nh, HD]), ALU.mult, [scb, self.attb], [scb])
        cos = self.rope_t[0:P, tt_ * 16:tt_ * 16 + 8].unsqueeze(1).to_broadcast([P, nh, 8])
        sin = self.rope_t[0:P, tt_ * 16 + 8:tt_ * 16 + 16].unsqueeze(1).to_broadcast([P, nh, 8])
        t1 = sc["t1"][0:P, 0:nh * 8].rearrange("p (h d) -> p h d", h=nh)
        t2 = sc["t2"][0:P, 0:nh * 8].rearrange("p (h d) -> p h d", h=nh)
        t3 = sc["t3"][0:P, 0:nh * 8].rearrange("p (h d) -> p h d", h=nh)
        t4 = sc["t4"][0:P, 0:nh * 8].rearrange("p (h d) -> p h d", h=nh)
        x1 = qnv[:, :, 0:8]
        x2 = qnv[:, :, 8:16]
        rb = [scb, self.ropeb]
        self.tt(t1, x1, cos, ALU.mult, rb, [scb])
        self.tt(t2, x2, sin, ALU.mult, rb, [scb])
        self.tt(t3, x2, cos, ALU.mult, rb, [scb])
        self.tt(t4, x1, sin, ALU.mult, rb, [scb])
        self.tt(x1, t1, t2, ALU.subtract, [scb], [scb])
        self.tt(x2, t3, t4, ALU.add, [scb], [scb])
        if dup:
            self.cp(out_bf, qn[0:P, 0:HD].unsqueeze(1).to_broadcast([P, 2, HD]), [scb], [outb], eng="act")
        else:
            self.cp(out_bf, qnv, [scb], [outb], eng="act")
        if out_f32 is not None:
            self.cp(out_f32, qn[0:P, 0:nh * 64], [scb], [out_f32b])

    def ptile(self):
        i = self.ptn % 2
        self.ptn += 1
        return self.pt[i], self.ptb[i]

    def attn_layer(self, l, seg):
        S = self.S
        j = l // 2
        last_seg = (seg == self.nseg - 1)
        with ExitStack() as ph:
            sb = lambda name, shape, dt: self.sb(ph, name, shape, dt)
            bufs = []

            def nb(name=""):
                b = S.buf(name)
                bufs.append(b)
                return b
            self.attb = nb("attc")
            qg_t = sb("qg_t", [128, HD], F32)
            kg_t = sb("kg_t", [128, HD], F32)
            es_t = sb("es_t", [128, NQH], F32)
            self.dma("sp", qg_t[:], self.d_qkg[j, 0].partition_broadcast(128), (), [self.attb])
            self.dma("sp", kg_t[:], self.d_qkg[j, 1].partition_broadcast(128), (), [self.attb])
            self.dma("sp", es_t[:], self.d_sinks[j].partition_broadcast(128), (), [self.attb])
            self.act(es_t[:], es_t[:], AF.Exp, [self.attb], [self.attb])
            ck_t = sb("ck_t", [128, NSQ * HD], F32)
            cv_t = sb("cv_t", [128, NSQ * HD], F32)
            ckb = nb("ck")
            for b in range(NSQ):
                sq_ = seg * NSQ + b
                self.dma("sp", self.o_kws[j, sq_, 0:120, :], self.d_ck[j, seg][8:128, b * 256:(b + 1) * 256], (), ())
                self.dma("sp", self.o_vws[j, sq_, 0:120, :], self.d_cv[j, seg][8:128, b * 256:(b + 1) * 256], (), ())
            gtok = sb("g1tok", [128, DC * TS], F32)
            gtokv = gtok[:].rearrange("p (c t) -> p c t", c=DC)
            gtokb = nb()
            stmp = sb("stmp", [128, TS], F32)
            stmpb = nb()
            self.tokmod(gtokv, gtokb, l, 2, seg)
            qT = sb("qT", [128, 4 * T], BF16)
            qTv = qT[:].rearrange("p (c t) -> p c t", c=4)
            qTb = [nb("qT%d" % i) for i in range(9)]
            oT = sb("oT", [128, 4 * T], BF16)
            oTv = oT[:].rearrange("p (c t) -> p c t", c=4)
            oTb = [nb("oT%d" % k) for k in range(3)]
            kT = sb("kT", [128, 9 * 128 + TS], BF16)
            kTb = [nb("kT%d" % i) for i in range(10)]
            va = sb("va", [128, 10 * 65], BF16)
            vav = va[:].rearrange("p (b d) -> p b d", d=65)
            vab = [nb("va%d" % i) for i in range(10)]
            ckT = sb("ckT", [128, NSQ * 128], BF16)
            ckTb = nb("ckT")
            cva = sb("cva", [128, NSQ * 65], BF16)
            cvav = cva[:].rearrange("p (b d) -> p b d", d=65)
            cvab = nb("cva")
            ckd = sb("ckd", [128, 128], BF16)
            ckdb = nb()
            PT = [sb("PT%d" % i, [128, 512], BF16) for i in range(6)]
            PTb = [nb() for i in range(6)]
            PTs = [sb("PTs%d" % i, [128, 128], BF16) for i in range(5)]
            PTsb = [nb() for i in range(5)]
            sc = {"sqf": sb("sqf", [128, 512], F32), "ss": sb("ss", [128, 8], F32), "ssl": sb("ssl", [128, 8], F32),
                  "rs": sb("rs", [128, 8], F32), "qn": sb("qn", [128, 512], F32),
                  "t1": sb("t1", [128, 64], F32), "t2": sb("t2", [128, 64], F32),
                  "t3": sb("t3", [128, 64], F32), "t4": sb("t4", [128, 64], F32)}
            scb = nb("qsc")
            qr = [sb("qr%d" % i, [128, 512], BF16) for i in range(2)]
            qrb = [nb() for i in range(2)]
            kf = [sb("kf%d" % i, [128, HD], F32) for i in range(2)]
            kfb = [nb() for i in range(2)]
            vf = [sb("vf%d" % i, [128, HD], F32) for i in range(2)]
            vfb = [nb() for i in range(2)]
            kr = sb("kr", [128, HD], BF16)
            krb = nb()
            krd = sb("krd", [128, 128], BF16)
            krdb = nb()
            den = sb("den", [128, 4], F32)
            denb = nb()
            otm = [sb("otm%d" % i, [128, 512], BF16) for i in range(2)]
            otmb = [nb() for i in range(2)]
            self.dma("sp", self.rope_t[:], self.d_rope[seg], (), [self.ropeb])
            self.dma("sp", self.hflag_t[:], self.d_hflag[seg], (), [self.mp0b])
            self.ts(self.mp0_t[:], self.masks_t[:, 512:1024], self.hflag_t[:, 0:1], None, ALU.mult, None,
                    [self.cb, self.mp0b], [self.mp0b])
            mp0 = self.mp0_t[:].rearrange("p (j t) -> p j t", j=4)
            wq_all = self.d_wqkv[j].rearrange("(kc p) n -> p kc n", p=128)
            wo_all = self.d_wo_a[j].rearrange("(kc p) n -> p kc n", p=128)
            nq = 0
            nkf = 0
            self.mset(vav[:, :, 64:65], 1.0, vab)
            self.mset(cvav[:, :, 64:65], 1.0, [cvab])
            for g in range(NKV):
                self.cp(kT[:, 0:128], self.halo_k[:, (j * NKV + g) * 128:(j * NKV + g + 1) * 128], [self.halob[j][g]], [kTb[0]])
                self.cp(vav[:, 0, 0:64], self.halo_v[:, (j * NKV + g) * 65:(j * NKV + g) * 65 + 64], [self.halob[j][g]], [vab[0]])
                w, wb = self.wget([(wq_all[:, :, g * 512:(g + 1) * 512], 0, 512)], 16, 512)
                dq = Defer(1)
                for tt_ in range(9):
                    P = 128 if tt_ < 8 else TS
                    t0 = tt_ * 128
                    k = min(tt_ // 4, 2)
                    pi = self.pn % 2
                    self.pn += 1
                    pso = self.ps[pi][0:P, :]
                    for kc in range(DC):
                        self.mm(pso, self.hv[:, kc, t0:t0 + P], w[:, kc, :], kc == 0, kc == DC - 1,
                                [wb, self.hb[k]], [self.psb[pi]])
                    i = nq % 2
                    nq += 1

                    def q_post(P=P, t0=t0, pi=pi, i=i, tt_=tt_, pso=pso):
                        self.qk_norm_rope(P, 8, pso, self.psb[pi], qg_t, tt_, sc, scb,
                                          qr[i][0:P, :].rearrange("p (h d) -> p h d", h=8), qrb[i])
                        ptile, ptb = self.ptile()
                        for c in range(4):
                            self.tr(ptile[:, c * 128:c * 128 + P], qr[i][0:P, c * 128:(c + 1) * 128], self.ident_t[0:P, 0:P],
                                    [qrb[i], self.cb], [ptb])
                        self.cp(qTv[:, :, t0:t0 + P], ptile[:, 0:512].rearrange("p (c t) -> p c t", c=4)[:, :, 0:P],
                                [ptb], [qTb[tt_]], eng="act")
                    dq.push(q_post)
                wkv = self.d_wqkv[j].rearrange("(kc p) n -> p kc n", p=128)
                w, wb = self.wget([(wkv[:, :, 2048 + g * 64:2048 + (g + 1) * 64], 0, 64),
                                   (wkv[:, :, 2304 + g * 64:2304 + (g + 1) * 64], 64, 64)], 16, 128)
                for tt_ in range(9):
                    P = 128 if tt_ < 8 else TS
                    t0 = tt_ * 128
                    k = min(tt_ // 4, 2)
                    pi = self.pn % 2
                    self.pn += 1
                    pso = self.ps[pi][0:P, 0:128]
                    for kc in range(DC):
                        self.mm(pso, self.hv[:, kc, t0:t0 + P], w[:, kc, :], kc == 0, kc == DC - 1,
                                [wb, self.hb[k]], [self.psb[pi]])
                    i = nkf % 2
                    nkf += 1

                    def kv_post(P=P, t0=t0, pi=pi, i=i, tt_=tt_, g=g):
                        wantf = tt_ >= 7
                        self.qk_norm_rope(P, 1, self.ps[pi][0:P, 0:64], self.psb[pi], kg_t, tt_, sc, scb,
                                          krd[0:P, :].rearrange("p (r d) -> p r d", r=2), krdb,
                                          kf[i][0:P, :] if wantf else None, kfb[i], dup=True)
                        if wantf:
                            self.cp(vf[i][0:P, :], self.ps[pi][0:P, 64:128], [self.psb[pi]], [vfb[i]], eng="act")
                        blk = tt_ + 1
                        self.cp(vav[0:P, blk, 0:64], self.ps[pi][0:P, 64:128], [self.psb[pi]], [vab[blk]])
                        ptile, ptb = self.ptile()
                        self.tr(ptile[:, 0:P], krd[0:P, :], self.ident_t[0:P, 0:P], [krdb, self.cb], [ptb])
                        self.cp(kT[:, blk * 128:blk * 128 + P], ptile[:, 0:P], [ptb], [kTb[blk]])
                        if tt_ == 7:
                            self.dma("sp", self.o_kwp[j, seg][:, g * 64:(g + 1) * 64], kf[i][:, :], [kfb[i]], ())
                            self.dma("sp", self.o_vwp[j, seg][:, g * 64:(g + 1) * 64], vf[i][:, :], [vfb[i]], ())
                            self.cp(self.halo_k[:, (j * NKV + g) * 128:(j * NKV + g + 1) * 128], kT[:, 8 * 128:9 * 128],
                                    [kTb[8]], [self.halob[j][g]])
                            self.cp(self.halo_v[:, (j * NKV + g) * 65:(j * NKV + g) * 65 + 64], vav[:, 8, 0:64],
                                    [vab[8]], [self.halob[j][g]])
                        if tt_ == 8:
                            for b in range(NSQ):
                                sq_ = seg * NSQ + b
                                self.dma("sp", self.o_kws[j, sq_, 120:128, g * 64:(g + 1) * 64], kf[i][b * 8:(b + 1) * 8, :], [kfb[i]], ())
                                self.dma("sp", self.o_vws[j, sq_, 120:128, g * 64:(g + 1) * 64], vf[i][b * 8:(b + 1) * 8, :], [vfb[i]], ())
                    dq.push(kv_post)
                dq.flush()
                ckv = ck_t[:].rearrange("p (b f) -> p b f", b=NSQ)
                cvv = cv_t[:].rearrange("p (b f) -> p b f", b=NSQ)
                self.dma("sp", ckv, self.d_ck[j, seg].rearrange("p (b f) -> p b f", b=NSQ)[:, :, g * 64:(g + 1) * 64], (), [ckb])
                self.dma("sp", cvv, self.d_cv[j, seg].rearrange("p (b f) -> p b f", b=NSQ)[:, :, g * 64:(g + 1) * 64], (), [ckb])
                for b in range(NSQ):
                    self.cp(ckd[:].rearrange("p (r d) -> p r d", r=2),
                            ckv[:, b, :].unsqueeze(1).to_broadcast([128, 2, HD]), [ckb], [ckdb])
                    ptile, ptb = self.ptile()
                    self.tr(ptile[:, 0:128], ckd[:], self.ident_t[:], [ckdb, self.cb], [ptb])
                    self.cp(ckT[:, b * 128:(b + 1) * 128], ptile[:, 0:128], [ptb], [ckTb], eng="act")
                self.cp(cvav[:, :, 0:64], cvv, [ckb], [cvab])
                NPT = len(PT)
                d1, d2, d3 = Defer(1), Defer(1), Defer(1)
                un = 0
                for i in range(8):
                    for par in range(2):
                        pp = slice(par * 64, (par + 1) * 64)
                        rhs = qTv[pp, :, i * 128:(i + 1) * 128]
                        bp = (un % 2) * 2
                        pts = [(2 * un) % NPT, (2 * un + 1) % NPT]
                        un += 1
                        for kbi, blk in enumerate((i, i + 1)):
                            self.mm(self.ps[bp + kbi][:, :], kT[pp, blk * 128:(blk + 1) * 128], rhs, True, True,
                                    [kTb[blk], qTb[i]], [self.psb[bp + kbi]])

                        def st1(i=i, par=par, bp=bp, pts=pts, g=g):
                            for kbi in range(2):
                                pb_ = pts[kbi]
                                self.act(PT[pb_][:], self.ps[bp + kbi][:, :], AF.Exp, [self.psb[bp + kbi]], [PTb[pb_]],
                                         scale=HD ** -0.5)
                                m = (mp0 if i == 0 else self.m_prev()) if kbi == 0 else self.m_cur()
                                mb = [self.mp0b] if (kbi == 0 and i == 0) else [self.cb]
                                self.tt(PT[pb_][:].rearrange("p (j t) -> p j t", j=4), PT[pb_][:].rearrange("p (j t) -> p j t", j=4),
                                        m, ALU.mult, [PTb[pb_]] + mb, [PTb[pb_]])

                            def st2():
                                po = 4 + par
                                for jj in range(4):
                                    for kbi, blk in enumerate((i, i + 1)):
                                        self.mm(self.ps[po][:, jj * 65:(jj + 1) * 65], PT[pts[kbi]][:, jj * 128:(jj + 1) * 128],
                                                vav[:, blk, :], kbi == 0, kbi == 1, [PTb[pts[kbi]], vab[blk]], [self.psb[po]])
                                pov = self.ps[po][:, 0:260].rearrange("p (j d) -> p j d", d=65)
                                self.tt(den[:, :], pov[:, :, 64], es_t[:, g * 8 + par:g * 8 + 8:2], ALU.add,
                                        [self.psb[po], self.attb], [denb])
                                self.recip(den[:, :], den[:, :], [denb], [denb])
                                oi = i % 2
                                ov = otm[oi][:].rearrange("p (j r d) -> p j r d", j=4, r=2)[:, :, par, :]
                                self.tt(ov, pov[:, :, 0:64], den[:, :].unsqueeze(2).to_broadcast([128, 4, HD]), ALU.mult,
                                        [self.psb[po], denb], [otmb[oi]])
                                if par == 1:
                                    def st3():
                                        ptile, ptb = self.ptile()
                                        for c in range(4):
                                            self.tr(ptile[:, c * 128:(c + 1) * 128], otm[oi][:, c * 128:(c + 1) * 128], self.ident_t[:],
                                                    [otmb[oi], self.cb], [ptb])
                                        self.cp(oTv[:, :, i * 128:(i + 1) * 128], ptile[:, 0:512].rearrange("p (c t) -> p c t", c=4),
                                                [ptb], [oTb[i // 4]], eng="act")
                                    d3.push(st3)
                            d2.push(st2)
                        d1.push(st1)
                d1.flush()
                d2.flush()
                d3.flush()
                for par in range(2):
                    pp = slice(par * 64, (par + 1) * 64)
                    rhs = qTv[pp, :, TP:T]
                    for b in range(NSQ + 1):
                        pi = 2 + (b % 2)
                        if b < NSQ:
                            Pk = 128
                            self.mm(self.ps[pi][:, 0:128], ckT[pp, b * 128:(b + 1) * 128], rhs, True, True,
                                    [ckTb, qTb[8]], [self.psb[pi]])
                            m = self.m_sprev(b)
                        else:
                            Pk = TS
                            self.mm(self.ps[pi][0:Pk, 0:128], kT[pp, 9 * 128:9 * 128 + TS], rhs, True, True,
                                    [kTb[9], qTb[8]], [self.psb[pi]])
                            m = self.m_scur()
                        self.act(PTs[b][0:Pk, :], self.ps[pi][0:Pk, 0:128], AF.Exp, [self.psb[pi]], [PTsb[b]], scale=HD ** -0.5)
                        self.tt(PTs[b][0:Pk, :].rearrange("p (j t) -> p j t", j=4),
                                PTs[b][0:Pk, :].rearrange("p (j t) -> p j t", j=4), m, ALU.mult,
                                [PTsb[b], self.cb], [PTsb[b]])
                    po = 4 + par
                    for jj in range(4):
                        for b in range(NSQ + 1):
                            if b < NSQ:
                                self.mm(self.ps[po][0:TS, jj * 65:(jj + 1) * 65], PTs[b][:, jj * 32:(jj + 1) * 32],
                                        cvav[:, b, :], b == 0, False, [PTsb[b], cvab], [self.psb[po]])
                            else:
                                self.mm(self.ps[po][0:TS, jj * 65:(jj + 1) * 65], PTs[b][0:TS, jj * 32:(jj + 1) * 32],
                                        vav[0:TS, 9, :], False, True, [PTsb[b], vab[9]], [self.psb[po]])
                    pov = self.ps[po][0:TS, 0:260].rearrange("p (j d) -> p j d", d=65)
                    self.tt(den[0:TS, :], pov[:, :, 64], es_t[0:TS, g * 8 + par:g * 8 + 8:2], ALU.add,
                            [self.psb[po], self.attb], [denb])
                    self.recip(den[0:TS, :], den[0:TS, :], [denb], [denb])
                    ov = otm[0][0:TS, :].rearrange("p (j r d) -> p j r d", j=4, r=2)[:, :, par, :]
                    self.tt(ov, pov[:, :, 0:64], den[0:TS, :].unsqueeze(2).to_broadcast([TS, 4, HD]), ALU.mult,
                            [self.psb[po], denb], [otmb[0]])
                ptile, ptb = self.ptile()
                for c in range(4):
                    self.tr(ptile[:, c * 128:c * 128 + TS], otm[0][0:TS, c * 128:(c + 1) * 128], self.ident_t[0:TS, 0:TS],
                            [otmb[0], self.cb], [ptb])
                self.cp(oTv[:, :, TP:T], ptile[:, 0:512].rearrange("p (c t) -> p c t", c=4)[:, :, 0:TS],
                        [ptb], [oTb[2]], eng="act")
                w, wb = self.wget([(wo_all[:, g * 4:(g + 1) * 4, :], 0, 2048)], 4, 2048)
                self.proj_resid(w, wb, 4, oTv, lambda kc, k: [oTb[k]], l, 2, seg, gtokv, gtokb, stmp[:], stmpb)
            self.fence(bufs)

    def hgrn_layer(self, l, seg):
        S = self.S
        j = l // 2
        tiles = [(i * HT, min(HT, TP - i * HT)) for i in range((TP + HT - 1) // HT)]
        NT = len(tiles)
        NCK = TP // CH
        with ExitStack() as ph:
            sb = lambda name, shape, dt: self.sb(ph, name, shape, dt)
            bufs = []

            def nb(name=""):
                b = S.buf(name)
                bufs.append(b)
                return b
            gtok = sb("g1tok", [128, DC * TS], F32)
            gtokv = gtok[:].rearrange("p (c t) -> p c t", c=DC)
            gtokb = nb()
            stmp = sb("stmp", [128, TS], F32)
            stmpb = nb()
            self.tokmod(gtokv, gtokb, l, 2, seg)
            qs = sb("qs", [128, T], BF16)
            A = sb("Aa", [128, T], F32)
            B = sb("Bb", [128, T], F32)
            C = sb("Cc", [128, T], F32)
            qsb, Ab_, Bb_, Cb_ = nb("qs"), nb("A"), nb("B"), nb("C")
            qt = sb("qt", [128, T], BF16)
            kt = sb("kt", [128, T], BF16)
            kh = sb("kh", [128, T], BF16)
            qtb, ktb, khb = nb("qt"), nb("kt"), nb("kh")
            qts = sb("qts", [128, NSQ * TS], BF16)
            khs = sb("khs", [128, NSQ * TS], BF16)
            qtsb, khsb = nb(), nb()
            Dch = sb("Dch", [128, NCK + NSQ], F32)
            Dchb = nb("Dch")
            vt = sb("vt", [128, (NT + 1) * 128], BF16)
            vtv = vt[:].rearrange("p (i d) -> p i d", d=128)
            vtb = nb("vt")
            sg = sb("sg", [128, (NT + 1) * 128], BF16)
            sgv = sg[:].rearrange("p (i d) -> p i d", d=128)
            sgb = nb("sg")
            kht = sb("kht", [128, NT * 128], BF16)
            khtv = kht[:].rearrange("p (i d) -> p i d", d=128)
            khtb = nb("kht")
            khts = sb("khts", [128, NSQ * 128], BF16)
            khtsv = khts[:].rearrange("p (i d) -> p i d", d=128)
            khtsb = nb("khts")
            NSB = 6
            Sb = sb("Sb", [128, NSB * 128], BF16)
            Sbv = Sb[:].rearrange("p (i d) -> p i d", d=128)
            Sbb = [nb("Sb%d" % i) for i in range(NSB)]
            St = sb("St", [128, 128], F32)
            Stb = nb("St")
            S0 = sb("S0", [128, NSQ * 128], F32)
            S0v = S0[:].rearrange("p (i d) -> p i d", d=128)
            S0b = [nb("S0_%d" % i) for i in range(NSQ)]
            S0h = sb("S0h", [128, NSQ * 128], BF16)
            S0hv = S0h[:].rearrange("p (i d) -> p i d", d=128)
            S0hb = nb("S0h")
            Am = [sb("Am%d" % i, [128, HT], BF16) for i in range(2)]
            Amb = [nb() for i in range(2)]
            og = [sb("og%d" % i, [128, 128], BF16) for i in range(2)]
            ogb = [nb() for i in range(2)]
            ss = sb("hss_", [128, 4], F32)
            ssb = nb()
            junk = sb("junk", [128, 128], F32)
            junkb = nb()
            oT = sb("oTr", [128, 4 * T], BF16)
            oTv = oT[:].rearrange("p (c t) -> p c t", c=4)
            oTb = [nb("oT%d" % k) for k in range(3)]
            ub = [nb("u%d" % i) for i in range(4)]
            ob4 = [nb("o%d" % i) for i in range(4)]
            bd = self.hmask_t[:, 0:128]
            reset = self.hmask_t[:, 128:128 + T]
            seqm = self.hmask_t[:, 128 + T:128 + T + 128].rearrange("p (b t) -> p b t", b=NSQ)
            bds = self.hmask_t[0:TS, 128 + T + 128:128 + T + 128 + TS]
            win = self.d_win[j].rearrange("(kc p) (g n) -> p kc g n", p=128, g=4)
            wo_all = self.d_wo_r[j].rearrange("(kc p) n -> p kc n", p=128)
            hspb = self.hspb
            nA = 0
            nO = 0
            nU = 0
            nog = 0
            dO = Defer(1)
            def fm_gen(w, wb, banks):
                n_ = 0
                for which, dst, dstb, func in ((0, qs, qsb, AF.Silu), (1, A, Ab_, AF.Sigmoid)):
                    for k, (t0, t1) in enumerate(TCH):
                        pi = banks[n_ % len(banks)]
                        n_ += 1
                        pso = self.ps[pi][:, 0:t1 - t0]
                        for kc in range(DC):
                            self.mm(pso, w[:, kc, which * 128:(which + 1) * 128], self.hv[:, kc, t0:t1], kc == 0, kc == DC - 1,
                                    [wb, self.hb[k]], [self.psb[pi]])
                        self.act(dst[:, t0:t1], pso, func, [self.psb[pi]], [dstb])
                        yield

            def get_w(hd_):
                return self.wget([(win[:, :, gi, hd_ * 128:(hd_ + 1) * 128], gi * 128, 128) for gi in range(4)], 16, 512)
            for hd in range(self.nheads):
                w, wb = get_w(hd)
                for _ in fm_gen(w, wb, [0, 1, 2, 3, 4, 5]):
                    pass
                def gates():
                    if j == 1:
                        self.ts(A[:], A[:], self.lb_t[:, 16 + hd:17 + hd], self.lb_t[:, hd:hd + 1], ALU.mult, ALU.add,
                                [Ab_, self.cb], [Ab_])
                        yield
                    self.act(B[:], A[:], AF.Ln, [Ab_], [Bb_])
                    yield
                    self.scan(C[:], reset, B[:], 0.0, [Bb_, self.cb], [Cb_])
                    yield
                    self.ts(A[:], A[:], -1.0, 1.0, ALU.mult, ALU.add, [Ab_], [Ab_])
                    yield
                    self.act(B[:], C[:], AF.Exp, [Cb_], [Bb_])
                    yield
                    self.tt(qt[:], qs[:], B[:], ALU.mult, [qsb, Bb_], [qtb])
                    yield
                    self.ts(B[:], C[:], -1.0, CLAMP, ALU.mult, ALU.min, [Cb_], [Bb_])
                    yield
                    self.act(B[:], B[:], AF.Exp, [Bb_], [Bb_])
                    yield
                    self.tt(kt[:], A[:], B[:], ALU.mult, [Ab_, Bb_], [ktb])
                    yield
                    Cp = C[:, 0:TP].rearrange("p (c i) -> p c i", i=CH)
                    Bp = B[:, 0:TP].rearrange("p (c i) -> p c i", i=CH)
                    self.tt(Bp, C[:, CH - 1:TP:CH].unsqueeze(2).to_broadcast([128, NCK, CH]), Cp, ALU.subtract, [Cb_], [Bb_])
                    yield
                    Cs = C[:, TP:T].rearrange("p (c i) -> p c i", i=DEC_SEQ)
                    Bs = B[:, TP:T].rearrange("p (c i) -> p c i", i=DEC_SEQ)
                    self.tt(Bs, C[:, TP + DEC_SEQ - 1:T:DEC_SEQ].unsqueeze(2).to_broadcast([128, NSQ, DEC_SEQ]), Cs, ALU.subtract, [Cb_], [Bb_])
                    yield
                    self.act(B[:], B[:], AF.Exp, [Bb_], [Bb_])
                    yield
                    self.tt(kh[:], A[:], B[:], ALU.mult, [Ab_, Bb_], [khb])
                    yield
                    self.act(Dch[:, 0:NCK], C[:, CH - 1:TP:CH], AF.Exp, [Cb_], [Dchb])
                    yield
                    self.act(Dch[:, NCK:NCK + NSQ], C[:, TP + DEC_SEQ - 1:T:DEC_SEQ], AF.Exp, [Cb_], [Dchb])
                    yield
                    self.tt(qts[:].rearrange("p (b t) -> p b t", b=NSQ), qt[:, TP:T].unsqueeze(1).to_broadcast([128, NSQ, TS]), seqm,
                            ALU.mult, [qtb, self.cb], [qtsb])
                    yield
                    self.tt(khs[:].rearrange("p (b t) -> p b t", b=NSQ), kh[:, TP:T].unsqueeze(1).to_broadcast([128, NSQ, TS]), seqm,
                            ALU.mult, [khb, self.cb], [khsb])
                    yield
                def tmtiles():
                    for which, dstv, dstb in ((2, vtv, vtb), (3, sgv, sgb)):
                        for ti in range(NT + 1):
                            t0, P = tiles[ti] if ti < NT else (TP, TS)
                            k = 2 if ti == NT else (0 if t0 + P <= 512 else 1)
                            hbs = [self.hb[k]] if (ti == NT or t0 >= 512 or t0 + P <= 512) else [self.hb[0], self.hb[1]]
                            pi = self.pn % 6
                            self.pn += 1
                            pso = self.ps[pi][0:P, 0:128]
                            for kc in range(DC):
                                self.mm(pso, self.hv[:, kc, t0:t0 + P], w[:, kc, which * 128:(which + 1) * 128], kc == 0, kc == DC - 1,
                                        [wb] + hbs, [self.psb[pi]])
                            if which == 2:
                                self.cp(dstv[0:P, ti, :], pso, [self.psb[pi]], [dstb])
                            else:
                                self.act(dstv[0:P, ti, :], pso, AF.Silu, [self.psb[pi]], [dstb])
                            yield
                ga, tmg = gates(), tmtiles()
                live = [ga, ga, tmg]
                while live:
                    for gen in list(live):
                        if gen not in live:
                            continue
                        try:
                            next(gen)
                        except StopIteration:
                            while gen in live:
                                live.remove(gen)
                for ti in range(NT):
                    t0, P = tiles[ti]
                    ptile, ptb = self.ptile()
                    self.tr(ptile[0:P, 0:128], kh[:, t0:t0 + P], self.ident_t[:], [khb, self.cb], [ptb])
                    self.cp(khtv[0:P, ti, :], ptile[0:P, 0:128], [ptb], [khtb], eng="act")
                ptile, ptb = self.ptile()
                for b in range(NSQ):
                    self.tr(ptile[0:TS, b * 128:(b + 1) * 128], khs[:, b * TS:(b + 1) * TS], self.ident_t[:], [khsb, self.cb], [ptb])
                self.cp(khts[0:TS, :], ptile[0:TS, 0:NSQ * 128], [ptb], [khtsb], eng="act")
                fmg = None

                def fm_step():
                    nonlocal fmg
                    if fmg is not None:
                        try:
                            next(fmg)
                        except StopIteration:
                            fmg = None
                if seg == 0:
                    self.mset(St[:], 0.0, [Stb])
                else:
                    self.dma("sp", St[:], self.o_hsp[j, seg - 1, hd], [hspb[j][hd]], [Stb])
                self.cp(Sbv[:, 0, :], St[:], [Stb], [Sbb[0]], eng="act")
                for c in range(NCK):
                    ti, a = divmod(c, HT // CH)
                    u = 4 + nU % 2
                    nU += 1
                    uo = self.ps[u][:, 0:128]
                    self.mm(uo, khtv[a * CH:(a + 1) * CH, ti, :], vtv[a * CH:(a + 1) * CH, ti, :], True, True,
                            [khtb, vtb], [self.psb[u]])
                    self.stt(St[:], St[:], Dch[:, c:c + 1], uo, ALU.mult, ALU.add, [Stb, Dchb, self.psb[u]], [Stb])
                    if c < NCK - 1:
                        self.cp(Sbv[:, (c + 1) % NSB, :], St[:], [Stb], [Sbb[(c + 1) % NSB]], eng="act")
                    if c % 5 == 1:
                        fm_step()
                    if a == HT // CH - 1 or c == NCK - 1:
                        t0, P = tiles[ti]
                        ai = nA % 2
                        nA += 1
                        pa = self.ps[ai]
                        self.mm(pa[0:P, 0:P], kt[:, t0:t0 + P], qt[:, t0:t0 + P], True, True, [ktb, qtb], [self.psb[ai]])
                        self.tt(Am[ai][0:P, 0:P], pa[0:P, 0:P], bd[0:P, 0:P], ALU.mult, [self.psb[ai], self.cb], [Amb[ai]])
                        ob_ = 2 + ai
                        po = self.ps[ob_][:, 0:128]
                        nchk = P // CH
                        self.mm(po[0:P, :], Am[ai][0:P, 0:P], vtv[0:P, ti, :], True, False, [Amb[ai], vtb], [self.psb[ob_]])
                        for a2 in range(nchk):
                            c2 = ti * (HT // CH) + a2
                            self.mm(po[a2 * CH:(a2 + 1) * CH, :], qt[:, t0 + a2 * CH:t0 + (a2 + 1) * CH], Sbv[:, c2 % NSB, :],
                                    False, True, [qtb, Sbb[c2 % NSB]], [self.psb[ob_]])

                        def o_post(po=po, ob_=ob_, P=P, ti=ti, t0=t0, nog=nog, hd=hd):
                            self.hgrn_out(po[0:P, :], self.psb[ob_], P, sgv[0:P, ti, :], sgb, ss, ssb, junk, junkb, og, ogb, nog,
                                          oTv[:, hd % 4, t0:t0 + P], oTb, t0, P, j, hd)
                        dO.push(o_post)
                        nog += 1
                self.dma("sp", self.o_hsp[j, seg, hd], St[:], [Stb], [hspb[j][hd]])
                for b in range(NSQ):
                    self.dma("sp", S0v[:, b, :], self.d_st[j, seg * NSQ + b, hd], (), [S0b[b]])
                self.cp(S0h[:], S0[:], S0b, [S0hb], eng="act")
                dO.flush()
                while fmg is not None:
                    fm_step()
                ai = nA % 2
                nA += 1
                pa = self.ps[ai]
                self.mm(pa[0:TS, 0:TS], kt[:, TP:T], qt[:, TP:T], True, True, [ktb, qtb], [self.psb[ai]])
                self.tt(Am[ai][0:TS, 0:TS], pa[0:TS, 0:TS], bds, ALU.mult, [self.psb[ai], self.cb], [Amb[ai]])
                ob_ = 2 + ai
                po = self.ps[ob_][:, 0:128]
                self.mm(po[0:TS, :], Am[ai][0:TS, 0:TS], vtv[0:TS, NT, :], True, False, [Amb[ai], vtb], [self.psb[ob_]])
                for b in range(NSQ):
                    self.mm(po[0:TS, :], qts[:, b * TS:(b + 1) * TS], S0hv[:, b, :], False, b == NSQ - 1, [qtsb, S0hb], [self.psb[ob_]])
                self.hgrn_out(po[0:TS, :], self.psb[ob_], TS, sgv[0:TS, NT, :], sgb, ss, ssb, junk, junkb, og, ogb, nog,
                              oTv[:, hd % 4, TP:T], oTb, TP, TS, j, hd)
                nog += 1
                for b in range(NSQ):
                    u = 4 + nU % 2
                    nU += 1
                    uo = self.ps[u][:, 0:128]
                    self.mm(uo, khtsv[0:TS, b, :], vtv[0:TS, NT, :], True, True, [khtsb, vtb], [self.psb[u]])
                    self.stt(S0v[:, b, :], S0v[:, b, :], Dch[:, NCK + b:NCK + b + 1], uo, ALU.mult, ALU.add,
                             [S0b[b], Dchb, self.psb[u], S0hb], [S0b[b]])
                    self.dma("sp", self.o_hss[j, seg * NSQ + b, hd], S0v[:, b, :], [S0b[b]], ())
                if hd % 4 == 3:
                    g = hd // 4
                    w2, wb2 = self.wget([(wo_all[:, g * 4:(g + 1) * 4, :], 0, 2048)], 4, 2048)
                    self.proj_resid(w2, wb2, 4, oTv, lambda kc, k: [oTb[k]], l, 2, seg, gtokv, gtokb, stmp[:], stmpb)
            self.fence(bufs)

    def hgrn_out(self, po, pob, P, sgt, sgb, ss, ssb, junk, junkb, og, ogb, n, oT_dst, oTb, t0, P2, j, hd):
        i = n % 2
        col = n % 2
        self.act(junk[0:P, :], po, AF.Square, [pob], [junkb], accum_out=ss[0:P, col:col + 1])
        self.rsqrt(ss[0:P, 2 + col:3 + col], ss[0:P, col:col + 1], 1.0 / 128.0, EPS, [junkb], [ssb], junk[0:P, 0:1], junkb)
        self.stt(og[i][0:P, :], po, ss[0:P, 2 + col:3 + col], sgt, ALU.mult, ALU.mult, [pob, ssb, sgb], [ogb[i]])
        ptile, ptb = self.ptile()
        self.tr(ptile[:, 0:P], og[i][0:P, :], self.ident_t[0:P, 0:P], [ogb[i], self.cb], [ptb])
        ks = sorted(set([min(t0 // 512, 2), min((t0 + P - 1) // 512, 2)]))
        self.ts(oT_dst, ptile[:, 0:P], self.og_t[:, j * RH + hd:j * RH + hd + 1], None, ALU.mult, None,
                [ptb, self.cb], [oTb[k] for k in ks])

    def segment(self, seg):
        xT = self.d_xT[seg].rearrange("(c p) t -> p c t", p=128)
        for c in range(DC):
            self.dma("sp", self.xv[:, c, :], xT[:, c, :], (), self.xb[c])
        for l in self.layers:
            self.norm_mod(l, 0, seg)
            if l % 2 == 0:
                self.attn_layer(l, seg)
            else:
                self.hgrn_layer(l, seg)
            if self.do_mlp:
                self.norm_mod(l, 1, seg)
                self.mlp(l, seg)
        yT = self.o_yT[seg].rearrange("(c p) t -> p c t", p=128)
        for c in range(DC):
            self.dma("sp", yT[:, c, :], self.xv[:, c, :], self.xb[c], ())


def _rope_tables():
    half = ROT // 2
    inv = (np.float32(THETA) ** (-np.arange(half, dtype=np.float32) / np.float32(half))).astype(np.float32)
    out = np.zeros((NSEG, 128, 9, 16), np.float32)
    for seg in range(NSEG):
        for tt in range(9):
            if tt < 8:
                pos = (seg * TP + tt * 128 + np.arange(128)).astype(np.float32)
            else:
                pos = (PAST_LEN + (np.arange(128) % DEC_SEQ)).astype(np.float32)
            ang = pos[:, None] * inv[None, :]
            out[seg, :, tt, 0:8] = np.cos(ang)
            out[seg, :, tt, 8:16] = np.sin(ang)
    return out.reshape(NSEG, 128, 9 * 16)


def _masks():
    m = np.zeros((128, 1792), np.float32)
    s = np.arange(128)[:, None]
    t = np.arange(128)[None, :]
    cur = (s <= t).astype(np.float32)
    prev = (s > t).astype(np.float32)
    m[:, 0:512] = np.tile(cur, (1, 4))
    m[:, 512:1024] = np.tile(prev, (1, 4))
    ts_ = np.arange(TS)[None, :]
    for b in range(NSQ):
        mm_ = ((ts_ // DEC_SEQ == b) & (s > (ts_ % DEC_SEQ))).astype(np.float32)
        m[:, 1024 + b * 128:1024 + (b + 1) * 128] = np.tile(mm_, (1, 4))
    s32 = np.arange(TS)[:, None]
    sc = ((s32 // DEC_SEQ == ts_ // DEC_SEQ) & (s32 % DEC_SEQ <= ts_ % DEC_SEQ)).astype(np.float32)
    m[0:TS, 1536:1664] = np.tile(sc, (1, 4))
    m[:, 1664:1792] = np.eye(128, dtype=np.float32)
    hm = np.zeros((128, 128 + T + 128 + TS), np.float32)
    hm[:, 0:128] = ((s // CH == t // CH) & (s <= t)).astype(np.float32)
    reset = np.ones(T, np.float32)
    reset[0:TP:CH] = 0.0
    reset[TP:T:DEC_SEQ] = 0.0
    hm[:, 128:128 + T] = reset[None, :]
    sq = np.zeros((NSQ, TS), np.float32)
    for b in range(NSQ):
        sq[b, b * DEC_SEQ:(b + 1) * DEC_SEQ] = 1.0
    hm[:, 128 + T:128 + T + 128] = sq.reshape(1, -1)
    hm[0:TS, 128 + T + 128:] = ((s32 // DEC_SEQ == ts_ // DEC_SEQ) & (s32 <= ts_)).astype(np.float32)
    return m, hm


def _core_assign(core):
    b = core % BATCH
    return b, [b * 8 + i for i in range(NSEG * NSQ)]


_PROG_CACHE = {}


def _get_prog(nlayers):
    if nlayers not in _PROG_CACHE:
        p = Prog(nlayers=nlayers)
        p.hspb = [[Buf("hsp%d_%d" % (j, h)) for h in range(RH)] for j in range(2)]
        p.build()
        _PROG_CACHE[nlayers] = p
    return _PROG_CACHE[nlayers]


def kernel(x_prompt, x_sample, cache_k_win, cache_v_win, state_hgrn, c_prompt, c_sample,
           norm_gain, w_ada, b_ada, attn_w_qkv, attn_q_gain, attn_k_gain, attn_sinks, attn_w_o,
           rec_w_in, rec_lb_logits, rec_o_gain, rec_w_o, mlp_w_up, mlp_w_down, _nlayers=DEPTH, _trace=False):
    f = lambda a: np.ascontiguousarray(np.asarray(a, dtype=np.float32))
    x_prompt, x_sample = f(x_prompt), f(x_sample)
    cache_k_win, cache_v_win, state_hgrn = f(cache_k_win), f(cache_v_win), f(state_hgrn)
    c_prompt, c_sample = f(c_prompt), f(c_sample)
    masks, hmask = _masks()
    rope = _rope_tables()
    shared = {
        "masks": masks, "hmask": hmask, "rope": rope,
        "gainT": f(np.asarray(norm_gain).reshape(DEPTH, 2, DC, 128).transpose(3, 0, 1, 2).reshape(128, -1)),
        "badaT": f(np.asarray(b_ada).reshape(DEPTH, 96, 128).transpose(2, 0, 1).reshape(128, -1)),
        "qkg": f(np.stack([np.asarray(attn_q_gain), np.asarray(attn_k_gain)], axis=1)),
        "sinks": f(attn_sinks),
        "lbT": f(np.asarray(rec_lb_logits).reshape(2, RH, 128).transpose(2, 0, 1).reshape(128, -1)),
        "ogT": f(np.asarray(rec_o_gain).reshape(2, RH, 128).transpose(2, 0, 1).reshape(128, -1)),
        "w_ada": f(w_ada), "attn_w_qkv": f(attn_w_qkv), "attn_w_o": f(attn_w_o), "rec_w_in": f(rec_w_in),
        "rec_w_o": f(rec_w_o), "mlp_w_up": f(mlp_w_up), "mlp_w_down": f(mlp_w_down),
    }
    hflag = np.zeros((NSEG, 128, 1), np.float32)
    hflag[1:] = 1.0
    in_maps = []
    for core in range(NCORES):
        b, sq = _core_assign(core)
        xT = np.empty((NSEG, D, T), np.float32)
        for seg in range(NSEG):
            xT[seg, :, 0:TP] = x_prompt[b, seg * TP:(seg + 1) * TP, :].T
            xs = x_sample[sq[seg * NSQ:(seg + 1) * NSQ]].reshape(TS, D)
            xT[seg, :, TP:T] = xs.T
        cs = np.concatenate([c_prompt[b:b + 1], c_sample[sq]], axis=0)
        cT = cs.reshape(NSEQ, DC, 128).transpose(2, 1, 0).reshape(128, DC * NSEQ)
        ck = cache_k_win[:, sq].reshape(2, NSEG, NSQ, 128, 256).transpose(0, 1, 3, 2, 4).reshape(2, NSEG, 128, NSQ * 256)
        cv = cache_v_win[:, sq].reshape(2, NSEG, NSQ, 128, 256).transpose(0, 1, 3, 2, 4).reshape(2, NSEG, 128, NSQ * 256)
        m = dict(shared)
        m.update({"xT": xT, "cT": f(cT), "ck": f(ck), "cv": f(cv), "st": f(state_hgrn[:, sq]), "hflag": hflag})
        in_maps.append(m)
    prog = _get_prog(_nlayers)
    res = run_bass_kernel_spmd(prog.nc, in_maps, core_ids=list(range(NCORES)), trace=_trace)
    R = res.results
    y_prompt = np.empty((BATCH, SEQ, D), np.float32)
    y_sample = np.empty((DEC_BATCH, DEC_SEQ, D), np.float32)
    kwp = np.empty((2, BATCH, 128, NKV, HD), np.float32)
    vwp = np.empty_like(kwp)
    kws = np.empty((2, DEC_BATCH, 128, NKV, HD), np.float32)
    vws = np.empty_like(kws)
    hsp = np.empty((2, BATCH, RH, 128, 128), np.float32)
    hss = np.empty((2, DEC_BATCH, RH, 128, 128), np.float32)
    for core in range(BATCH):
        b, sq = _core_assign(core)
        r = R[core]
        for seg in range(NSEG):
            y_prompt[b, seg * TP:(seg + 1) * TP, :] = r["yT"][seg][:, 0:TP].T
            ys = r["yT"][seg][:, TP:T].T.reshape(NSQ, DEC_SEQ, D)
            y_sample[sq[seg * NSQ:(seg + 1) * NSQ]] = ys
        kwp[:, b] = r["kwp"][:, NSEG - 1].reshape(2, 128, NKV, HD)
        vwp[:, b] = r["vwp"][:, NSEG - 1].reshape(2, 128, NKV, HD)
        kws[:, sq] = r["kws"].reshape(2, NSEG * NSQ, 128, NKV, HD)
        vws[:, sq] = r["vws"].reshape(2, NSEG * NSQ, 128, NKV, HD)
        hsp[:, b] = r["hsp"][:, NSEG - 1]
        hss[:, sq] = r["hss"]
    if _trace:
        kernel.last_exec_ns = res.exec_time_ns
    return (y_prompt, y_sample, kwp, vwp, kws, vws, hsp, hss)
```

```python
import math
import numpy as np
from contextlib import ExitStack
import concourse.bass as bass
import concourse.mybir as mybir
from concourse.bass_utils import run_bass_kernel_spmd

F32 = mybir.dt.float32
BF16 = mybir.dt.bfloat16
AF = mybir.ActivationFunctionType
ALU = mybir.AluOpType
AX = mybir.AxisListType

D = 2048
DC = 16
DEPTH = 4
BATCH = 4
SEQ = 2048
DEC_BATCH = 32
DEC_SEQ = 8
PAST_LEN = 16384
HD = 64
NQH = 32
NKV = 4
WIN = 128
ROT = 16
THETA = 500000.0
RH = 16
DFF = 8192
EPS = 1e-6

NCORES = 8
NSEG = 2
TP = 1024
NSQ = 4
TS = NSQ * DEC_SEQ
T = TP + TS
NSEQ = 1 + NSEG * NSQ
TCH = [(0, 512), (512, 1024), (1024, T)]
FB = 1024
CH = 32
HT = 96
CLAMP = 70.0

ENGS = ("pe", "act", "dve", "pool", "sp")


class Buf:
    __slots__ = ("name", "w", "r")

    def __init__(self, name="", w=None):
        self.name = name
        self.w = w
        self.r = []


class DmaSem:
    __slots__ = ("sem", "count")

    def __init__(self, sem):
        self.sem = sem
        self.count = 0


class Op:
    __slots__ = ("eng", "idx", "fn", "waits", "signal", "count", "dsem", "dcount")

    def __init__(self, eng, idx, fn):
        self.eng = eng
        self.idx = idx
        self.fn = fn
        self.waits = []
        self.signal = False
        self.count = 0
        self.dsem = None
        self.dcount = 0


class Sched:
    def __init__(self, nc, stack):
        self.nc = nc
        self.stack = stack
        self.ops = {e: [] for e in ENGS}
        self.sems = {e: stack.enter_context(nc.semaphore("s_" + e)) for e in ENGS}
        self.wm = {a: {b: -1 for b in ENGS} for a in ENGS}
        self.dwm = {a: {} for a in ENGS}
        self.nsem = 0
        self.fence = None
        self.pools = {}
        self.pool_i = {}
        self.all_dsems = []

    def make_pool(self, eng, n):
        self.pools[eng] = [self.dma_sem() for _ in range(n)]
        self.pool_i[eng] = 0

    def dma_sem(self, name=None):
        self.nsem += 1
        s = self.stack.enter_context(self.nc.semaphore(name or ("dq%d" % self.nsem)))
        d = DmaSem(s)
        self.all_dsems.append(d)
        return d

    def buf(self, name=""):
        return Buf(name, self.fence)

    def op(self, eng, fn, reads=(), writes=(), dsem=None):
        y = Op(eng, len(self.ops[eng]), fn)
        a = eng
        best = {}
        dbest = {}
        cands = []
        if dsem == "auto":
            pl = self.pools[eng]
            dsem = pl[self.pool_i[eng] % len(pl)]
            self.pool_i[eng] += 1
            if dsem.count > 0 and self.dwm[a].get(id(dsem), 0) < dsem.count:
                self.dwm[a][id(dsem)] = dsem.count
                y.waits.append((dsem, dsem.count))
        for b in reads:
            cands.append(b.w)
        for b in writes:
            cands.append(b.w)
            cands.extend(b.r)
        for x in cands:
            if x is None:
                continue
            if x.dsem is not None:
                if x.dsem is dsem:
                    continue
                key = id(x.dsem)
                if key not in dbest or dbest[key].dcount < x.dcount:
                    dbest[key] = x
            else:
                if x.eng == "pe" and a == "pe":
                    continue
                if x.eng not in best or best[x.eng].idx < x.idx:
                    best[x.eng] = x
        for key, x in dbest.items():
            if self.dwm[a].get(key, 0) >= x.dcount:
                continue
            self.dwm[a][key] = x.dcount
            y.waits.append((x.dsem, x.dcount))
        for b_, x in best.items():
            if self.wm[a][b_] >= x.idx:
                continue
            self.wm[a][b_] = x.idx
            y.waits.append(x)
        for b in reads:
            b.r.append(y)
        for b in writes:
            b.w = y
            b.r = []
        if dsem is not None:
            dsem.count += 16
            y.dsem = dsem
            y.dcount = dsem.count
        self.ops[eng].append(y)
        return y

    def emit(self, final_waits=()):
        nc = self.nc
        for e in ENGS:
            for y in self.ops[e]:
                for w in y.waits:
                    if isinstance(w, Op):
                        w.signal = True
        for e in ENGS:
            c = 0
            for y in self.ops[e]:
                if y.signal:
                    c += 1
                    y.count = c
        sems = self.sems
        ops = self.ops
        self.stats = {e: (len(ops[e]), sum(1 for y in ops[e] if y.signal),
                          sum(len(y.waits) for y in ops[e])) for e in ENGS}

        def run(e, h):
            for y in ops[e]:
                for w in y.waits:
                    if isinstance(w, Op):
                        h.wait_ge(sems[w.eng], w.count)
                    else:
                        h.wait_ge(w[0].sem, w[1])
                ins = y.fn(h)
                if y.dsem is not None:
                    ins.then_inc(y.dsem.sem, 16)
                elif y.signal:
                    ins.then_inc(sems[e], 1)
            if e == "sp":
                for d in self.all_dsems:
                    if d.count > 0:
                        h.wait_ge(d.sem, d.count)

        with nc.Block() as block:
            @block.tensor
            def _(h):
                run("pe", h)

            @block.scalar
            def _(h):
                run("act", h)

            @block.vector
            def _(h):
                run("dve", h)

            @block.gpsimd
            def _(h):
                run("pool", h)

            @block.sync
            def _(h):
                run("sp", h)


class Defer:
    def __init__(self, lag):
        self.q = []
        self.lag = lag

    def push(self, fn):
        self.q.append(fn)
        while len(self.q) > self.lag:
            self.q.pop(0)()

    def flush(self):
        while self.q:
            self.q.pop(0)()


class Prog:
    def __init__(self, nlayers=DEPTH, nseg=NSEG, layers=None, do_mlp=True, nheads=RH):
        self.nlayers = nlayers
        self.nseg = nseg
        self.layers = list(range(nlayers)) if layers is None else layers
        self.do_mlp = do_mlp
        self.nheads = nheads
        self.nc = bass.Bass("TRN2", target_bir_lowering=False)
        self.out_sems = []

    def mm(self, out, lhsT, rhs, start, stop, r, w):
        self.S.op("pe", lambda e: e.matmul(out, lhsT, rhs, start=start, stop=stop), r, w)

    def tr(self, out, in_, ident, r, w):
        self.S.op("pe", lambda e: e.transpose(out, in_, ident), r, w)

    def act(self, out, in_, func, r, w, **kw):
        self.S.op("act", lambda e: e.activation(out, in_, func, **kw), r, w)

    def tt(self, out, in0, in1, op, r, w, eng="dve"):
        self.S.op(eng, lambda e: e.tensor_tensor(out, in0, in1, op), r, w)

    def ts(self, out, in0, s1, s2, op0, op1, r, w, eng="dve"):
        if op1 is None:
            self.S.op(eng, lambda e: e.tensor_scalar(out, in0, s1, None, op0), r, w)
        else:
            self.S.op(eng, lambda e: e.tensor_scalar(out, in0, s1, s2, op0, op1), r, w)

    def stt(self, out, in0, scalar, in1, op0, op1, r, w, eng="dve"):
        self.S.op(eng, lambda e: e.scalar_tensor_tensor(out, in0, scalar, in1, op0, op1), r, w)

    def cp(self, out, in_, r, w, eng="dve"):
        if eng == "act":
            self.S.op("act", lambda e: e.activation(out, in_, AF.Copy), r, w)
        else:
            self.S.op(eng, lambda e: e.tensor_copy(out, in_), r, w)

    def red(self, out, in_, r, w):
        self.S.op("dve", lambda e: e.reduce_sum(out, in_, axis=AX.X), r, w)

    def recip(self, out, in_, r, w):
        self.S.op("dve", lambda e: e.reciprocal(out, in_), r, w)

    def scan(self, out, d0, d1, init, r, w):
        self.S.op("dve", lambda e: e.tensor_tensor_scan(out, d0, d1, init, ALU.mult, ALU.add), r, w)

    def mset(self, ap, val, w, eng="dve"):
        self.S.op(eng, lambda e: e.memset(ap, val), (), w)

    def dma(self, eng, out, in_, r, w, dsem="auto"):
        self.S.op(eng, lambda e: e.dma_start(out=out, in_=in_), r, w, dsem)

    def sb(self, st, name, shape, dt):
        self.nsb = getattr(self, "nsb", 0) + 1
        return st.enter_context(self.nc.sbuf_tensor("%s_%d" % (name, self.nsb), shape, dt))

    def fence(self, bufs):
        op = self.S.op("dve", lambda e: e.memset(self.fence_t[:, 0:1], 0.0), (), list(bufs) + [self.fence_b] + self.psb + self.ptb)
        self.S.fence = op

    def rsqrt(self, out, in_, scale, bias, r, w, tmp, tmpb):
        self.act(tmp, in_, AF.Ln, r, [tmpb], scale=scale, bias=bias)
        self.act(out, tmp, AF.Exp, [tmpb], w, scale=-0.5)

    def wget(self, srcs, kc, ncols):
        i = self.wn % len(self.wr_t)
        self.wn += 1
        view = self.wr_t[i][:, 0:kc * ncols].rearrange("p (k n) -> p k n", k=kc)
        for src, c0, n in srcs:
            self.dma("pool", view[:, :, c0:c0 + n], src, (), [self.wr_b[i]], self.wr_s[i])
        return view, self.wr_b[i]

    def build(self):
        nc = self.nc
        L = self.nlayers
        NS = self.nseg
        di = lambda name, shape: nc.dram_tensor(name, list(shape), F32, kind="ExternalInput").ap()
        do = lambda name, shape: nc.dram_tensor(name, list(shape), F32, kind="ExternalOutput").ap()
        self.d_xT = di("xT", [NS, D, T])
        self.d_cT = di("cT", [128, DC * NSEQ])
        self.d_rope = di("rope", [NS, 128, 9 * 16])
        self.d_ck = di("ck", [2, NS, 128, NSQ * 256])
        self.d_cv = di("cv", [2, NS, 128, NSQ * 256])
        self.d_st = di("st", [2, NS * NSQ, RH, 128, 128])
        self.d_hflag = di("hflag", [NS, 128, 1])
        self.d_masks = di("masks", [128, 1792])
        self.d_hmask = di("hmask", [128, 128 + T + 4 * 32 + 32])
        self.d_gainT = di("gainT", [128, DEPTH * 2 * DC])
        self.d_badaT = di("badaT", [128, DEPTH * 96])
        self.d_qkg = di("qkg", [2, 2, HD])
        self.d_sinks = di("sinks", [2, NQH])
        self.d_lbT = di("lbT", [128, 2 * RH])
        self.d_ogT = di("ogT", [128, 2 * RH])
        self.d_wada = di("w_ada", [DEPTH, D, 6 * D])
        self.d_wqkv = di("attn_w_qkv", [2, D, 2560])
        self.d_wo_a = di("attn_w_o", [2, D, D])
        self.d_win = di("rec_w_in", [2, D, 4 * D])
        self.d_wo_r = di("rec_w_o", [2, D, D])
        self.d_wup = di("mlp_w_up", [DEPTH, D, DFF])
        self.d_wdn = di("mlp_w_down", [DEPTH, DFF, D])
        self.o_yT = do("yT", [NS, D, T])
        self.o_kwp = do("kwp", [2, NS, 128, 256])
        self.o_vwp = do("vwp", [2, NS, 128, 256])
        self.o_kws = do("kws", [2, NS * NSQ, 128, 256])
        self.o_vws = do("vws", [2, NS * NSQ, 128, 256])
        self.o_hsp = do("hsp", [2, NS, RH, 128, 128])
        self.o_hss = do("hss", [2, NS * NSQ, RH, 128, 128])

        with ExitStack() as st:
            self.S = S = Sched(nc, st)
            sb = lambda name, shape, dt: self.sb(st, name, shape, dt)
            self.x_t = sb("x_t", [128, DC * T], F32)
            self.xv = self.x_t[:].rearrange("p (c t) -> p c t", c=DC)
            self.xb = [[Buf("x%d_%d" % (c, k)) for k in range(3)] for c in range(DC)]
            self.h_t = sb("h_t", [128, DC * T], BF16)
            self.hv = self.h_t[:].rearrange("p (c t) -> p c t", c=DC)
            self.hb = [Buf("h%d" % k) for k in range(3)]
            self.wr_t = [sb("wr%d" % i, [128, 8192], BF16) for i in range(2)]
            self.wr_b = [Buf("wr%d" % i) for i in range(2)]
            self.wr_s = [S.dma_sem() for i in range(2)]
            self.wn = 0
            self.mod_t = sb("mod_t", [128, DEPTH * 6 * DC * NSEQ], F32)
            self.modv = self.mod_t[:].rearrange("p (l j c s) -> p l j c s", l=DEPTH, j=6, c=DC)
            self.modb = [Buf("mod%d" % l_) for l_ in range(DEPTH)]
            self.fence_t = sb("fence_t", [128, 2], F32)
            self.fence_b = Buf("fence")
            self.ones_t = sb("ones_t", [128, 128], BF16)
            self.ident_t = sb("ident_t", [128, 128], BF16)
            self.masks_t = sb("masks_t", [128, 1792], BF16)
            self.hmask_t = sb("hmask_t", [128, 128 + T + 4 * 32 + 32], BF16)
            self.rope_t = sb("rope_t", [128, 9 * 16], F32)
            self.ropeb = Buf("rope")
            self.hflag_t = sb("hflag_t", [128, 1], F32)
            self.mp0_t = sb("mp0_t", [128, 512], BF16)
            self.mp0b = Buf("mp0")
            self.cb = Buf("consts")
            self.gain_t = sb("gain_t", [128, DEPTH * 2 * DC], F32)
            self.bada_t = sb("bada_t", [128, DEPTH * 96], F32)
            self.sc_t = sb("sc_t", [128, DC * NSEQ], BF16)
            self.gbb = Buf("gb")
            self.ada_s = [S.dma_sem() for i in range(2)]
            self.adan = 0
            self.lb_t = sb("lb_t", [128, 2 * RH], F32)
            self.og_t = sb("og_t", [128, 2 * RH], F32)
            self.halo_k = sb("halo_k", [128, 2 * NKV * 128], BF16)
            self.halo_v = sb("halo_v", [128, 2 * NKV * 65], BF16)
            self.halob = [[Buf("halo%d_%d" % (j, g)) for g in range(NKV)] for j in range(2)]
            self.ps = [st.enter_context(nc.psum_tensor("ps%d" % i, [128, 512], F32)) for i in range(6)]
            self.psb = [Buf("ps%d" % i) for i in range(6)]
            self.pt = [st.enter_context(nc.psum_tensor("pt%d" % i, [128, 1024], BF16)) for i in range(2)]
            self.ptb = [Buf("pt%d" % i) for i in range(2)]
            self.ptn = 0
            S.make_pool("sp", 20)
            S.make_pool("pool", 6)
            self.pn = 0

            self.load_consts(st)
            self.compute_mod(st)
            for seg in range(self.nseg):
                self.segment(seg)
            S.emit()
        return nc

    def load_consts(self, st):
        S = self.S
        self.dma("pool", self.masks_t[:], self.d_masks, (), [self.cb])
        self.dma("pool", self.hmask_t[:], self.d_hmask, (), [self.cb])
        self.dma("sp", self.og_t[:], self.d_ogT, (), [self.cb])
        self.dma("sp", self.lb_t[:], self.d_lbT, (), [self.cb])
        self.mset(self.ones_t[:], 1.0, [self.cb])
        self.cp(self.ident_t[:], self.masks_t[:, 1664:1792], [self.cb], [self.cb])
        lb = self.lb_t
        self.tt(lb[:, 0:16], lb[:, 16:32], lb[:, 0:16], ALU.subtract, [self.cb], [self.cb])
        self.act(lb[:, 0:16], lb[:, 0:16], AF.Sigmoid, [self.cb], [self.cb])
        self.ts(lb[:, 16:32], lb[:, 0:16], -1.0, 1.0, ALU.mult, ALU.add, [self.cb], [self.cb])
        self.mset(self.halo_k[:], 0.0, [b for row in self.halob for b in row])
        self.mset(self.halo_v[:], 0.0, [b for row in self.halob for b in row])

    def m_cur(self):
        return self.masks_t[:, 0:512].rearrange("p (j t) -> p j t", j=4)

    def m_prev(self):
        return self.masks_t[:, 512:1024].rearrange("p (j t) -> p j t", j=4)

    def m_sprev(self, b):
        return self.masks_t[:, 1024 + b * 128:1024 + (b + 1) * 128].rearrange("p (j t) -> p j t", j=4)

    def m_scur(self):
        return self.masks_t[0:32, 1536:1664].rearrange("p (j t) -> p j t", j=4)

    def compute_mod(self, st):
        S = self.S
        with ExitStack() as ph:
            cT = self.sb(ph, "cT", [128, DC * NSEQ], F32)
            cTb = S.buf("cT")
            self.dma("sp", cT[:], self.d_cT, (), [cTb])
            self.dma("sp", self.gain_t[:], self.d_gainT, (), [self.gbb])
            self.dma("sp", self.bada_t[:], self.d_badaT, (), [self.gbb])
            self.act(self.sc_t[:], cT[:], AF.Silu, [cTb], [self.gbb])
            wt = [self.sb(ph, "adaw%d" % i, [128, 4096], BF16) for i in range(2)]
            wtb = [S.buf() for i in range(2)]
            for _ in self.ada_gen(self.layers[0], wt, wtb, [4, 5]):
                pass
            self.fence([cTb] + wtb)

    def ada_gen(self, l, wt, wtb, banks):
        S = self.S
        scv = self.sc_t[:].rearrange("p (c s) -> p c s", c=DC)
        wv = self.d_wada[l].rearrange("(kc p) n -> p kc n", p=128)
        gbb = self.gbb
        for nt in range(48):
            i = self.adan % 2
            self.adan += 1
            w = wt[i][:].rearrange("p (k n) -> p k n", k=DC)
            self.dma("pool", w, wv[:, :, nt * 256:(nt + 1) * 256], (), [wtb[i]], self.ada_s[i])
            for q in range(2):
                nch = nt * 2 + q
                j, c = divmod(nch, DC)
                pi = banks[j % len(banks)]
                pso = self.ps[pi][:, c * NSEQ:(c + 1) * NSEQ]
                for kc in range(DC):
                    self.mm(pso, w[:, kc, q * 128:(q + 1) * 128], scv[:, kc, :], kc == 0, kc == DC - 1,
                            [wtb[i], gbb], [self.psb[pi]])
                if c == DC - 1:
                    bias = self.bada_t[:, l * 96 + j * 16:l * 96 + (j + 1) * 16].unsqueeze(2).to_broadcast([128, DC, NSEQ])
                    self.tt(self.modv[:, l, j], self.ps[pi][:, 0:DC * NSEQ].rearrange("p (c s) -> p c s", c=DC),
                            bias, ALU.add, [self.psb[pi], gbb], [self.modb[l]])
            yield
        for jn, j in ((0, 1), (1, 4)):
            g = self.gain_t[:, (l * 2 + jn) * DC:(l * 2 + jn + 1) * DC].unsqueeze(2).to_broadcast([128, DC, NSEQ])
            self.ts(self.modv[:, l, j], self.modv[:, l, j], 1.0, math.sqrt(D), ALU.add, ALU.mult,
                    [self.modb[l]], [self.modb[l]])
            self.tt(self.modv[:, l, j], self.modv[:, l, j], g, ALU.mult, [self.modb[l], gbb], [self.modb[l]])
        yield

    def tokmod(self, dst, dstb, l, j, seg):
        src = self.modv[:, l, j, :, 1 + seg * NSQ:1 + (seg + 1) * NSQ].unsqueeze(3).to_broadcast([128, DC, NSQ, DEC_SEQ])
        self.cp(dst.rearrange("p c (b i) -> p c b i", b=NSQ), src, [self.modb[l]], [dstb])

    def norm_mod(self, l, which, seg):
        S = self.S
        js, jh = (1, 0) if which == 0 else (4, 3)
        with ExitStack() as ph:
            sq = self.sb(ph, "sq", [128, DC * 512], BF16)
            sqb = S.buf("sq")
            rs = self.sb(ph, "rs", [128, 512], F32)
            rsb = S.buf("rs")
            ln = self.sb(ph, "ln", [128, 512], F32)
            lnb = S.buf("ln")
            tmp = [self.sb(ph, "ntmp%d" % i, [128, 512], F32) for i in range(2)]
            tmpb = [S.buf("ntmp%d" % i) for i in range(2)]
            gst = self.sb(ph, "gst", [128, DC * TS], F32)
            sht = self.sb(ph, "sht", [128, DC * TS], F32)
            big = self.sb(ph, "nbig", [128, DC * TS], F32)
            gsb, shb, bigb = S.buf(), S.buf(), S.buf()
            gsv = gst[:].rearrange("p (c t) -> p c t", c=DC)
            shv = sht[:].rearrange("p (c t) -> p c t", c=DC)
            bigv = big[:].rearrange("p (c t) -> p c t", c=DC)
            self.tokmod(gsv, gsb, l, js, seg)
            self.tokmod(shv, shb, l, jh, seg)
            n = 0
            for k, (t0, t1) in enumerate(TCH):
                W = t1 - t0
                xr = [self.xb[c][k] for c in range(DC)]
                sqv = sq[:, 0:DC * W].rearrange("p (c t) -> p c t", c=DC)
                self.act(sqv, self.xv[:, :, t0:t1], AF.Square, xr, [sqb])
                pi = 5
                for c in range(DC):
                    self.mm(self.ps[pi][:, 0:W], self.ones_t[:], sqv[:, c, :], c == 0, c == DC - 1,
                            [sqb, self.cb], [self.psb[pi]])
                self.rsqrt(rs[:, 0:W], self.ps[pi][:, 0:W], 1.0, D * EPS, [self.psb[pi]], [rsb], ln[:, 0:W], lnb)
                if k < 2:
                    for c in range(DC):
                        i = n % 2
                        n += 1
                        self.stt(tmp[i][:, 0:W], self.xv[:, c, t0:t1], self.modv[:, l, js, c, 0:1], rs[:, 0:W],
                                 ALU.mult, ALU.mult, [self.xb[c][k], self.modb[l], rsb], [tmpb[i]])
                        self.act(self.hv[:, c, t0:t1], tmp[i][:, 0:W], AF.Identity, [tmpb[i], self.modb[l]], [self.hb[k]],
                                 bias=self.modv[:, l, jh, c, 0:1])
                else:
                    rbc = rs[:, 0:W].unsqueeze(1).to_broadcast([128, DC, W])
                    self.tt(bigv, self.xv[:, :, t0:t1], rbc, ALU.mult, xr + [rsb], [bigb])
                    self.tt(bigv, bigv, gsv, ALU.mult, [bigb, gsb], [bigb])
                    self.tt(self.hv[:, :, t0:t1], bigv, shv, ALU.add, [bigb, shb], [self.hb[k]])
            self.fence([sqb, rsb, lnb, gsb, shb, bigb] + tmpb)

    def resid_add(self, pso, psb, l, jg, nch, k, seg, gtok, gtokb, stmp, stmpb):
        t0, t1 = TCH[k]
        if k < 2:
            self.stt(self.xv[:, nch, t0:t1], pso, self.modv[:, l, jg, nch, 0:1], self.xv[:, nch, t0:t1],
                     ALU.mult, ALU.add, [psb, self.modb[l], self.xb[nch][k]], [self.xb[nch][k]])
        else:
            self.tt(stmp, pso, gtok[:, nch, :], ALU.mult, [psb, gtokb], [stmpb])
            self.tt(self.xv[:, nch, t0:t1], self.xv[:, nch, t0:t1], stmp, ALU.add, [stmpb, self.xb[nch][k]], [self.xb[nch][k]])

    def proj_resid(self, w, wb, nkc, rhs_v, rhs_b, l, jg, seg, gtok, gtokb, stmp, stmpb):
        for nch in range(DC):
            for k, (t0, t1) in enumerate(TCH):
                pi = self.pn % 6
                self.pn += 1
                pso = self.ps[pi][:, 0:t1 - t0]
                for kc in range(nkc):
                    self.mm(pso, w[:, kc, nch * 128:(nch + 1) * 128], rhs_v[:, kc, t0:t1], kc == 0, kc == nkc - 1,
                            [wb] + rhs_b(kc, k), [self.psb[pi]])
                self.resid_add(pso, self.psb[pi], l, jg, nch, k, seg, gtok, gtokb, stmp, stmpb)

    def mlp(self, l, seg):
        S = self.S
        with ExitStack() as ph:
            a_t = self.sb(ph, "a_t", [128, 8 * T], BF16)
            av = a_t[:].rearrange("p (c t) -> p c t", c=8)
            ab = [[S.buf("a%d_%d" % (c, k)) for k in range(3)] for c in range(8)]
            rl = [self.sb(ph, "rl%d" % i, [128, 512], BF16) for i in range(2)]
            rlb = [S.buf() for i in range(2)]
            gtok = self.sb(ph, "g2tok", [128, DC * TS], F32)
            gtokv = gtok[:].rearrange("p (c t) -> p c t", c=DC)
            gtokb = S.buf()
            stmp = self.sb(ph, "stmp", [128, TS], F32)
            stmpb = S.buf()
            self.tokmod(gtokv, gtokb, l, 5, seg)
            li = self.layers.index(l)
            adag = None
            nbk = 6
            adab = []
            if seg == 0 and li + 1 < len(self.layers):
                wt = [self.sb(ph, "adaw%d" % i, [128, 4096], BF16) for i in range(2)]
                adab = [S.buf() for i in range(2)]
                adag = self.ada_gen(self.layers[li + 1], wt, adab, [5])
                nbk = 5

            def ada_step(n):
                nonlocal adag
                for _ in range(n):
                    if adag is None:
                        return
                    try:
                        next(adag)
                    except StopIteration:
                        adag = None
            nstep = 0
            wu = self.d_wup[l].rearrange("(kc p) n -> p kc n", p=128)
            wd = self.d_wdn[l].rearrange("(kc p) n -> p kc n", p=128)
            nr = 0
            for b in range(DFF // FB):
                for half in range(2):
                    c0 = b * FB + half * 512
                    w, wb = self.wget([(wu[:, :, c0:c0 + 512], 0, 512)], 16, 512)
                    for q in range(4):
                        fc = half * 4 + q
                        for k, (t0, t1) in enumerate(TCH):
                            W = t1 - t0
                            pi = self.pn % nbk
                            self.pn += 1
                            pso = self.ps[pi][:, 0:W]
                            for kc in range(DC):
                                self.mm(pso, w[:, kc, q * 128:(q + 1) * 128], self.hv[:, kc, t0:t1], kc == 0, kc == DC - 1,
                                        [wb, self.hb[k]], [self.psb[pi]])
                            i = nr % 2
                            nr += 1
                            self.act(rl[i][:, 0:W], pso, AF.Relu, [self.psb[pi]], [rlb[i]])
                            self.tt(av[:, fc, t0:t1], rl[i][:, 0:W], rl[i][:, 0:W], ALU.mult, [rlb[i]], [ab[fc][k]])
                    nstep += 1
                    ada_step(2 if nstep % 2 == 0 else 1)
                for half in range(2):
                    w, wb = self.wget([(wd[:, b * 8:(b + 1) * 8, half * 1024:(half + 1) * 1024], 0, 1024)], 8, 1024)
                    for q in range(8):
                        nch = half * 8 + q
                        for k, (t0, t1) in enumerate(TCH):
                            pi = self.pn % nbk
                            self.pn += 1
                            pso = self.ps[pi][:, 0:t1 - t0]
                            for kc in range(8):
                                self.mm(pso, w[:, kc, q * 128:(q + 1) * 128], av[:, kc, t0:t1], kc == 0, kc == 7,
                                        [wb, ab[kc][k]], [self.psb[pi]])
                            self.resid_add(pso, self.psb[pi], l, 5, nch, k, seg, gtokv, gtokb, stmp[:], stmpb)
                    nstep += 1
                    ada_step(2 if nstep % 2 == 0 else 1)
            ada_step(1000)
            self.fence([b_ for row in ab for b_ in row] + rlb + [gtokb, stmpb] + adab)

    def qk_norm_rope(self, P, nh, src, srcb, gain_bc, tt_, sc, scb, out_bf, outb, out_f32=None, out_f32b=None, dup=False):
        S = self.S
        sqf, ss, ssl, rs_, qn = sc["sqf"], sc["ss"], sc["ssl"], sc["rs"], sc["qn"]
        srcv = src.rearrange("p (h d) -> p h d", h=nh)
        sqv = sqf[0:P, 0:nh * 64].rearrange("p (h d) -> p h d", h=nh)
        qnv = qn[0:P, 0:nh * 64].rearrange("p (h d) -> p h d", h=nh)
        self.act(sqv, srcv, AF.Square, [srcb], [scb])
        self.red(ss[0:P, 0:nh], sqv, [scb], [scb])
        self.rsqrt(rs_[0:P, 0:nh], ss[0:P, 0:nh], 1.0 / HD, EPS, [scb], [scb], ssl[0:P, 0:nh], scb)
        self.tt(qnv, srcv, rs_[0:P, 0:nh].unsqueeze(2).to_broadcast([P, nh, HD]), ALU.mult, [srcb, scb], [scb])
        self.tt(qnv, qnv, gain_bc[0:P].unsqueeze(1).to_broadcast([P, nh, HD]), ALU.mult, [scb, self.attb], [scb])
        cos = self.rope_t[0:P, tt_ * 16:tt_ * 16 + 8].unsqueeze(1).to_broadcast([P, nh, 8])
        sin = self.rope_t[0:P, tt_ * 16 + 8:tt_ * 16 + 16].unsqueeze(1).to_broadcast([P, nh, 8])
        t1 = sc["t1"][0:P, 0:nh * 8].rearrange("p (h d) -> p h d", h=nh)
        t2 = sc["t2"][0:P, 0:nh * 8].rearrange("p (h d) -> p h d", h=nh)
        t3 = sc["t3"][0:P, 0:nh * 8].rearrange("p (h d) -> p h d", h=nh)
        t4 = sc["t4"][0:P, 0:nh * 8].rearrange("p (h d) -> p h d", h=nh)
        x1 = qnv[:, :, 0:8]
        x2 = qnv[:, :, 8:16]
        rb = [scb, self.ropeb]
        self.tt(t1, x1, cos, ALU.mult, rb, [scb])
        self.tt(t2, x2, sin, ALU.mult, rb, [scb])
        self.tt(t3, x2, cos, ALU.mult, rb, [scb])
        self.tt(t4, x1, sin, ALU.mult, rb, [scb])
        self.tt(x1, t1, t2, ALU.subtract, [scb], [scb])
        self.tt(x2, t3, t4, ALU.add, [scb], [scb])
        if dup:
            self.cp(out_bf, qn[0:P, 0:HD].unsqueeze(1).to_broadcast([P, 2, HD]), [scb], [outb], eng="act")
        else:
            self.cp(out_bf, qnv, [scb], [outb], eng="act")
        if out_f32 is not None:
            self.cp(out_f32, qn[0:P, 0:nh * 64], [scb], [out_f32b])

    def ptile(self):
        i = self.ptn % 2
        self.ptn += 1
        return self.pt[i], self.ptb[i]

    def attn_layer(self, l, seg):
        S = self.S
        j = l // 2
        last_seg = (seg == self.nseg - 1)
        with ExitStack() as ph:
            sb = lambda name, shape, dt: self.sb(ph, name, shape, dt)
            bufs = []

            def nb(name=""):
                b = S.buf(name)
                bufs.append(b)
                return b
            self.attb = nb("attc")
            qg_t = sb("qg_t", [128, HD], F32)
            kg_t = sb("kg_t", [128, HD], F32)
            es_t = sb("es_t", [128, NQH], F32)
            self.dma("sp", qg_t[:], self.d_qkg[j, 0].partition_broadcast(128), (), [self.attb])
            self.dma("sp", kg_t[:], self.d_qkg[j, 1].partition_broadcast(128), (), [self.attb])
            self.dma("sp", es_t[:], self.d_sinks[j].partition_broadcast(128), (), [self.attb])
            self.act(es_t[:], es_t[:], AF.Exp, [self.attb], [self.attb])
            ck_t = sb("ck_t", [128, NSQ * HD], F32)
            cv_t = sb("cv_t", [128, NSQ * HD], F32)
            ckb = nb("ck")
            for b in range(NSQ):
                sq_ = seg * NSQ + b
                self.dma("sp", self.o_kws[j, sq_, 0:120, :], self.d_ck[j, seg][8:128, b * 256:(b + 1) * 256], (), ())
                self.dma("sp", self.o_vws[j, sq_, 0:120, :], self.d_cv[j, seg][8:128, b * 256:(b + 1) * 256], (), ())
            gtok = sb("g1tok", [128, DC * TS], F32)
            gtokv = gtok[:].rearrange("p (c t) -> p c t", c=DC)
            gtokb = nb()
            stmp = sb("stmp", [128, TS], F32)
            stmpb = nb()
            self.tokmod(gtokv, gtokb, l, 2, seg)
            qT = sb("qT", [128, 4 * T], BF16)
            qTv = qT[:].rearrange("p (c t) -> p c t", c=4)
            qTb = [nb("qT%d" % i) for i in range(9)]
            oT = sb("oT", [128, 4 * T], BF16)
            oTv = oT[:].rearrange("p (c t) -> p c t", c=4)
            oTb = [nb("oT%d" % k) for k in range(3)]
            kT = sb("kT", [128, 9 * 128 + TS], BF16)
            kTb = [nb("kT%d" % i) for i in range(10)]
            va = sb("va", [128, 10 * 65], BF16)
            vav = va[:].rearrange("p (b d) -> p b d", d=65)
            vab = [nb("va%d" % i) for i in range(10)]
            ckT = sb("ckT", [128, NSQ * 128], BF16)
            ckTb = nb("ckT")
            cva = sb("cva", [128, NSQ * 65], BF16)
            cvav = cva[:].rearrange("p (b d) -> p b d", d=65)
            cvab = nb("cva")
            ckd = sb("ckd", [128, 128], BF16)
            ckdb = nb()
            PT = [sb("PT%d" % i, [128, 512], BF16) for i in range(6)]
            PTb = [nb() for i in range(6)]
            PTs = [sb("PTs%d" % i, [128, 128], BF16) for i in range(5)]
            PTsb = [nb() for i in range(5)]
            sc = {"sqf": sb("sqf", [128, 512], F32), "ss": sb("ss", [128, 8], F32), "ssl": sb("ssl", [128, 8], F32),
                  "rs": sb("rs", [128, 8], F32), "qn": sb("qn", [128, 512], F32),
                  "t1": sb("t1", [128, 64], F32), "t2": sb("t2", [128, 64], F32),
                  "t3": sb("t3", [128, 64], F32), "t4": sb("t4", [128, 64], F32)}
            scb = nb("qsc")
            qr = [sb("qr%d" % i, [128, 512], BF16) for i in range(2)]
            qrb = [nb() for i in range(2)]
            kf = [sb("kf%d" % i, [128, HD], F32) for i in range(2)]
            kfb = [nb() for i in range(2)]
            vf = [sb("vf%d" % i, [128, HD], F32) for i in range(2)]
            vfb = [nb() for i in range(2)]
            kr = sb("kr", [128, HD], BF16)
            krb = nb()
            krd = sb("krd", [128, 128], BF16)
            krdb = nb()
            den = sb("den", [128, 4], F32)
            denb = nb()
            otm = [sb("otm%d" % i, [128, 512], BF16) for i in range(2)]
            otmb = [nb() for i in range(2)]
            self.dma("sp", self.rope_t[:], self.d_rope[seg], (), [self.ropeb])
            self.dma("sp", self.hflag_t[:], self.d_hflag[seg], (), [self.mp0b])
            self.ts(self.mp0_t[:], self.masks_t[:, 512:1024], self.hflag_t[:, 0:1], None, ALU.mult, None,
                    [self.cb, self.mp0b], [self.mp0b])
            mp0 = self.mp0_t[:].rearrange("p (j t) -> p j t", j=4)
            wq_all = self.d_wqkv[j].rearrange("(kc p) n -> p kc n", p=128)
            wo_all = self.d_wo_a[j].rearrange("(kc p) n -> p kc n", p=128)
            nq = 0
            nkf = 0
            self.mset(vav[:, :, 64:65], 1.0, vab)
            self.mset(cvav[:, :, 64:65], 1.0, [cvab])
            for g in range(NKV):
                self.cp(kT[:, 0:128], self.halo_k[:, (j * NKV + g) * 128:(j * NKV + g + 1) * 128], [self.halob[j][g]], [kTb[0]])
                self.cp(vav[:, 0, 0:64], self.halo_v[:, (j * NKV + g) * 65:(j * NKV + g) * 65 + 64], [self.halob[j][g]], [vab[0]])
                w, wb = self.wget([(wq_all[:, :, g * 512:(g + 1) * 512], 0, 512)], 16, 512)
                dq = Defer(1)
                for tt_ in range(9):
                    P = 128 if tt_ < 8 else TS
                    t0 = tt_ * 128
                    k = min(tt_ // 4, 2)
                    pi = self.pn % 2
                    self.pn += 1
                    pso = self.ps[pi][0:P, :]
                    for kc in range(DC):
                        self.mm(pso, self.hv[:, kc, t0:t0 + P], w[:, kc, :], kc == 0, kc == DC - 1,
                                [wb, self.hb[k]], [self.psb[pi]])
                    i = nq % 2
                    nq += 1

                    def q_post(P=P, t0=t0, pi=pi, i=i, tt_=tt_, pso=pso):
                        self.qk_norm_rope(P, 8, pso, self.psb[pi], qg_t, tt_, sc, scb,
                                          qr[i][0:P, :].rearrange("p (h d) -> p h d", h=8), qrb[i])
                        ptile, ptb = self.ptile()
                        for c in range(4):
                            self.tr(ptile[:, c * 128:c * 128 + P], qr[i][0:P, c * 128:(c + 1) * 128], self.ident_t[0:P, 0:P],
                                    [qrb[i], self.cb], [ptb])
                        self.cp(qTv[:, :, t0:t0 + P], ptile[:, 0:512].rearrange("p (c t) -> p c t", c=4)[:, :, 0:P],
                                [ptb], [qTb[tt_]], eng="act")
                    dq.push(q_post)
                wkv = self.d_wqkv[j].rearrange("(kc p) n -> p kc n", p=128)
                w, wb = self.wget([(wkv[:, :, 2048 + g * 64:2048 + (g + 1) * 64], 0, 64),
                                   (wkv[:, :, 2304 + g * 64:2304 + (g + 1) * 64], 64, 64)], 16, 128)
                for tt_ in range(9):
                    P = 128 if tt_ < 8 else TS
                    t0 = tt_ * 128
                    k = min(tt_ // 4, 2)
                    pi = self.pn % 2
                    self.pn += 1
                    pso = self.ps[pi][0:P, 0:128]
                    for kc in range(DC):
                        self.mm(pso, self.hv[:, kc, t0:t0 + P], w[:, kc, :], kc == 0, kc == DC - 1,
                                [wb, self.hb[k]], [self.psb[pi]])
                    i = nkf % 2
                    nkf += 1

                    def kv_post(P=P, t0=t0, pi=pi, i=i, tt_=tt_, g=g):
                        wantf = tt_ >= 7
                        self.qk_norm_rope(P, 1, self.ps[pi][0:P, 0:64], self.psb[pi], kg_t, tt_, sc, scb,
                                          krd[0:P, :].rearrange("p (r d) -> p r d", r=2), krdb,
                                          kf[i][0:P, :] if wantf else None, kfb[i], dup=True)
                        if wantf:
                            self.cp(vf[i][0:P, :], self.ps[pi][0:P, 64:128], [self.psb[pi]], [vfb[i]], eng="act")
                        blk = tt_ + 1
                        self.cp(vav[0:P, blk, 0:64], self.ps[pi][0:P, 64:128], [self.psb[pi]], [vab[blk]])
                        ptile, ptb = self.ptile()
                        self.tr(ptile[:, 0:P], krd[0:P, :], self.ident_t[0:P, 0:P], [krdb, self.cb], [ptb])
                        self.cp(kT[:, blk * 128:blk * 128 + P], ptile[:, 0:P], [ptb], [kTb[blk]])
                        if tt_ == 7:
                            self.dma("sp", self.o_kwp[j, seg][:, g * 64:(g + 1) * 64], kf[i][:, :], [kfb[i]], ())
                            self.dma("sp", self.o_vwp[j, seg][:, g * 64:(g + 1) * 64], vf[i][:, :], [vfb[i]], ())
                            self.cp(self.halo_k[:, (j * NKV + g) * 128:(j * NKV + g + 1) * 128], kT[:, 8 * 128:9 * 128],
                                    [kTb[8]], [self.halob[j][g]])
                            self.cp(self.halo_v[:, (j * NKV + g) * 65:(j * NKV + g) * 65 + 64], vav[:, 8, 0:64],
                                    [vab[8]], [self.halob[j][g]])
                        if tt_ == 8:
                            for b in range(NSQ):
                                sq_ = seg * NSQ + b
                                self.dma("sp", self.o_kws[j, sq_, 120:128, g * 64:(g + 1) * 64], kf[i][b * 8:(b + 1) * 8, :], [kfb[i]], ())
                                self.dma("sp", self.o_vws[j, sq_, 120:128, g * 64:(g + 1) * 64], vf[i][b * 8:(b + 1) * 8, :], [vfb[i]], ())
                    dq.push(kv_post)
                dq.flush()
                ckv = ck_t[:].rearrange("p (b f) -> p b f", b=NSQ)
                cvv = cv_t[:].rearrange("p (b f) -> p b f", b=NSQ)
                self.dma("sp", ckv, self.d_ck[j, seg].rearrange("p (b f) -> p b f", b=NSQ)[:, :, g * 64:(g + 1) * 64], (), [ckb])
                self.dma("sp", cvv, self.d_cv[j, seg].rearrange("p (b f) -> p b f", b=NSQ)[:, :, g * 64:(g + 1) * 64], (), [ckb])
                for b in range(NSQ):
                    self.cp(ckd[:].rearrange("p (r d) -> p r d", r=2),
                            ckv[:, b, :].unsqueeze(1).to_broadcast([128, 2, HD]), [ckb], [ckdb])
                    ptile, ptb = self.ptile()
                    self.tr(ptile[:, 0:128], ckd[:], self.ident_t[:], [ckdb, self.cb], [ptb])
                    self.cp(ckT[:, b * 128:(b + 1) * 128], ptile[:, 0:128], [ptb], [ckTb], eng="act")
                self.cp(cvav[:, :, 0:64], cvv, [ckb], [cvab])
                NPT = len(PT)
                d1, d2, d3 = Defer(1), Defer(1), Defer(1)
                un = 0
                for i in range(8):
                    for par in range(2):
                        pp = slice(par * 64, (par + 1) * 64)
                        rhs = qTv[pp, :, i * 128:(i + 1) * 128]
                        bp = (un % 2) * 2
                        pts = [(2 * un) % NPT, (2 * un + 1) % NPT]
                        un += 1
                        for kbi, blk in enumerate((i, i + 1)):
                            self.mm(self.ps[bp + kbi][:, :], kT[pp, blk * 128:(blk + 1) * 128], rhs, True, True,
                                    [kTb[blk], qTb[i]], [self.psb[bp + kbi]])

                        def st1(i=i, par=par, bp=bp, pts=pts, g=g):
                            for kbi in range(2):
                                pb_ = pts[kbi]
                                self.act(PT[pb_][:], self.ps[bp + kbi][:, :], AF.Exp, [self.psb[bp + kbi]], [PTb[pb_]],
                                         scale=HD ** -0.5)
                                m = (mp0 if i == 0 else self.m_prev()) if kbi == 0 else self.m_cur()
                                mb = [self.mp0b] if (kbi == 0 and i == 0) else [self.cb]
                                self.tt(PT[pb_][:].rearrange("p (j t) -> p j t", j=4), PT[pb_][:].rearrange("p (j t) -> p j t", j=4),
                                        m, ALU.mult, [PTb[pb_]] + mb, [PTb[pb_]])

                            def st2():
                                po = 4 + par
                                for jj in range(4):
                                    for kbi, blk in enumerate((i, i + 1)):
                                        self.mm(self.ps[po][:, jj * 65:(jj + 1) * 65], PT[pts[kbi]][:, jj * 128:(jj + 1) * 128],
                                                vav[:, blk, :], kbi == 0, kbi == 1, [PTb[pts[kbi]], vab[blk]], [self.psb[po]])
                                pov = self.ps[po][:, 0:260].rearrange("p (j d) -> p j d", d=65)
                                self.tt(den[:, :], pov[:, :, 64], es_t[:, g * 8 + par:g * 8 + 8:2], ALU.add,
                                        [self.psb[po], self.attb], [denb])
                                self.recip(den[:, :], den[:, :], [denb], [denb])
                                oi = i % 2
                                ov = otm[oi][:].rearrange("p (j r d) -> p j r d", j=4, r=2)[:, :, par, :]
                                self.tt(ov, pov[:, :, 0:64], den[:, :].unsqueeze(2).to_broadcast([128, 4, HD]), ALU.mult,
                                        [self.psb[po], denb], [otmb[oi]])
                                if par == 1:
                                    def st3():
                                        ptile, ptb = self.ptile()
                                        for c in range(4):
                                            self.tr(ptile[:, c * 128:(c + 1) * 128], otm[oi][:, c * 128:(c + 1) * 128], self.ident_t[:],
                                                    [otmb[oi], self.cb], [ptb])
                                        self.cp(oTv[:, :, i * 128:(i + 1) * 128], ptile[:, 0:512].rearrange("p (c t) -> p c t", c=4),
                                                [ptb], [oTb[i // 4]], eng="act")
                                    d3.push(st3)
                            d2.push(st2)
                        d1.push(st1)
                d1.flush()
                d2.flush()
                d3.flush()
                for par in range(2):
                    pp = slice(par * 64, (par + 1) * 64)
                    rhs = qTv[pp, :, TP:T]
                    for b in range(NSQ + 1):
                        pi = 2 + (b % 2)
                        if b < NSQ:
                            Pk = 128
                            self.mm(self.ps[pi][:, 0:128], ckT[pp, b * 128:(b + 1) * 128], rhs, True, True,
                                    [ckTb, qTb[8]], [self.psb[pi]])
                            m = self.m_sprev(b)
                        else:
                            Pk = TS
                            self.mm(self.ps[pi][0:Pk, 0:128], kT[pp, 9 * 128:9 * 128 + TS], rhs, True, True,
                                    [kTb[9], qTb[8]], [self.psb[pi]])
                            m = self.m_scur()
                        self.act(PTs[b][0:Pk, :], self.ps[pi][0:Pk, 0:128], AF.Exp, [self.psb[pi]], [PTsb[b]], scale=HD ** -0.5)
                        self.tt(PTs[b][0:Pk, :].rearrange("p (j t) -> p j t", j=4),
                                PTs[b][0:Pk, :].rearrange("p (j t) -> p j t", j=4), m, ALU.mult,
                                [PTsb[b], self.cb], [PTsb[b]])
                    po = 4 + par
                    for jj in range(4):
                        for b in range(NSQ + 1):
                            if b < NSQ:
                                self.mm(self.ps[po][0:TS, jj * 65:(jj + 1) * 65], PTs[b][:, jj * 32:(jj + 1) * 32],
                                        cvav[:, b, :], b == 0, False, [PTsb[b], cvab], [self.psb[po]])
                            else:
                                self.mm(self.ps[po][0:TS, jj * 65:(jj + 1) * 65], PTs[b][0:TS, jj * 32:(jj + 1) * 32],
                                        vav[0:TS, 9, :], False, True, [PTsb[b], vab[9]], [self.psb[po]])
                    pov = self.ps[po][0:TS, 0:260].rearrange("p (j d) -> p j d", d=65)
                    self.tt(den[0:TS, :], pov[:, :, 64], es_t[0:TS, g * 8 + par:g * 8 + 8:2], ALU.add,
                            [self.psb[po], self.attb], [denb])
                    self.recip(den[0:TS, :], den[0:TS, :], [denb], [denb])
                    ov = otm[0][0:TS, :].rearrange("p (j r d) -> p j r d", j=4, r=2)[:, :, par, :]
                    self.tt(ov, pov[:, :, 0:64], den[0:TS, :].unsqueeze(2).to_broadcast([TS, 4, HD]), ALU.mult,
                            [self.psb[po], denb], [otmb[0]])
                ptile, ptb = self.ptile()
                for c in range(4):
                    self.tr(ptile[:, c * 128:c * 128 + TS], otm[0][0:TS, c * 128:(c + 1) * 128], self.ident_t[0:TS, 0:TS],
                            [otmb[0], self.cb], [ptb])
                self.cp(oTv[:, :, TP:T], ptile[:, 0:512].rearrange("p (c t) -> p c t", c=4)[:, :, 0:TS],
                        [ptb], [oTb[2]], eng="act")
                w, wb = self.wget([(wo_all[:, g * 4:(g + 1) * 4, :], 0, 2048)], 4, 2048)
                self.proj_resid(w, wb, 4, oTv, lambda kc, k: [oTb[k]], l, 2, seg, gtokv, gtokb, stmp[:], stmpb)
            self.fence(bufs)

    def hgrn_layer(self, l, seg):
        S = self.S
        j = l // 2
        tiles = [(i * HT, min(HT, TP - i * HT)) for i in range((TP + HT - 1) // HT)]
        NT = len(tiles)
        NCK = TP // CH
        with ExitStack() as ph:
            sb = lambda name, shape, dt: self.sb(ph, name, shape, dt)
            bufs = []

            def nb(name=""):
                b = S.buf(name)
                bufs.append(b)
                return b
            gtok = sb("g1tok", [128, DC * TS], F32)
            gtokv = gtok[:].rearrange("p (c t) -> p c t", c=DC)
            gtokb = nb()
            stmp = sb("stmp", [128, TS], F32)
            stmpb = nb()
            self.tokmod(gtokv, gtokb, l, 2, seg)
            qs = sb("qs", [128, T], BF16)
            A = sb("Aa", [128, T], F32)
            B = sb("Bb", [128, T], F32)
            C = sb("Cc", [128, T], F32)
            qsb, Ab_, Bb_, Cb_ = nb("qs"), nb("A"), nb("B"), nb("C")
            qt = sb("qt", [128, T], BF16)
            kt = sb("kt", [128, T], BF16)
            kh = sb("kh", [128, T], BF16)
            qtb, ktb, khb = nb("qt"), nb("kt"), nb("kh")
            qts = sb("qts", [128, NSQ * TS], BF16)
            khs = sb("khs", [128, NSQ * TS], BF16)
            qtsb, khsb = nb(), nb()
            Dch = sb("Dch", [128, NCK + NSQ], F32)
            Dchb = nb("Dch")
            vt = sb("vt", [128, (NT + 1) * 128], BF16)
            vtv = vt[:].rearrange("p (i d) -> p i d", d=128)
            vtb = nb("vt")
            sg = sb("sg", [128, (NT + 1) * 128], BF16)
            sgv = sg[:].rearrange("p (i d) -> p i d", d=128)
            sgb = nb("sg")
            kht = sb("kht", [128, NT * 128], BF16)
            khtv = kht[:].rearrange("p (i d) -> p i d", d=128)
            khtb = nb("kht")
            khts = sb("khts", [128, NSQ * 128], BF16)
            khtsv = khts[:].rearrange("p (i d) -> p i d", d=128)
            khtsb = nb("khts")
            NSB = 6
            Sb = sb("Sb", [128, NSB * 128], BF16)
            Sbv = Sb[:].rearrange("p (i d) -> p i d", d=128)
            Sbb = [nb("Sb%d" % i) for i in range(NSB)]
            St2 = [sb("St%d" % i, [128, 128], F32) for i in range(2)]
            St2b = [nb("St%d" % i) for i in range(2)]
            S0 = sb("S0", [128, NSQ * 128], F32)
            S0v = S0[:].rearrange("p (i d) -> p i d", d=128)
            S0b = [nb("S0_%d" % i) for i in range(NSQ)]
            S0h = sb("S0h", [128, NSQ * 128], BF16)
            S0hv = S0h[:].rearrange("p (i d) -> p i d", d=128)
            S0hb = nb("S0h")
            Am = [sb("Am%d" % i, [128, HT], BF16) for i in range(2)]
            Amb = [nb() for i in range(2)]
            og = [sb("og%d" % i, [128, 128], BF16) for i in range(2)]
            ogb = [nb() for i in range(2)]
            ss = sb("hss_", [128, 4], F32)
            ssb = nb()
            junk = sb("junk", [128, 128], F32)
            junkb = nb()
            oT = sb("oTr", [128, 4 * T], BF16)
            oTv = oT[:].rearrange("p (c t) -> p c t", c=4)
            oTb = [nb("oT%d" % k) for k in range(3)]
            ub = [nb("u%d" % i) for i in range(4)]
            ob4 = [nb("o%d" % i) for i in range(4)]
            bd = self.hmask_t[:, 0:128]
            reset = self.hmask_t[:, 128:128 + T]
            seqm = self.hmask_t[:, 128 + T:128 + T + 128].rearrange("p (b t) -> p b t", b=NSQ)
            bds = self.hmask_t[0:TS, 128 + T + 128:128 + T + 128 + TS]
            win = self.d_win[j].rearrange("(kc p) (g n) -> p kc g n", p=128, g=4)
            wo_all = self.d_wo_r[j].rearrange("(kc p) n -> p kc n", p=128)
            hspb = self.hspb
            nA = 0
            nO = 0
            nU = 0
            nog = 0
            dO = Defer(1)
            def fm_gen(w, wb, banks):
                n_ = 0
                for which, dst, dstb, func in ((0, qs, qsb, AF.Silu), (1, A, Ab_, AF.Sigmoid)):
                    for k, (t0, t1) in enumerate(TCH):
                        pi = banks[n_ % len(banks)]
                        n_ += 1
                        pso = self.ps[pi][:, 0:t1 - t0]
                        for kc in range(DC):
                            self.mm(pso, w[:, kc, which * 128:(which + 1) * 128], self.hv[:, kc, t0:t1], kc == 0, kc == DC - 1,
                                    [wb, self.hb[k]], [self.psb[pi]])
                        self.act(dst[:, t0:t1], pso, func, [self.psb[pi]], [dstb])
                        yield

            def get_w(hd_):
                return self.wget([(win[:, :, gi, hd_ * 128:(hd_ + 1) * 128], gi * 128, 128) for gi in range(4)], 16, 512)
            for hd in range(self.nheads):
                w, wb = get_w(hd)
                for _ in fm_gen(w, wb, [0, 1, 2, 3, 4, 5]):
                    pass
                def gates():
                    if j == 1:
                        self.ts(A[:], A[:], self.lb_t[:, 16 + hd:17 + hd], self.lb_t[:, hd:hd + 1], ALU.mult, ALU.add,
                                [Ab_, self.cb], [Ab_])
                        yield
                    self.act(B[:], A[:], AF.Ln, [Ab_], [Bb_])
                    yield
                    self.scan(C[:], reset, B[:], 0.0, [Bb_, self.cb], [Cb_])
                    yield
                    self.ts(A[:], A[:], -1.0, 1.0, ALU.mult, ALU.add, [Ab_], [Ab_])
                    yield
                    self.act(B[:], C[:], AF.Exp, [Cb_], [Bb_])
                    yield
                    self.tt(qt[:], qs[:], B[:], ALU.mult, [qsb, Bb_], [qtb])
                    yield
                    self.ts(B[:], C[:], -1.0, CLAMP, ALU.mult, ALU.min, [Cb_], [Bb_])
                    yield
                    self.act(B[:], B[:], AF.Exp, [Bb_], [Bb_])
                    yield
                    self.tt(kt[:], A[:], B[:], ALU.mult, [Ab_, Bb_], [ktb])
                    yield
                    Cp = C[:, 0:TP].rearrange("p (c i) -> p c i", i=CH)
                    Bp = B[:, 0:TP].rearrange("p (c i) -> p c i", i=CH)
                    self.tt(Bp, C[:, CH - 1:TP:CH].unsqueeze(2).to_broadcast([128, NCK, CH]), Cp, ALU.subtract, [Cb_], [Bb_])
                    yield
                    Cs = C[:, TP:T].rearrange("p (c i) -> p c i", i=DEC_SEQ)
                    Bs = B[:, TP:T].rearrange("p (c i) -> p c i", i=DEC_SEQ)
                    self.tt(Bs, C[:, TP + DEC_SEQ - 1:T:DEC_SEQ].unsqueeze(2).to_broadcast([128, NSQ, DEC_SEQ]), Cs, ALU.subtract, [Cb_], [Bb_])
                    yield
                    self.act(B[:], B[:], AF.Exp, [Bb_], [Bb_])
                    yield
                    self.tt(kh[:], A[:], B[:], ALU.mult, [Ab_, Bb_], [khb])
                    yield
                    self.act(Dch[:, 0:NCK], C[:, CH - 1:TP:CH], AF.Exp, [Cb_], [Dchb])
                    yield
                    self.act(Dch[:, NCK:NCK + NSQ], C[:, TP + DEC_SEQ - 1:T:DEC_SEQ], AF.Exp, [Cb_], [Dchb])
                    yield
                    self.tt(qts[:].rearrange("p (b t) -> p b t", b=NSQ), qt[:, TP:T].unsqueeze(1).to_broadcast([128, NSQ, TS]), seqm,
                            ALU.mult, [qtb, self.cb], [qtsb])
                    yield
                    self.tt(khs[:].rearrange("p (b t) -> p b t", b=NSQ), kh[:, TP:T].unsqueeze(1).to_broadcast([128, NSQ, TS]), seqm,
                            ALU.mult, [khb, self.cb], [khsb])
                    yield
                def tmtiles():
                    for which, dstv, dstb in ((2, vtv, vtb), (3, sgv, sgb)):
                        for ti in range(NT + 1):
                            t0, P = tiles[ti] if ti < NT else (TP, TS)
                            k = 2 if ti == NT else (0 if t0 + P <= 512 else 1)
                            hbs = [self.hb[k]] if (ti == NT or t0 >= 512 or t0 + P <= 512) else [self.hb[0], self.hb[1]]
                            pi = self.pn % 6
                            self.pn += 1
                            pso = self.ps[pi][0:P, 0:128]
                            for kc in range(DC):
                                self.mm(pso, self.hv[:, kc, t0:t0 + P], w[:, kc, which * 128:(which + 1) * 128], kc == 0, kc == DC - 1,
                                        [wb] + hbs, [self.psb[pi]])
                            if which == 2:
                                self.cp(dstv[0:P, ti, :], pso, [self.psb[pi]], [dstb])
                            else:
                                self.act(dstv[0:P, ti, :], pso, AF.Silu, [self.psb[pi]], [dstb])
                            yield
                ga, tmg = gates(), tmtiles()
                live = [ga, ga, tmg]
                while live:
                    for gen in list(live):
                        if gen not in live:
                            continue
                        try:
                            next(gen)
                        except StopIteration:
                            while gen in live:
                                live.remove(gen)
                for ti in range(NT):
                    t0, P = tiles[ti]
                    ptile, ptb = self.ptile()
                    self.tr(ptile[0:P, 0:128], kh[:, t0:t0 + P], self.ident_t[:], [khb, self.cb], [ptb])
                    self.cp(khtv[0:P, ti, :], ptile[0:P, 0:128], [ptb], [khtb], eng="act")
                ptile, ptb = self.ptile()
                for b in range(NSQ):
                    self.tr(ptile[0:TS, b * 128:(b + 1) * 128], khs[:, b * TS:(b + 1) * TS], self.ident_t[:], [khsb, self.cb], [ptb])
                self.cp(khts[0:TS, :], ptile[0:TS, 0:NSQ * 128], [ptb], [khtsb], eng="act")
                fmg = None

                def fm_step():
                    nonlocal fmg
                    if fmg is not None:
                        try:
                            next(fmg)
                        except StopIteration:
                            fmg = None
                if seg == 0:
                    self.mset(St2[0][:], 0.0, [St2b[0]])
                else:
                    self.dma("sp", St2[0][:], self.o_hsp[j, seg - 1, hd], [hspb[j][hd]], [St2b[0]])
                self.cp(Sbv[:, 0, :], St2[0][:], [St2b[0]], [Sbb[0]], eng="act")
                for c in range(NCK):
                    ti, a = divmod(c, HT // CH)
                    u = 4 + nU % 2
                    nU += 1
                    uo = self.ps[u][:, 0:128]
                    self.mm(uo, khtv[a * CH:(a + 1) * CH, ti, :], vtv[a * CH:(a + 1) * CH, ti, :], True, True,
                            [khtb, vtb], [self.psb[u]])
                    si, so = c % 2, (c + 1) % 2
                    self.stt(St2[so][:], St2[si][:], Dch[:, c:c + 1], uo, ALU.mult, ALU.add,
                             [St2b[si], Dchb, self.psb[u]], [St2b[so]])
                    if c < NCK - 1:
                        self.cp(Sbv[:, (c + 1) % NSB, :], St2[so][:], [St2b[so]], [Sbb[(c + 1) % NSB]], eng="act")
                    if c % 5 == 1:
                        fm_step()
                    if a == HT // CH - 1 or c == NCK - 1:
                        t0, P = tiles[ti]
                        ai = nA % 2
                        nA += 1
                        pa = self.ps[ai]
                        self.mm(pa[0:P, 0:P], kt[:, t0:t0 + P], qt[:, t0:t0 + P], True, True, [ktb, qtb], [self.psb[ai]])
                        self.tt(Am[ai][0:P, 0:P], pa[0:P, 0:P], bd[0:P, 0:P], ALU.mult, [self.psb[ai], self.cb], [Amb[ai]])
                        ob_ = 2 + ai
                        po = self.ps[ob_][:, 0:128]
                        nchk = P // CH
                        self.mm(po[0:P, :], Am[ai][0:P, 0:P], vtv[0:P, ti, :], True, False, [Amb[ai], vtb], [self.psb[ob_]])
                        for a2 in range(nchk):
                            c2 = ti * (HT // CH) + a2
                            self.mm(po[a2 * CH:(a2 + 1) * CH, :], qt[:, t0 + a2 * CH:t0 + (a2 + 1) * CH], Sbv[:, c2 % NSB, :],
                                    False, True, [qtb, Sbb[c2 % NSB]], [self.psb[ob_]])

                        def o_post(po=po, ob_=ob_, P=P, ti=ti, t0=t0, nog=nog, hd=hd):
                            self.hgrn_out(po[0:P, :], self.psb[ob_], P, sgv[0:P, ti, :], sgb, ss, ssb, junk, junkb, og, ogb, nog,
                                          oTv[:, hd % 4, t0:t0 + P], oTb, t0, P, j, hd)
                        dO.push(o_post)
                        nog += 1
                self.dma("sp", self.o_hsp[j, seg, hd], St2[NCK % 2][:], [St2b[NCK % 2]], [hspb[j][hd]])
                for b in range(NSQ):
                    self.dma("sp", S0v[:, b, :], self.d_st[j, seg * NSQ + b, hd], (), [S0b[b]])
                self.cp(S0h[:], S0[:], S0b, [S0hb], eng="act")
                dO.flush()
                while fmg is not None:
                    fm_step()
                ai = nA % 2
                nA += 1
                pa = self.ps[ai]
                self.mm(pa[0:TS, 0:TS], kt[:, TP:T], qt[:, TP:T], True, True, [ktb, qtb], [self.psb[ai]])
                self.tt(Am[ai][0:TS, 0:TS], pa[0:TS, 0:TS], bds, ALU.mult, [self.psb[ai], self.cb], [Amb[ai]])
                ob_ = 2 + ai
                po = self.ps[ob_][:, 0:128]
                self.mm(po[0:TS, :], Am[ai][0:TS, 0:TS], vtv[0:TS, NT, :], True, False, [Amb[ai], vtb], [self.psb[ob_]])
                for b in range(NSQ):
                    self.mm(po[0:TS, :], qts[:, b * TS:(b + 1) * TS], S0hv[:, b, :], False, b == NSQ - 1, [qtsb, S0hb], [self.psb[ob_]])
                self.hgrn_out(po[0:TS, :], self.psb[ob_], TS, sgv[0:TS, NT, :], sgb, ss, ssb, junk, junkb, og, ogb, nog,
                              oTv[:, hd % 4, TP:T], oTb, TP, TS, j, hd)
                nog += 1
                for b in range(NSQ):
                    u = 4 + nU % 2
                    nU += 1
                    uo = self.ps[u][:, 0:128]
                    self.mm(uo, khtsv[0:TS, b, :], vtv[0:TS, NT, :], True, True, [khtsb, vtb], [self.psb[u]])
                    self.stt(S0v[:, b, :], S0v[:, b, :], Dch[:, NCK + b:NCK + b + 1], uo, ALU.mult, ALU.add,
                             [S0b[b], Dchb, self.psb[u], S0hb], [S0b[b]])
                    self.dma("sp", self.o_hss[j, seg * NSQ + b, hd], S0v[:, b, :], [S0b[b]], ())
                if hd % 4 == 3:
                    g = hd // 4
                    w2, wb2 = self.wget([(wo_all[:, g * 4:(g + 1) * 4, :], 0, 2048)], 4, 2048)
                    self.proj_resid(w2, wb2, 4, oTv, lambda kc, k: [oTb[k]], l, 2, seg, gtokv, gtokb, stmp[:], stmpb)
            self.fence(bufs)

    def hgrn_out(self, po, pob, P, sgt, sgb, ss, ssb, junk, junkb, og, ogb, n, oT_dst, oTb, t0, P2, j, hd):
        i = n % 2
        col = n % 2
        self.act(junk[0:P, :], po, AF.Square, [pob], [junkb], accum_out=ss[0:P, col:col + 1])
        self.rsqrt(ss[0:P, 2 + col:3 + col], ss[0:P, col:col + 1], 1.0 / 128.0, EPS, [junkb], [ssb], junk[0:P, 0:1], junkb)
        self.stt(og[i][0:P, :], po, ss[0:P, 2 + col:3 + col], sgt, ALU.mult, ALU.mult, [pob, ssb, sgb], [ogb[i]])
        ptile, ptb = self.ptile()
        self.tr(ptile[:, 0:P], og[i][0:P, :], self.ident_t[0:P, 0:P], [ogb[i], self.cb], [ptb])
        ks = sorted(set([min(t0 // 512, 2), min((t0 + P - 1) // 512, 2)]))
        self.ts(oT_dst, ptile[:, 0:P], self.og_t[:, j * RH + hd:j * RH + hd + 1], None, ALU.mult, None,
                [ptb, self.cb], [oTb[k] for k in ks])

    def segment(self, seg):
        xT = self.d_xT[seg].rearrange("(c p) t -> p c t", p=128)
        for c in range(DC):
            self.dma("sp", self.xv[:, c, :], xT[:, c, :], (), self.xb[c])
        for l in self.layers:
            self.norm_mod(l, 0, seg)
            if l % 2 == 0:
                self.attn_layer(l, seg)
            else:
                self.hgrn_layer(l, seg)
            if self.do_mlp:
                self.norm_mod(l, 1, seg)
                self.mlp(l, seg)
        yT = self.o_yT[seg].rearrange("(c p) t -> p c t", p=128)
        for c in range(DC):
            self.dma("sp", yT[:, c, :], self.xv[:, c, :], self.xb[c], ())


def _rope_tables():
    half = ROT // 2
    inv = (np.float32(THETA) ** (-np.arange(half, dtype=np.float32) / np.float32(half))).astype(np.float32)
    out = np.zeros((NSEG, 128, 9, 16), np.float32)
    for seg in range(NSEG):
        for tt in range(9):
            if tt < 8:
                pos = (seg * TP + tt * 128 + np.arange(128)).astype(np.float32)
            else:
                pos = (PAST_LEN + (np.arange(128) % DEC_SEQ)).astype(np.float32)
            ang = pos[:, None] * inv[None, :]
            out[seg, :, tt, 0:8] = np.cos(ang)
            out[seg, :, tt, 8:16] = np.sin(ang)
    return out.reshape(NSEG, 128, 9 * 16)


def _masks():
    m = np.zeros((128, 1792), np.float32)
    s = np.arange(128)[:, None]
    t = np.arange(128)[None, :]
    cur = (s <= t).astype(np.float32)
    prev = (s > t).astype(np.float32)
    m[:, 0:512] = np.tile(cur, (1, 4))
    m[:, 512:1024] = np.tile(prev, (1, 4))
    ts_ = np.arange(TS)[None, :]
    for b in range(NSQ):
        mm_ = ((ts_ // DEC_SEQ == b) & (s > (ts_ % DEC_SEQ))).astype(np.float32)
        m[:, 1024 + b * 128:1024 + (b + 1) * 128] = np.tile(mm_, (1, 4))
    s32 = np.arange(TS)[:, None]
    sc = ((s32 // DEC_SEQ == ts_ // DEC_SEQ) & (s32 % DEC_SEQ <= ts_ % DEC_SEQ)).astype(np.float32)
    m[0:TS, 1536:1664] = np.tile(sc, (1, 4))
    m[:, 1664:1792] = np.eye(128, dtype=np.float32)
    hm = np.zeros((128, 128 + T + 128 + TS), np.float32)
    hm[:, 0:128] = ((s // CH == t // CH) & (s <= t)).astype(np.float32)
    reset = np.ones(T, np.float32)
    reset[0:TP:CH] = 0.0
    reset[TP:T:DEC_SEQ] = 0.0
    hm[:, 128:128 + T] = reset[None, :]
    sq = np.zeros((NSQ, TS), np.float32)
    for b in range(NSQ):
        sq[b, b * DEC_SEQ:(b + 1) * DEC_SEQ] = 1.0
    hm[:, 128 + T:128 + T + 128] = sq.reshape(1, -1)
    hm[0:TS, 128 + T + 128:] = ((s32 // DEC_SEQ == ts_ // DEC_SEQ) & (s32 <= ts_)).astype(np.float32)
    return m, hm


def _core_assign(core):
    b = core % BATCH
    return b, [b * 8 + i for i in range(NSEG * NSQ)]


_PROG_CACHE = {}


def _get_prog(nlayers):
    if nlayers not in _PROG_CACHE:
        p = Prog(nlayers=nlayers)
        p.hspb = [[Buf("hsp%d_%d" % (j, h)) for h in range(RH)] for j in range(2)]
        p.build()
        _PROG_CACHE[nlayers] = p
    return _PROG_CACHE[nlayers]


def kernel(x_prompt, x_sample, cache_k_win, cache_v_win, state_hgrn, c_prompt, c_sample,
           norm_gain, w_ada, b_ada, attn_w_qkv, attn_q_gain, attn_k_gain, attn_sinks, attn_w_o,
           rec_w_in, rec_lb_logits, rec_o_gain, rec_w_o, mlp_w_up, mlp_w_down, _nlayers=DEPTH, _trace=False):
    f = lambda a: np.ascontiguousarray(np.asarray(a, dtype=np.float32))
    x_prompt, x_sample = f(x_prompt), f(x_sample)
    cache_k_win, cache_v_win, state_hgrn = f(cache_k_win), f(cache_v_win), f(state_hgrn)
    c_prompt, c_sample = f(c_prompt), f(c_sample)
    masks, hmask = _masks()
    rope = _rope_tables()
    shared = {
        "masks": masks, "hmask": hmask, "rope": rope,
        "gainT": f(np.asarray(norm_gain).reshape(DEPTH, 2, DC, 128).transpose(3, 0, 1, 2).reshape(128, -1)),
        "badaT": f(np.asarray(b_ada).reshape(DEPTH, 96, 128).transpose(2, 0, 1).reshape(128, -1)),
        "qkg": f(np.stack([np.asarray(attn_q_gain), np.asarray(attn_k_gain)], axis=1)),
        "sinks": f(attn_sinks),
        "lbT": f(np.asarray(rec_lb_logits).reshape(2, RH, 128).transpose(2, 0, 1).reshape(128, -1)),
        "ogT": f(np.asarray(rec_o_gain).reshape(2, RH, 128).transpose(2, 0, 1).reshape(128, -1)),
        "w_ada": f(w_ada), "attn_w_qkv": f(attn_w_qkv), "attn_w_o": f(attn_w_o), "rec_w_in": f(rec_w_in),
        "rec_w_o": f(rec_w_o), "mlp_w_up": f(mlp_w_up), "mlp_w_down": f(mlp_w_down),
    }
    hflag = np.zeros((NSEG, 128, 1), np.float32)
    hflag[1:] = 1.0
    in_maps = []
    for core in range(NCORES):
        b, sq = _core_assign(core)
        xT = np.empty((NSEG, D, T), np.float32)
        for seg in range(NSEG):
            xT[seg, :, 0:TP] = x_prompt[b, seg * TP:(seg + 1) * TP, :].T
            xs = x_sample[sq[seg * NSQ:(seg + 1) * NSQ]].reshape(TS, D)
            xT[seg, :, TP:T] = xs.T
        cs = np.concatenate([c_prompt[b:b + 1], c_sample[sq]], axis=0)
        cT = cs.reshape(NSEQ, DC, 128).transpose(2, 1, 0).reshape(128, DC * NSEQ)
        ck = cache_k_win[:, sq].reshape(2, NSEG, NSQ, 128, 256).transpose(0, 1, 3, 2, 4).reshape(2, NSEG, 128, NSQ * 256)
        cv = cache_v_win[:, sq].reshape(2, NSEG, NSQ, 128, 256).transpose(0, 1, 3, 2, 4).reshape(2, NSEG, 128, NSQ * 256)
        m = dict(shared)
        m.update({"xT": xT, "cT": f(cT), "ck": f(ck), "cv": f(cv), "st": f(state_hgrn[:, sq]), "hflag": hflag})
        in_maps.append(m)
    prog = _get_prog(_nlayers)
    res = run_bass_kernel_spmd(prog.nc, in_maps, core_ids=list(range(NCORES)), trace=_trace)
    R = res.results
    y_prompt = np.empty((BATCH, SEQ, D), np.float32)
    y_sample = np.empty((DEC_BATCH, DEC_SEQ, D), np.float32)
    kwp = np.empty((2, BATCH, 128, NKV, HD), np.float32)
    vwp = np.empty_like(kwp)
    kws = np.empty((2, DEC_BATCH, 128, NKV, HD), np.float32)
    vws = np.empty_like(kws)
    hsp = np.empty((2, BATCH, RH, 128, 128), np.float32)
    hss = np.empty((2, DEC_BATCH, RH, 128, 128), np.float32)
    for core in range(BATCH):
        b, sq = _core_assign(core)
        r = R[core]
        for seg in range(NSEG):
            y_prompt[b, seg * TP:(seg + 1) * TP, :] = r["yT"][seg][:, 0:TP].T
            ys = r["yT"][seg][:, TP:T].T.reshape(NSQ, DEC_SEQ, D)
            y_sample[sq[seg * NSQ:(seg + 1) * NSQ]] = ys
        kwp[:, b] = r["kwp"][:, NSEG - 1].reshape(2, 128, NKV, HD)
        vwp[:, b] = r["vwp"][:, NSEG - 1].reshape(2, 128, NKV, HD)
        kws[:, sq] = r["kws"].reshape(2, NSEG * NSQ, 128, NKV, HD)
        vws[:, sq] = r["vws"].reshape(2, NSEG * NSQ, 128, NKV, HD)
        hsp[:, b] = r["hsp"][:, NSEG - 1]
        hss[:, sq] = r["hss"]
    if _trace:
        kernel.last_exec_ns = res.exec_time_ns
    return (y_prompt, y_sample, kwp, vwp, kws, vws, hsp, hss)
```

```python
import math
import numpy as np
from contextlib import ExitStack
import concourse.bass as bass
import concourse.mybir as mybir
from concourse.bass_utils import run_bass_kernel_spmd

F32 = mybir.dt.float32
BF16 = mybir.dt.bfloat16
AF = mybir.ActivationFunctionType
ALU = mybir.AluOpType
AX = mybir.AxisListType

D = 2048
DC = 16
DEPTH = 4
BATCH = 4
SEQ = 2048
DEC_BATCH = 32
DEC_SEQ = 8
PAST_LEN = 16384
HD = 64
NQH = 32
NKV = 4
WIN = 128
ROT = 16
THETA = 500000.0
RH = 16
DFF = 8192
EPS = 1e-6

NCORES = 8
NSEG = 2
TP = 1024
NSQ = 4
TS = NSQ * DEC_SEQ
T = TP + TS
NSEQ = 1 + NSEG * NSQ
TCH = [(0, 512), (512, 1024), (1024, T)]
FB = 1024
CH = 32
HT = 96
CLAMP = 70.0

ENGS = ("pe", "act", "dve", "pool", "sp")


class Buf:
    __slots__ = ("name", "w", "r")

    def __init__(self, name="", w=None):
        self.name = name
        self.w = w
        self.r = []


class DmaSem:
    __slots__ = ("sem", "count")

    def __init__(self, sem):
        self.sem = sem
        self.count = 0


class Op:
    __slots__ = ("eng", "idx", "fn", "waits", "signal", "count", "dsem", "dcount")

    def __init__(self, eng, idx, fn):
        self.eng = eng
        self.idx = idx
        self.fn = fn
        self.waits = []
        self.signal = False
        self.count = 0
        self.dsem = None
        self.dcount = 0


class Sched:
    def __init__(self, nc, stack):
        self.nc = nc
        self.stack = stack
        self.ops = {e: [] for e in ENGS}
        self.sems = {e: stack.enter_context(nc.semaphore("s_" + e)) for e in ENGS}
        self.wm = {a: {b: -1 for b in ENGS} for a in ENGS}
        self.dwm = {a: {} for a in ENGS}
        self.nsem = 0
        self.fence = None
        self.pools = {}
        self.pool_i = {}
        self.all_dsems = []

    def make_pool(self, eng, n):
        self.pools[eng] = [self.dma_sem() for _ in range(n)]
        self.pool_i[eng] = 0

    def dma_sem(self, name=None):
        self.nsem += 1
        s = self.stack.enter_context(self.nc.semaphore(name or ("dq%d" % self.nsem)))
        d = DmaSem(s)
        self.all_dsems.append(d)
        return d

    def buf(self, name=""):
        return Buf(name, self.fence)

    def op(self, eng, fn, reads=(), writes=(), dsem=None):
        y = Op(eng, len(self.ops[eng]), fn)
        a = eng
        best = {}
        dbest = {}
        cands = []
        if dsem == "auto":
            pl = self.pools[eng]
            dsem = pl[self.pool_i[eng] % len(pl)]
            self.pool_i[eng] += 1
            if dsem.count > 0 and self.dwm[a].get(id(dsem), 0) < dsem.count:
                self.dwm[a][id(dsem)] = dsem.count
                y.waits.append((dsem, dsem.count))
        for b in reads:
            cands.append(b.w)
        for b in writes:
            cands.append(b.w)
            cands.extend(b.r)
        for x in cands:
            if x is None:
                continue
            if x.dsem is not None:
                if x.dsem is dsem:
                    continue
                key = id(x.dsem)
                if key not in dbest or dbest[key].dcount < x.dcount:
                    dbest[key] = x
            else:
                if x.eng == "pe" and a == "pe":
                    continue
                if x.eng not in best or best[x.eng].idx < x.idx:
                    best[x.eng] = x
        for key, x in dbest.items():
            if self.dwm[a].get(key, 0) >= x.dcount:
                continue
            self.dwm[a][key] = x.dcount
            y.waits.append((x.dsem, x.dcount))
        for b_, x in best.items():
            if self.wm[a][b_] >= x.idx:
                continue
            self.wm[a][b_] = x.idx
            y.waits.append(x)
        for b in reads:
            b.r.append(y)
        for b in writes:
            b.w = y
            b.r = []
        if dsem is not None:
            dsem.count += 16
            y.dsem = dsem
            y.dcount = dsem.count
        self.ops[eng].append(y)
        return y

    def emit(self, final_waits=()):
        nc = self.nc
        for e in ENGS:
            for y in self.ops[e]:
                for w in y.waits:
                    if isinstance(w, Op):
                        w.signal = True
        for e in ENGS:
            c = 0
            for y in self.ops[e]:
                if y.signal:
                    c += 1
                    y.count = c
        sems = self.sems
        ops = self.ops
        self.stats = {e: (len(ops[e]), sum(1 for y in ops[e] if y.signal),
                          sum(len(y.waits) for y in ops[e])) for e in ENGS}

        def run(e, h):
            for y in ops[e]:
                for w in y.waits:
                    if isinstance(w, Op):
                        h.wait_ge(sems[w.eng], w.count)
                    else:
                        h.wait_ge(w[0].sem, w[1])
                ins = y.fn(h)
                if y.dsem is not None:
                    ins.then_inc(y.dsem.sem, 16)
                elif y.signal:
                    ins.then_inc(sems[e], 1)
            if e == "sp":
                for d in self.all_dsems:
                    if d.count > 0:
                        h.wait_ge(d.sem, d.count)

        with nc.Block() as block:
            @block.tensor
            def _(h):
                run("pe", h)

            @block.scalar
            def _(h):
                run("act", h)

            @block.vector
            def _(h):
                run("dve", h)

            @block.gpsimd
            def _(h):
                run("pool", h)

            @block.sync
            def _(h):
                run("sp", h)


class Defer:
    def __init__(self, lag):
        self.q = []
        self.lag = lag

    def push(self, fn):
        self.q.append(fn)
        while len(self.q) > self.lag:
            self.q.pop(0)()

    def flush(self):
        while self.q:
            self.q.pop(0)()


class Prog:
    def __init__(self, nlayers=DEPTH, nseg=NSEG, layers=None, do_mlp=True, nheads=RH):
        self.nlayers = nlayers
        self.nseg = nseg
        self.layers = list(range(nlayers)) if layers is None else layers
        self.do_mlp = do_mlp
        self.nheads = nheads
        self.nc = bass.Bass("TRN2", target_bir_lowering=False)
        self.out_sems = []

    def mm(self, out, lhsT, rhs, start, stop, r, w):
        self.S.op("pe", lambda e: e.matmul(out, lhsT, rhs, start=start, stop=stop), r, w)

    def tr(self, out, in_, ident, r, w):
        self.S.op("pe", lambda e: e.transpose(out, in_, ident), r, w)

    def act(self, out, in_, func, r, w, **kw):
        self.S.op("act", lambda e: e.activation(out, in_, func, **kw), r, w)

    def tt(self, out, in0, in1, op, r, w, eng="dve"):
        self.S.op(eng, lambda e: e.tensor_tensor(out, in0, in1, op), r, w)

    def ts(self, out, in0, s1, s2, op0, op1, r, w, eng="dve"):
        if op1 is None:
            self.S.op(eng, lambda e: e.tensor_scalar(out, in0, s1, None, op0), r, w)
        else:
            self.S.op(eng, lambda e: e.tensor_scalar(out, in0, s1, s2, op0, op1), r, w)

    def stt(self, out, in0, scalar, in1, op0, op1, r, w, eng="dve"):
        self.S.op(eng, lambda e: e.scalar_tensor_tensor(out, in0, scalar, in1, op0, op1), r, w)

    def cp(self, out, in_, r, w, eng="dve"):
        if eng == "act":
            self.S.op("act", lambda e: e.activation(out, in_, AF.Copy), r, w)
        else:
            self.S.op(eng, lambda e: e.tensor_copy(out, in_), r, w)

    def red(self, out, in_, r, w):
        self.S.op("dve", lambda e: e.reduce_sum(out, in_, axis=AX.X), r, w)

    def recip(self, out, in_, r, w):
        self.S.op("dve", lambda e: e.reciprocal(out, in_), r, w)

    def scan(self, out, d0, d1, init, r, w):
        self.S.op("dve", lambda e: e.tensor_tensor_scan(out, d0, d1, init, ALU.mult, ALU.add), r, w)

    def mset(self, ap, val, w, eng="dve"):
        self.S.op(eng, lambda e: e.memset(ap, val), (), w)

    def dma(self, eng, out, in_, r, w, dsem="auto"):
        self.S.op(eng, lambda e: e.dma_start(out=out, in_=in_), r, w, dsem)

    def sb(self, st, name, shape, dt):
        self.nsb = getattr(self, "nsb", 0) + 1
        return st.enter_context(self.nc.sbuf_tensor("%s_%d" % (name, self.nsb), shape, dt))

    def fence(self, bufs):
        op = self.S.op("dve", lambda e: e.memset(self.fence_t[:, 0:1], 0.0), (), list(bufs) + [self.fence_b] + self.psb + self.ptb)
        self.S.fence = op

    def rsqrt(self, out, in_, scale, bias, r, w, tmp, tmpb):
        self.act(tmp, in_, AF.Ln, r, [tmpb], scale=scale, bias=bias)
        self.act(out, tmp, AF.Exp, [tmpb], w, scale=-0.5)

    def wget(self, srcs, kc, ncols):
        i = self.wn % len(self.wr_t)
        self.wn += 1
        view = self.wr_t[i][:, 0:kc * ncols].rearrange("p (k n) -> p k n", k=kc)
        for src, c0, n in srcs:
            self.dma("pool", view[:, :, c0:c0 + n], src, (), [self.wr_b[i]], self.wr_s[i])
        return view, self.wr_b[i]

    def build(self):
        nc = self.nc
        L = self.nlayers
        NS = self.nseg
        di = lambda name, shape: nc.dram_tensor(name, list(shape), F32, kind="ExternalInput").ap()
        do = lambda name, shape: nc.dram_tensor(name, list(shape), F32, kind="ExternalOutput").ap()
        self.d_xT = di("xT", [NS, D, T])
        self.d_cT = di("cT", [128, DC * NSEQ])
        self.d_rope = di("rope", [NS, 128, 9 * 16])
        self.d_ck = di("ck", [2, NS, 128, NSQ * 256])
        self.d_cv = di("cv", [2, NS, 128, NSQ * 256])
        self.d_st = di("st", [2, NS * NSQ, RH, 128, 128])
        self.d_hflag = di("hflag", [NS, 128, 1])
        self.d_masks = di("masks", [128, 1792])
        self.d_hmask = di("hmask", [128, 128 + T + 4 * 32 + 32])
        self.d_gainT = di("gainT", [128, DEPTH * 2 * DC])
        self.d_badaT = di("badaT", [128, DEPTH * 96])
        self.d_qkg = di("qkg", [2, 2, HD])
        self.d_sinks = di("sinks", [2, NQH])
        self.d_lbT = di("lbT", [128, 2 * RH])
        self.d_ogT = di("ogT", [128, 2 * RH])
        self.d_wada = di("w_ada", [DEPTH, D, 6 * D])
        self.d_wqkv = di("attn_w_qkv", [2, D, 2560])
        self.d_wo_a = di("attn_w_o", [2, D, D])
        self.d_win = di("rec_w_in", [2, D, 4 * D])
        self.d_wo_r = di("rec_w_o", [2, D, D])
        self.d_wup = di("mlp_w_up", [DEPTH, D, DFF])
        self.d_wdn = di("mlp_w_down", [DEPTH, DFF, D])
        self.o_yT = do("yT", [NS, D, T])
        self.o_kwp = do("kwp", [2, NS, 128, 256])
        self.o_vwp = do("vwp", [2, NS, 128, 256])
        self.o_kws = do("kws", [2, NS * NSQ, 128, 256])
        self.o_vws = do("vws", [2, NS * NSQ, 128, 256])
        self.o_hsp = do("hsp", [2, NS, RH, 128, 128])
        self.o_hss = do("hss", [2, NS * NSQ, RH, 128, 128])

        with ExitStack() as st:
            self.S = S = Sched(nc, st)
            sb = lambda name, shape, dt: self.sb(st, name, shape, dt)
            self.x_t = sb("x_t", [128, DC * T], F32)
            self.xv = self.x_t[:].rearrange("p (c t) -> p c t", c=DC)
            self.xb = [[Buf("x%d_%d" % (c, k)) for k in range(3)] for c in range(DC)]
            self.h_t = sb("h_t", [128, DC * T], BF16)
            self.hv = self.h_t[:].rearrange("p (c t) -> p c t", c=DC)
            self.hb = [Buf("h%d" % k) for k in range(3)]
            self.wr_t = [sb("wr%d" % i, [128, 8192], BF16) for i in range(2)]
            self.wr_b = [Buf("wr%d" % i) for i in range(2)]
            self.wr_s = [S.dma_sem() for i in range(2)]
            self.wn = 0
            self.mod_t = sb("mod_t", [128, DEPTH * 6 * DC * NSEQ], F32)
            self.modv = self.mod_t[:].rearrange("p (l j c s) -> p l j c s", l=DEPTH, j=6, c=DC)
            self.modb = [Buf("mod%d" % l_) for l_ in range(DEPTH)]
            self.fence_t = sb("fence_t", [128, 2], F32)
            self.fence_b = Buf("fence")
            self.ones_t = sb("ones_t", [128, 128], BF16)
            self.ident_t = sb("ident_t", [128, 128], BF16)
            self.masks_t = sb("masks_t", [128, 1792], BF16)
            self.hmask_t = sb("hmask_t", [128, 128 + T + 4 * 32 + 32], BF16)
            self.rope_t = sb("rope_t", [128, 9 * 16], F32)
            self.ropeb = Buf("rope")
            self.hflag_t = sb("hflag_t", [128, 1], F32)
            self.mp0_t = sb("mp0_t", [128, 512], BF16)
            self.mp0b = Buf("mp0")
            self.cb = Buf("consts")
            self.gain_t = sb("gain_t", [128, DEPTH * 2 * DC], F32)
            self.bada_t = sb("bada_t", [128, DEPTH * 96], F32)
            self.sc_t = sb("sc_t", [128, DC * NSEQ], BF16)
            self.gbb = Buf("gb")
            self.ada_s = [S.dma_sem() for i in range(2)]
            self.adan = 0
            self.lb_t = sb("lb_t", [128, 2 * RH], F32)
            self.og_t = sb("og_t", [128, 2 * RH], F32)
            self.halo_k = sb("halo_k", [128, 2 * NKV * 128], BF16)
            self.halo_v = sb("halo_v", [128, 2 * NKV * 65], BF16)
            self.halob = [[Buf("halo%d_%d" % (j, g)) for g in range(NKV)] for j in range(2)]
            self.ps = [st.enter_context(nc.psum_tensor("ps%d" % i, [128, 512], F32)) for i in range(6)]
            self.psb = [Buf("ps%d" % i) for i in range(6)]
            self.pt = [st.enter_context(nc.psum_tensor("pt%d" % i, [128, 1024], BF16)) for i in range(2)]
            self.ptb = [Buf("pt%d" % i) for i in range(2)]
            self.ptn = 0
            S.make_pool("sp", 20)
            S.make_pool("pool", 6)
            self.pn = 0

            self.load_consts(st)
            self.compute_mod(st)
            for seg in range(self.nseg):
                self.segment(seg)
            S.emit()
        return nc

    def load_consts(self, st):
        S = self.S
        self.dma("pool", self.masks_t[:], self.d_masks, (), [self.cb])
        self.dma("pool", self.hmask_t[:], self.d_hmask, (), [self.cb])
        self.dma("sp", self.og_t[:], self.d_ogT, (), [self.cb])
        self.dma("sp", self.lb_t[:], self.d_lbT, (), [self.cb])
        self.mset(self.ones_t[:], 1.0, [self.cb])
        self.cp(self.ident_t[:], self.masks_t[:, 1664:1792], [self.cb], [self.cb])
        lb = self.lb_t
        self.tt(lb[:, 0:16], lb[:, 16:32], lb[:, 0:16], ALU.subtract, [self.cb], [self.cb])
        self.act(lb[:, 0:16], lb[:, 0:16], AF.Sigmoid, [self.cb], [self.cb])
        self.ts(lb[:, 16:32], lb[:, 0:16], -1.0, 1.0, ALU.mult, ALU.add, [self.cb], [self.cb])
        self.mset(self.halo_k[:], 0.0, [b for row in self.halob for b in row])
        self.mset(self.halo_v[:], 0.0, [b for row in self.halob for b in row])

    def m_cur(self):
        return self.masks_t[:, 0:512].rearrange("p (j t) -> p j t", j=4)

    def m_prev(self):
        return self.masks_t[:, 512:1024].rearrange("p (j t) -> p j t", j=4)

    def m_sprev(self, b):
        return self.masks_t[:, 1024 + b * 128:1024 + (b + 1) * 128].rearrange("p (j t) -> p j t", j=4)

    def m_scur(self):
        return self.masks_t[0:32, 1536:1664].rearrange("p (j t) -> p j t", j=4)

    def compute_mod(self, st):
        S = self.S
        with ExitStack() as ph:
            cT = self.sb(ph, "cT", [128, DC * NSEQ], F32)
            cTb = S.buf("cT")
            self.dma("sp", cT[:], self.d_cT, (), [cTb])
            self.dma("sp", self.gain_t[:], self.d_gainT, (), [self.gbb])
            self.dma("sp", self.bada_t[:], self.d_badaT, (), [self.gbb])
            self.act(self.sc_t[:], cT[:], AF.Silu, [cTb], [self.gbb])
            wt = [self.sb(ph, "adaw%d" % i, [128, 4096], BF16) for i in range(2)]
            wtb = [S.buf() for i in range(2)]
            for _ in self.ada_gen(self.layers[0], wt, wtb, [4, 5]):
                pass
            self.fence([cTb] + wtb)

    def ada_gen(self, l, wt, wtb, banks):
        S = self.S
        scv = self.sc_t[:].rearrange("p (c s) -> p c s", c=DC)
        wv = self.d_wada[l].rearrange("(kc p) n -> p kc n", p=128)
        gbb = self.gbb
        for nt in range(48):
            i = self.adan % 2
            self.adan += 1
            w = wt[i][:].rearrange("p (k n) -> p k n", k=DC)
            self.dma("pool", w, wv[:, :, nt * 256:(nt + 1) * 256], (), [wtb[i]], self.ada_s[i])
            for q in range(2):
                nch = nt * 2 + q
                j, c = divmod(nch, DC)
                pi = banks[j % len(banks)]
                pso = self.ps[pi][:, c * NSEQ:(c + 1) * NSEQ]
                for kc in range(DC):
                    self.mm(pso, w[:, kc, q * 128:(q + 1) * 128], scv[:, kc, :], kc == 0, kc == DC - 1,
                            [wtb[i], gbb], [self.psb[pi]])
                if c == DC - 1:
                    bias = self.bada_t[:, l * 96 + j * 16:l * 96 + (j + 1) * 16].unsqueeze(2).to_broadcast([128, DC, NSEQ])
                    self.tt(self.modv[:, l, j], self.ps[pi][:, 0:DC * NSEQ].rearrange("p (c s) -> p c s", c=DC),
                            bias, ALU.add, [self.psb[pi], gbb], [self.modb[l]])
            yield
        for jn, j in ((0, 1), (1, 4)):
            g = self.gain_t[:, (l * 2 + jn) * DC:(l * 2 + jn + 1) * DC].unsqueeze(2).to_broadcast([128, DC, NSEQ])
            self.ts(self.modv[:, l, j], self.modv[:, l, j], 1.0, math.sqrt(D), ALU.add, ALU.mult,
                    [self.modb[l]], [self.modb[l]])
            self.tt(self.modv[:, l, j], self.modv[:, l, j], g, ALU.mult, [self.modb[l], gbb], [self.modb[l]])
        yield

    def tokmod(self, dst, dstb, l, j, seg):
        src = self.modv[:, l, j, :, 1 + seg * NSQ:1 + (seg + 1) * NSQ].unsqueeze(3).to_broadcast([128, DC, NSQ, DEC_SEQ])
        self.cp(dst.rearrange("p c (b i) -> p c b i", b=NSQ), src, [self.modb[l]], [dstb])

    def norm_mod(self, l, which, seg):
        S = self.S
        js, jh = (1, 0) if which == 0 else (4, 3)
        with ExitStack() as ph:
            sq = self.sb(ph, "sq", [128, DC * 512], BF16)
            sqb = S.buf("sq")
            rs = self.sb(ph, "rs", [128, 512], F32)
            rsb = S.buf("rs")
            ln = self.sb(ph, "ln", [128, 512], F32)
            lnb = S.buf("ln")
            tmp = [self.sb(ph, "ntmp%d" % i, [128, 512], F32) for i in range(2)]
            tmpb = [S.buf("ntmp%d" % i) for i in range(2)]
            gst = self.sb(ph, "gst", [128, DC * TS], F32)
            sht = self.sb(ph, "sht", [128, DC * TS], F32)
            big = self.sb(ph, "nbig", [128, DC * TS], F32)
            gsb, shb, bigb = S.buf(), S.buf(), S.buf()
            gsv = gst[:].rearrange("p (c t) -> p c t", c=DC)
            shv = sht[:].rearrange("p (c t) -> p c t", c=DC)
            bigv = big[:].rearrange("p (c t) -> p c t", c=DC)
            self.tokmod(gsv, gsb, l, js, seg)
            self.tokmod(shv, shb, l, jh, seg)
            n = 0
            for k, (t0, t1) in enumerate(TCH):
                W = t1 - t0
                xr = [self.xb[c][k] for c in range(DC)]
                sqv = sq[:, 0:DC * W].rearrange("p (c t) -> p c t", c=DC)
                self.act(sqv, self.xv[:, :, t0:t1], AF.Square, xr, [sqb])
                pi = 5
                for c in range(DC):
                    self.mm(self.ps[pi][:, 0:W], self.ones_t[:], sqv[:, c, :], c == 0, c == DC - 1,
                            [sqb, self.cb], [self.psb[pi]])
                self.rsqrt(rs[:, 0:W], self.ps[pi][:, 0:W], 1.0, D * EPS, [self.psb[pi]], [rsb], ln[:, 0:W], lnb)
                if k < 2:
                    for c in range(DC):
                        i = n % 2
                        n += 1
                        self.stt(tmp[i][:, 0:W], self.xv[:, c, t0:t1], self.modv[:, l, js, c, 0:1], rs[:, 0:W],
                                 ALU.mult, ALU.mult, [self.xb[c][k], self.modb[l], rsb], [tmpb[i]])
                        self.act(self.hv[:, c, t0:t1], tmp[i][:, 0:W], AF.Identity, [tmpb[i], self.modb[l]], [self.hb[k]],
                                 bias=self.modv[:, l, jh, c, 0:1])
                else:
                    rbc = rs[:, 0:W].unsqueeze(1).to_broadcast([128, DC, W])
                    self.tt(bigv, self.xv[:, :, t0:t1], rbc, ALU.mult, xr + [rsb], [bigb])
                    self.tt(bigv, bigv, gsv, ALU.mult, [bigb, gsb], [bigb])
                    self.tt(self.hv[:, :, t0:t1], bigv, shv, ALU.add, [bigb, shb], [self.hb[k]])
            self.fence([sqb, rsb, lnb, gsb, shb, bigb] + tmpb)

    def resid_add(self, pso, psb, l, jg, nch, k, seg, gtok, gtokb, stmp, stmpb):
        t0, t1 = TCH[k]
        if k < 2:
            self.stt(self.xv[:, nch, t0:t1], pso, self.modv[:, l, jg, nch, 0:1], self.xv[:, nch, t0:t1],
                     ALU.mult, ALU.add, [psb, self.modb[l], self.xb[nch][k]], [self.xb[nch][k]])
        else:
            self.tt(stmp, pso, gtok[:, nch, :], ALU.mult, [psb, gtokb], [stmpb])
            self.tt(self.xv[:, nch, t0:t1], self.xv[:, nch, t0:t1], stmp, ALU.add, [stmpb, self.xb[nch][k]], [self.xb[nch][k]])

    def proj_resid(self, w, wb, nkc, rhs_v, rhs_b, l, jg, seg, gtok, gtokb, stmp, stmpb):
        for nch in range(DC):
            for k, (t0, t1) in enumerate(TCH):
                pi = self.pn % 6
                self.pn += 1
                pso = self.ps[pi][:, 0:t1 - t0]
                for kc in range(nkc):
                    self.mm(pso, w[:, kc, nch * 128:(nch + 1) * 128], rhs_v[:, kc, t0:t1], kc == 0, kc == nkc - 1,
                            [wb] + rhs_b(kc, k), [self.psb[pi]])
                self.resid_add(pso, self.psb[pi], l, jg, nch, k, seg, gtok, gtokb, stmp, stmpb)

    def mlp(self, l, seg):
        S = self.S
        with ExitStack() as ph:
            a_t = self.sb(ph, "a_t", [128, 8 * T], BF16)
            av = a_t[:].rearrange("p (c t) -> p c t", c=8)
            ab = [[S.buf("a%d_%d" % (c, k)) for k in range(3)] for c in range(8)]
            rl = [self.sb(ph, "rl%d" % i, [128, 512], BF16) for i in range(2)]
            rlb = [S.buf() for i in range(2)]
            gtok = self.sb(ph, "g2tok", [128, DC * TS], F32)
            gtokv = gtok[:].rearrange("p (c t) -> p c t", c=DC)
            gtokb = S.buf()
            stmp = self.sb(ph, "stmp", [128, TS], F32)
            stmpb = S.buf()
            self.tokmod(gtokv, gtokb, l, 5, seg)
            li = self.layers.index(l)
            adag = None
            nbk = 6
            adab = []
            if seg == 0 and li + 1 < len(self.layers):
                wt = [self.sb(ph, "adaw%d" % i, [128, 4096], BF16) for i in range(2)]
                adab = [S.buf() for i in range(2)]
                adag = self.ada_gen(self.layers[li + 1], wt, adab, [5])
                nbk = 5

            def ada_step(n):
                nonlocal adag
                for _ in range(n):
                    if adag is None:
                        return
                    try:
                        next(adag)
                    except StopIteration:
                        adag = None
            nstep = 0
            wu = self.d_wup[l].rearrange("(kc p) n -> p kc n", p=128)
            wd = self.d_wdn[l].rearrange("(kc p) n -> p kc n", p=128)
            nr = 0
            for b in range(DFF // FB):
                for half in range(2):
                    c0 = b * FB + half * 512
                    w, wb = self.wget([(wu[:, :, c0:c0 + 512], 0, 512)], 16, 512)
                    for q in range(4):
                        fc = half * 4 + q
                        for k, (t0, t1) in enumerate(TCH):
                            W = t1 - t0
                            pi = self.pn % nbk
                            self.pn += 1
                            pso = self.ps[pi][:, 0:W]
                            for kc in range(DC):
                                self.mm(pso, w[:, kc, q * 128:(q + 1) * 128], self.hv[:, kc, t0:t1], kc == 0, kc == DC - 1,
                                        [wb, self.hb[k]], [self.psb[pi]])
                            i = nr % 2
                            nr += 1
                            self.act(rl[i][:, 0:W], pso, AF.Relu, [self.psb[pi]], [rlb[i]])
                            self.tt(av[:, fc, t0:t1], rl[i][:, 0:W], rl[i][:, 0:W], ALU.mult, [rlb[i]], [ab[fc][k]])
                    nstep += 1
                    ada_step(2 if nstep % 2 == 0 else 1)
                for half in range(2):
                    w, wb = self.wget([(wd[:, b * 8:(b + 1) * 8, half * 1024:(half + 1) * 1024], 0, 1024)], 8, 1024)
                    for q in range(8):
                        nch = half * 8 + q
                        for k, (t0, t1) in enumerate(TCH):
                            pi = self.pn % nbk
                            self.pn += 1
                            pso = self.ps[pi][:, 0:t1 - t0]
                            for kc in range(8):
                                self.mm(pso, w[:, kc, q * 128:(q + 1) * 128], av[:, kc, t0:t1], kc == 0, kc == 7,
                                        [wb, ab[kc][k]], [self.psb[pi]])
                            self.resid_add(pso, self.psb[pi], l, 5, nch, k, seg, gtokv, gtokb, stmp[:], stmpb)
                    nstep += 1
                    ada_step(2 if nstep % 2 == 0 else 1)
            ada_step(1000)
            self.fence([b_ for row in ab for b_ in row] + rlb + [gtokb, stmpb] + adab)

    def qk_norm_rope(self, P, nh, src, srcb, gain_bc, tt_, sc, scb, out_bf, outb, out_f32=None, out_f32b=None, dup=False):
        S = self.S
        sqf, ss, ssl, rs_, qn = sc["sqf"], sc["ss"], sc["ssl"], sc["rs"], sc["qn"]
        srcv = src.rearrange("p (h d) -> p h d", h=nh)
        sqv = sqf[0:P, 0:nh * 64].rearrange("p (h d) -> p h d", h=nh)
        qnv = qn[0:P, 0:nh * 64].rearrange("p (h d) -> p h d", h=nh)
        self.act(sqv, srcv, AF.Square, [srcb], [scb])
        self.red(ss[0:P, 0:nh], sqv, [scb], [scb])
        self.rsqrt(rs_[0:P, 0:nh], ss[0:P, 0:nh], 1.0 / HD, EPS, [scb], [scb], ssl[0:P, 0:nh], scb)
        self.tt(qnv, srcv, rs_[0:P, 0:nh].unsqueeze(2).to_broadcast([P, nh, HD]), ALU.mult, [srcb, scb], [scb])
        self.tt(qnv, qnv, gain_bc[0:P].unsqueeze(1).to_broadcast([P, nh, HD]), ALU.mult, [scb, self.attb], [scb])
        cos = self.rope_t[0:P, tt_ * 16:tt_ * 16 + 8].unsqueeze(1).to_broadcast([P, nh, 8])
        sin = self.rope_t[0:P, tt_ * 16 + 8:tt_ * 16 + 16].unsqueeze(1).to_broadcast([P, nh, 8])
        t1 = sc["t1"][0:P, 0:nh * 8].rearrange("p (h d) -> p h d", h=nh)
        t2 = sc["t2"][0:P, 0:nh * 8].rearrange("p (h d) -> p h d", h=nh)
        t3 = sc["t3"][0:P, 0:nh * 8].rearrange("p (h d) -> p h d", h=nh)
        t4 = sc["t4"][0:P, 0:nh * 8].rearrange("p (h d) -> p h d", h=nh)
        x1 = qnv[:, :, 0:8]
        x2 = qnv[:, :, 8:16]
        rb = [scb, self.ropeb]
        self.tt(t1, x1, cos, ALU.mult, rb, [scb])
        self.tt(t2, x2, sin, ALU.mult, rb, [scb])
        self.tt(t3, x2, cos, ALU.mult, rb, [scb])
        self.tt(t4, x1, sin, ALU.mult, rb, [scb])
        self.tt(x1, t1, t2, ALU.subtract, [scb], [scb])
        self.tt(x2, t3, t4, ALU.add, [scb], [scb])
        if dup:
            self.cp(out_bf, qn[0:P, 0:HD].unsqueeze(1).to_broadcast([P, 2, HD]), [scb], [outb], eng="act")
        else:
            self.cp(out_bf, qnv, [scb], [outb], eng="act")
        if out_f32 is not None:
            self.cp(out_f32, qn[0:P, 0:nh * 64], [scb], [out_f32b])

    def ptile(self):
        i = self.ptn % 2
        self.ptn += 1
        return self.pt[i], self.ptb[i]

    def attn_layer(self, l, seg):
        S = self.S
        j = l // 2
        last_seg = (seg == self.nseg - 1)
        with ExitStack() as ph:
            sb = lambda name, shape, dt: self.sb(ph, name, shape, dt)
            bufs = []

            def nb(name=""):
                b = S.buf(name)
                bufs.append(b)
                return b
            self.attb = nb("attc")
            qg_t = sb("qg_t", [128, HD], F32)
            kg_t = sb("kg_t", [128, HD], F32)
            es_t = sb("es_t", [128, NQH], F32)
            self.dma("sp", qg_t[:], self.d_qkg[j, 0].partition_broadcast(128), (), [self.attb])
            self.dma("sp", kg_t[:], self.d_qkg[j, 1].partition_broadcast(128), (), [self.attb])
            self.dma("sp", es_t[:], self.d_sinks[j].partition_broadcast(128), (), [self.attb])
            self.act(es_t[:], es_t[:], AF.Exp, [self.attb], [self.attb])
            ck_t = sb("ck_t", [128, NSQ * HD], F32)
            cv_t = sb("cv_t", [128, NSQ * HD], F32)
            ckb = nb("ck")
            for b in range(NSQ):
                sq_ = seg * NSQ + b
                self.dma("sp", self.o_kws[j, sq_, 0:120, :], self.d_ck[j, seg][8:128, b * 256:(b + 1) * 256], (), ())
                self.dma("sp", self.o_vws[j, sq_, 0:120, :], self.d_cv[j, seg][8:128, b * 256:(b + 1) * 256], (), ())
            gtok = sb("g1tok", [128, DC * TS], F32)
            gtokv = gtok[:].rearrange("p (c t) -> p c t", c=DC)
            gtokb = nb()
            stmp = sb("stmp", [128, TS], F32)
            stmpb = nb()
            self.tokmod(gtokv, gtokb, l, 2, seg)
            qT = sb("qT", [128, 4 * T], BF16)
            qTv = qT[:].rearrange("p (c t) -> p c t", c=4)
            qTb = [nb("qT%d" % i) for i in range(9)]
            oT = sb("oT", [128, 4 * T], BF16)
            oTv = oT[:].rearrange("p (c t) -> p c t", c=4)
            oTb = [nb("oT%d" % k) for k in range(3)]
            kT = sb("kT", [128, 9 * 128 + TS], BF16)
            kTb = [nb("kT%d" % i) for i in range(10)]
            va = sb("va", [128, 10 * 65], BF16)
            vav = va[:].rearrange("p (b d) -> p b d", d=65)
            vab = [nb("va%d" % i) for i in range(10)]
            ckT = sb("ckT", [128, NSQ * 128], BF16)
            ckTb = nb("ckT")
            cva = sb("cva", [128, NSQ * 65], BF16)
            cvav = cva[:].rearrange("p (b d) -> p b d", d=65)
            cvab = nb("cva")
            ckd = sb("ckd", [128, 128], BF16)
            ckdb = nb()
            PT = [sb("PT%d" % i, [128, 512], BF16) for i in range(6)]
            PTb = [nb() for i in range(6)]
            PTs = [sb("PTs%d" % i, [128, 128], BF16) for i in range(5)]
            PTsb = [nb() for i in range(5)]
            sc = {"sqf": sb("sqf", [128, 512], F32), "ss": sb("ss", [128, 8], F32), "ssl": sb("ssl", [128, 8], F32),
                  "rs": sb("rs", [128, 8], F32), "qn": sb("qn", [128, 512], F32),
                  "t1": sb("t1", [128, 64], F32), "t2": sb("t2", [128, 64], F32),
                  "t3": sb("t3", [128, 64], F32), "t4": sb("t4", [128, 64], F32)}
            scb = nb("qsc")
            qr = [sb("qr%d" % i, [128, 512], BF16) for i in range(2)]
            qrb = [nb() for i in range(2)]
            kf = [sb("kf%d" % i, [128, HD], F32) for i in range(2)]
            kfb = [nb() for i in range(2)]
            vf = [sb("vf%d" % i, [128, HD], F32) for i in range(2)]
            vfb = [nb() for i in range(2)]
            kr = sb("kr", [128, HD], BF16)
            krb = nb()
            krd = sb("krd", [128, 128], BF16)
            krdb = nb()
            den = sb("den", [128, 4], F32)
            denb = nb()
            otm = [sb("otm%d" % i, [128, 512], BF16) for i in range(2)]
            otmb = [nb() for i in range(2)]
            self.dma("sp", self.rope_t[:], self.d_rope[seg], (), [self.ropeb])
            self.dma("sp", self.hflag_t[:], self.d_hflag[seg], (), [self.mp0b])
            self.ts(self.mp0_t[:], self.masks_t[:, 512:1024], self.hflag_t[:, 0:1], None, ALU.mult, None,
                    [self.cb, self.mp0b], [self.mp0b])
            mp0 = self.mp0_t[:].rearrange("p (j t) -> p j t", j=4)
            wq_all = self.d_wqkv[j].rearrange("(kc p) n -> p kc n", p=128)
            wo_all = self.d_wo_a[j].rearrange("(kc p) n -> p kc n", p=128)
            nq = 0
            nkf = 0
            self.mset(vav[:, :, 64:65], 1.0, vab)
            self.mset(cvav[:, :, 64:65], 1.0, [cvab])
            for g in range(NKV):
                self.cp(kT[:, 0:128], self.halo_k[:, (j * NKV + g) * 128:(j * NKV + g + 1) * 128], [self.halob[j][g]], [kTb[0]])
                self.cp(vav[:, 0, 0:64], self.halo_v[:, (j * NKV + g) * 65:(j * NKV + g) * 65 + 64], [self.halob[j][g]], [vab[0]])
                w, wb = self.wget([(wq_all[:, :, g * 512:(g + 1) * 512], 0, 512)], 16, 512)
                dq = Defer(1)
                for tt_ in range(9):
                    P = 128 if tt_ < 8 else TS
                    t0 = tt_ * 128
                    k = min(tt_ // 4, 2)
                    pi = self.pn % 2
                    self.pn += 1
                    pso = self.ps[pi][0:P, :]
                    for kc in range(DC):
                        self.mm(pso, self.hv[:, kc, t0:t0 + P], w[:, kc, :], kc == 0, kc == DC - 1,
                                [wb, self.hb[k]], [self.psb[pi]])
                    i = nq % 2
                    nq += 1

                    def q_post(P=P, t0=t0, pi=pi, i=i, tt_=tt_, pso=pso):
                        self.qk_norm_rope(P, 8, pso, self.psb[pi], qg_t, tt_, sc, scb,
                                          qr[i][0:P, :].rearrange("p (h d) -> p h d", h=8), qrb[i])
                        ptile, ptb = self.ptile()
                        for c in range(4):
                            self.tr(ptile[:, c * 128:c * 128 + P], qr[i][0:P, c * 128:(c + 1) * 128], self.ident_t[0:P, 0:P],
                                    [qrb[i], self.cb], [ptb])
                        self.cp(qTv[:, :, t0:t0 + P], ptile[:, 0:512].rearrange("p (c t) -> p c t", c=4)[:, :, 0:P],
                                [ptb], [qTb[tt_]], eng="act")
                    dq.push(q_post)
                wkv = self.d_wqkv[j].rearrange("(kc p) n -> p kc n", p=128)
                w, wb = self.wget([(wkv[:, :, 2048 + g * 64:2048 + (g + 1) * 64], 0, 64),
                                   (wkv[:, :, 2304 + g * 64:2304 + (g + 1) * 64], 64, 64)], 16, 128)
                for tt_ in range(9):
                    P = 128 if tt_ < 8 else TS
                    t0 = tt_ * 128
                    k = min(tt_ // 4, 2)
                    pi = self.pn % 2
                    self.pn += 1
                    pso = self.ps[pi][0:P, 0:128]
                    for kc in range(DC):
                        self.mm(pso, self.hv[:, kc, t0:t0 + P], w[:, kc, :], kc == 0, kc == DC - 1,
                                [wb, self.hb[k]], [self.psb[pi]])
                    i = nkf % 2
                    nkf += 1

                    def kv_post(P=P, t0=t0, pi=pi, i=i, tt_=tt_, g=g):
                        wantf = tt_ >= 7
                        self.qk_norm_rope(P, 1, self.ps[pi][0:P, 0:64], self.psb[pi], kg_t, tt_, sc, scb,
                                          krd[0:P, :].rearrange("p (r d) -> p r d", r=2), krdb,
                                          kf[i][0:P, :] if wantf else None, kfb[i], dup=True)
                        if wantf:
                            self.cp(vf[i][0:P, :], self.ps[pi][0:P, 64:128], [self.psb[pi]], [vfb[i]], eng="act")
                        blk = tt_ + 1
                        self.cp(vav[0:P, blk, 0:64], self.ps[pi][0:P, 64:128], [self.psb[pi]], [vab[blk]])
                        ptile, ptb = self.ptile()
                        self.tr(ptile[:, 0:P], krd[0:P, :], self.ident_t[0:P, 0:P], [krdb, self.cb], [ptb])
                        self.cp(kT[:, blk * 128:blk * 128 + P], ptile[:, 0:P], [ptb], [kTb[blk]])
                        if tt_ == 7:
                            self.dma("sp", self.o_kwp[j, seg][:, g * 64:(g + 1) * 64], kf[i][:, :], [kfb[i]], ())
                            self.dma("sp", self.o_vwp[j, seg][:, g * 64:(g + 1) * 64], vf[i][:, :], [vfb[i]], ())
                            self.cp(self.halo_k[:, (j * NKV + g) * 128:(j * NKV + g + 1) * 128], kT[:, 8 * 128:9 * 128],
                                    [kTb[8]], [self.halob[j][g]])
                            self.cp(self.halo_v[:, (j * NKV + g) * 65:(j * NKV + g) * 65 + 64], vav[:, 8, 0:64],
                                    [vab[8]], [self.halob[j][g]])
                        if tt_ == 8:
                            for b in range(NSQ):
                                sq_ = seg * NSQ + b
                                self.dma("sp", self.o_kws[j, sq_, 120:128, g * 64:(g + 1) * 64], kf[i][b * 8:(b + 1) * 8, :], [kfb[i]], ())
                                self.dma("sp", self.o_vws[j, sq_, 120:128, g * 64:(g + 1) * 64], vf[i][b * 8:(b + 1) * 8, :], [vfb[i]], ())
                    dq.push(kv_post)
                dq.flush()
                ckv = ck_t[:].rearrange("p (b f) -> p b f", b=NSQ)
                cvv = cv_t[:].rearrange("p (b f) -> p b f", b=NSQ)
                self.dma("sp", ckv, self.d_ck[j, seg].rearrange("p (b f) -> p b f", b=NSQ)[:, :, g * 64:(g + 1) * 64], (), [ckb])
                self.dma("sp", cvv, self.d_cv[j, seg].rearrange("p (b f) -> p b f", b=NSQ)[:, :, g * 64:(g + 1) * 64], (), [ckb])
                for b in range(NSQ):
                    self.cp(ckd[:].rearrange("p (r d) -> p r d", r=2),
                            ckv[:, b, :].unsqueeze(1).to_broadcast([128, 2, HD]), [ckb], [ckdb])
                    ptile, ptb = self.ptile()
                    self.tr(ptile[:, 0:128], ckd[:], self.ident_t[:], [ckdb, self.cb], [ptb])
                    self.cp(ckT[:, b * 128:(b + 1) * 128], ptile[:, 0:128], [ptb], [ckTb], eng="act")
                self.cp(cvav[:, :, 0:64], cvv, [ckb], [cvab])
                NPT = len(PT)
                d1, d2, d3 = Defer(1), Defer(1), Defer(1)
                un = 0
                for i in range(8):
                    for par in range(2):
                        pp = slice(par * 64, (par + 1) * 64)
                        rhs = qTv[pp, :, i * 128:(i + 1) * 128]
                        bp = (un % 2) * 2
                        pts = [(2 * un) % NPT, (2 * un + 1) % NPT]
                        un += 1
                        for kbi, blk in enumerate((i, i + 1)):
                            self.mm(self.ps[bp + kbi][:, :], kT[pp, blk * 128:(blk + 1) * 128], rhs, True, True,
                                    [kTb[blk], qTb[i]], [self.psb[bp + kbi]])

                        def st1(i=i, par=par, bp=bp, pts=pts, g=g):
                            for kbi in range(2):
                                pb_ = pts[kbi]
                                self.act(PT[pb_][:], self.ps[bp + kbi][:, :], AF.Exp, [self.psb[bp + kbi]], [PTb[pb_]],
                                         scale=HD ** -0.5)
                                m = (mp0 if i == 0 else self.m_prev()) if kbi == 0 else self.m_cur()
                                mb = [self.mp0b] if (kbi == 0 and i == 0) else [self.cb]
                                self.tt(PT[pb_][:].rearrange("p (j t) -> p j t", j=4), PT[pb_][:].rearrange("p (j t) -> p j t", j=4),
                                        m, ALU.mult, [PTb[pb_]] + mb, [PTb[pb_]])

                            def st2():
                                po = 4 + par
                                for jj in range(4):
                                    for kbi, blk in enumerate((i, i + 1)):
                                        self.mm(self.ps[po][:, jj * 65:(jj + 1) * 65], PT[pts[kbi]][:, jj * 128:(jj + 1) * 128],
                                                vav[:, blk, :], kbi == 0, kbi == 1, [PTb[pts[kbi]], vab[blk]], [self.psb[po]])
                                pov = self.ps[po][:, 0:260].rearrange("p (j d) -> p j d", d=65)
                                self.tt(den[:, :], pov[:, :, 64], es_t[:, g * 8 + par:g * 8 + 8:2], ALU.add,
                                        [self.psb[po], self.attb], [denb])
                                self.recip(den[:, :], den[:, :], [denb], [denb])
                                oi = i % 2
                                ov = otm[oi][:].rearrange("p (j r d) -> p j r d", j=4, r=2)[:, :, par, :]
                                self.tt(ov, pov[:, :, 0:64], den[:, :].unsqueeze(2).to_broadcast([128, 4, HD]), ALU.mult,
                                        [self.psb[po], denb], [otmb[oi]])
                                if par == 1:
                                    def st3():
                                        ptile, ptb = self.ptile()
                                        for c in range(4):
                                            self.tr(ptile[:, c * 128:(c + 1) * 128], otm[oi][:, c * 128:(c + 1) * 128], self.ident_t[:],
                                                    [otmb[oi], self.cb], [ptb])
                                        self.cp(oTv[:, :, i * 128:(i + 1) * 128], ptile[:, 0:512].rearrange("p (c t) -> p c t", c=4),
                                                [ptb], [oTb[i // 4]], eng="act")
                                    d3.push(st3)
                            d2.push(st2)
                        d1.push(st1)
                d1.flush()
                d2.flush()
                d3.flush()
                for par in range(2):
                    pp = slice(par * 64, (par + 1) * 64)
                    rhs = qTv[pp, :, TP:T]
                    for b in range(NSQ + 1):
                        pi = 2 + (b % 2)
                        if b < NSQ:
                            Pk = 128
                            self.mm(self.ps[pi][:, 0:128], ckT[pp, b * 128:(b + 1) * 128], rhs, True, True,
                                    [ckTb, qTb[8]], [self.psb[pi]])
                            m = self.m_sprev(b)
                        else:
                            Pk = TS
                            self.mm(self.ps[pi][0:Pk, 0:128], kT[pp, 9 * 128:9 * 128 + TS], rhs, True, True,
                                    [kTb[9], qTb[8]], [self.psb[pi]])
                            m = self.m_scur()
                        self.act(PTs[b][0:Pk, :], self.ps[pi][0:Pk, 0:128], AF.Exp, [self.psb[pi]], [PTsb[b]], scale=HD ** -0.5)
                        self.tt(PTs[b][0:Pk, :].rearrange("p (j t) -> p j t", j=4),
                                PTs[b][0:Pk, :].rearrange("p (j t) -> p j t", j=4), m, ALU.mult,
                                [PTsb[b], self.cb], [PTsb[b]])
                    po = 4 + par
                    for jj in range(4):
                        for b in range(NSQ + 1):
                            if b < NSQ:
                                self.mm(self.ps[po][0:TS, jj * 65:(jj + 1) * 65], PTs[b][:, jj * 32:(jj + 1) * 32],
                                        cvav[:, b, :], b == 0, False, [PTsb[b], cvab], [self.psb[po]])
                            else:
                                self.mm(self.ps[po][0:TS, jj * 65:(jj + 1) * 65], PTs[b][0:TS, jj * 32:(jj + 1) * 32],
                                        vav[0:TS, 9, :], False, True, [PTsb[b], vab[9]], [self.psb[po]])
                    pov = self.ps[po][0:TS, 0:260].rearrange("p (j d) -> p j d", d=65)
                    self.tt(den[0:TS, :], pov[:, :, 64], es_t[0:TS, g * 8 + par:g * 8 + 8:2], ALU.add,
                            [self.psb[po], self.attb], [denb])
                    self.recip(den[0:TS, :], den[0:TS, :], [denb], [denb])
                    ov = otm[0][0:TS, :].rearrange("p (j r d) -> p j r d", j=4, r=2)[:, :, par, :]
                    self.tt(ov, pov[:, :, 0:64], den[0:TS, :].unsqueeze(2).to_broadcast([TS, 4, HD]), ALU.mult,
                            [self.psb[po], denb], [otmb[0]])
                ptile, ptb = self.ptile()
                for c in range(4):
                    self.tr(ptile[:, c * 128:c * 128 + TS], otm[0][0:TS, c * 128:(c + 1) * 128], self.ident_t[0:TS, 0:TS],
                            [otmb[0], self.cb], [ptb])
                self.cp(oTv[:, :, TP:T], ptile[:, 0:512].rearrange("p (c t) -> p c t", c=4)[:, :, 0:TS],
                        [ptb], [oTb[2]], eng="act")
                w, wb = self.wget([(wo_all[:, g * 4:(g + 1) * 4, :], 0, 2048)], 4, 2048)
                self.proj_resid(w, wb, 4, oTv, lambda kc, k: [oTb[k]], l, 2, seg, gtokv, gtokb, stmp[:], stmpb)
            self.fence(bufs)

    def hgrn_layer(self, l, seg):
        S = self.S
        j = l // 2
        tiles = [(i * HT, min(HT, TP - i * HT)) for i in range((TP + HT - 1) // HT)]
        NT = len(tiles)
        NCK = TP // CH
        with ExitStack() as ph:
            sb = lambda name, shape, dt: self.sb(ph, name, shape, dt)
            bufs = []

            def nb(name=""):
                b = S.buf(name)
                bufs.append(b)
                return b
            gtok = sb("g1tok", [128, DC * TS], F32)
            gtokv = gtok[:].rearrange("p (c t) -> p c t", c=DC)
            gtokb = nb()
            stmp = sb("stmp", [128, TS], F32)
            stmpb = nb()
            self.tokmod(gtokv, gtokb, l, 2, seg)
            qs = sb("qs", [128, T], BF16)
            A = sb("Aa", [128, T], F32)
            B = sb("Bb", [128, T], F32)
            C = sb("Cc", [128, T], F32)
            qsb, Ab_, Bb_, Cb_ = nb("qs"), nb("A"), nb("B"), nb("C")
            qt = sb("qt", [128, T], BF16)
            kt = sb("kt", [128, T], BF16)
            kh = sb("kh", [128, T], BF16)
            qtb, ktb, khb = nb("qt"), nb("kt"), nb("kh")
            qts = sb("qts", [128, NSQ * TS], BF16)
            khs = sb("khs", [128, NSQ * TS], BF16)
            qtsb, khsb = nb(), nb()
            Dch = sb("Dch", [128, NCK + NSQ], F32)
            Dchb = nb("Dch")
            vt = sb("vt", [128, (NT + 1) * 128], BF16)
            vtv = vt[:].rearrange("p (i d) -> p i d", d=128)
            vtb = nb("vt")
            sg = sb("sg", [128, (NT + 1) * 128], BF16)
            sgv = sg[:].rearrange("p (i d) -> p i d", d=128)
            sgb = nb("sg")
            kht = sb("kht", [128, NT * 128], BF16)
            khtv = kht[:].rearrange("p (i d) -> p i d", d=128)
            khtb = nb("kht")
            khts = sb("khts", [128, NSQ * 128], BF16)
            khtsv = khts[:].rearrange("p (i d) -> p i d", d=128)
            khtsb = nb("khts")
            NSB = 8
            Sb = sb("Sb", [128, NSB * 128], BF16)
            Sbv = Sb[:].rearrange("p (i d) -> p i d", d=128)
            Sbb = [nb("Sb%d" % i) for i in range(NSB)]
            St2 = [sb("St%d" % i, [128, 128], F32) for i in range(2)]
            St2b = [nb("St%d" % i) for i in range(2)]
            S0 = sb("S0", [128, NSQ * 128], F32)
            S0v = S0[:].rearrange("p (i d) -> p i d", d=128)
            S0b = [nb("S0_%d" % i) for i in range(NSQ)]
            S0h = sb("S0h", [128, NSQ * 128], BF16)
            S0hv = S0h[:].rearrange("p (i d) -> p i d", d=128)
            S0hb = nb("S0h")
            Am = [sb("Am%d" % i, [128, HT], BF16) for i in range(2)]
            Amb = [nb() for i in range(2)]
            og = [sb("og%d" % i, [128, 128], BF16) for i in range(2)]
            ogb = [nb() for i in range(2)]
            ss = sb("hss_", [128, 4], F32)
            ssb = nb()
            junk = sb("junk", [128, 128], F32)
            junkb = nb()
            oT = sb("oTr", [128, 4 * T], BF16)
            oTv = oT[:].rearrange("p (c t) -> p c t", c=4)
            oTb = [nb("oT%d" % k) for k in range(3)]
            ub = [nb("u%d" % i) for i in range(4)]
            ob4 = [nb("o%d" % i) for i in range(4)]
            bd = self.hmask_t[:, 0:128]
            reset = self.hmask_t[:, 128:128 + T]
            seqm = self.hmask_t[:, 128 + T:128 + T + 128].rearrange("p (b t) -> p b t", b=NSQ)
            bds = self.hmask_t[0:TS, 128 + T + 128:128 + T + 128 + TS]
            win = self.d_win[j].rearrange("(kc p) (g n) -> p kc g n", p=128, g=4)
            wo_all = self.d_wo_r[j].rearrange("(kc p) n -> p kc n", p=128)
            hspb = self.hspb
            nA = 0
            nO = 0
            nU = 0
            nog = 0
            dO = Defer(1)
            dT = Defer(1)
            def fm_gen(w, wb, banks):
                n_ = 0
                for which, dst, dstb, func in ((0, qs, qsb, AF.Silu), (1, A, Ab_, AF.Sigmoid)):
                    for k, (t0, t1) in enumerate(TCH):
                        pi = banks[n_ % len(banks)]
                        n_ += 1
                        pso = self.ps[pi][:, 0:t1 - t0]
                        for kc in range(DC):
                            self.mm(pso, w[:, kc, which * 128:(which + 1) * 128], self.hv[:, kc, t0:t1], kc == 0, kc == DC - 1,
                                    [wb, self.hb[k]], [self.psb[pi]])
                        self.act(dst[:, t0:t1], pso, func, [self.psb[pi]], [dstb])
                        yield

            def get_w(hd_):
                return self.wget([(win[:, :, gi, hd_ * 128:(hd_ + 1) * 128], gi * 128, 128) for gi in range(4)], 16, 512)
            for hd in range(self.nheads):
                w, wb = get_w(hd)
                for _ in fm_gen(w, wb, [0, 1, 2, 3, 4, 5]):
                    pass
                def gates():
                    if j == 1:
                        self.ts(A[:], A[:], self.lb_t[:, 16 + hd:17 + hd], self.lb_t[:, hd:hd + 1], ALU.mult, ALU.add,
                                [Ab_, self.cb], [Ab_])
                        yield
                    self.act(B[:], A[:], AF.Ln, [Ab_], [Bb_])
                    yield
                    self.scan(C[:], reset, B[:], 0.0, [Bb_, self.cb], [Cb_])
                    yield
                    self.ts(A[:], A[:], -1.0, 1.0, ALU.mult, ALU.add, [Ab_], [Ab_])
                    yield
                    self.act(B[:], C[:], AF.Exp, [Cb_], [Bb_])
                    yield
                    self.tt(qt[:], qs[:], B[:], ALU.mult, [qsb, Bb_], [qtb])
                    yield
                    self.ts(B[:], C[:], -1.0, CLAMP, ALU.mult, ALU.min, [Cb_], [Bb_])
                    yield
                    self.act(B[:], B[:], AF.Exp, [Bb_], [Bb_])
                    yield
                    self.tt(kt[:], A[:], B[:], ALU.mult, [Ab_, Bb_], [ktb])
                    yield
                    Cp = C[:, 0:TP].rearrange("p (c i) -> p c i", i=CH)
                    Bp = B[:, 0:TP].rearrange("p (c i) -> p c i", i=CH)
                    self.tt(Bp, C[:, CH - 1:TP:CH].unsqueeze(2).to_broadcast([128, NCK, CH]), Cp, ALU.subtract, [Cb_], [Bb_])
                    yield
                    Cs = C[:, TP:T].rearrange("p (c i) -> p c i", i=DEC_SEQ)
                    Bs = B[:, TP:T].rearrange("p (c i) -> p c i", i=DEC_SEQ)
                    self.tt(Bs, C[:, TP + DEC_SEQ - 1:T:DEC_SEQ].unsqueeze(2).to_broadcast([128, NSQ, DEC_SEQ]), Cs, ALU.subtract, [Cb_], [Bb_])
                    yield
                    self.act(B[:], B[:], AF.Exp, [Bb_], [Bb_])
                    yield
                    self.tt(kh[:], A[:], B[:], ALU.mult, [Ab_, Bb_], [khb])
                    yield
                    self.act(Dch[:, 0:NCK], C[:, CH - 1:TP:CH], AF.Exp, [Cb_], [Dchb])
                    yield
                    self.act(Dch[:, NCK:NCK + NSQ], C[:, TP + DEC_SEQ - 1:T:DEC_SEQ], AF.Exp, [Cb_], [Dchb])
                    yield
                    self.tt(qts[:].rearrange("p (b t) -> p b t", b=NSQ), qt[:, TP:T].unsqueeze(1).to_broadcast([128, NSQ, TS]), seqm,
                            ALU.mult, [qtb, self.cb], [qtsb])
                    yield
                    self.tt(khs[:].rearrange("p (b t) -> p b t", b=NSQ), kh[:, TP:T].unsqueeze(1).to_broadcast([128, NSQ, TS]), seqm,
                            ALU.mult, [khb, self.cb], [khsb])
                    yield
                def tmtiles():
                    for which, dstv, dstb in ((2, vtv, vtb), (3, sgv, sgb)):
                        for ti in range(NT + 1):
                            t0, P = tiles[ti] if ti < NT else (TP, TS)
                            k = 2 if ti == NT else (0 if t0 + P <= 512 else 1)
                            hbs = [self.hb[k]] if (ti == NT or t0 >= 512 or t0 + P <= 512) else [self.hb[0], self.hb[1]]
                            pi = self.pn % 6
                            self.pn += 1
                            pso = self.ps[pi][0:P, 0:128]
                            for kc in range(DC):
                                self.mm(pso, self.hv[:, kc, t0:t0 + P], w[:, kc, which * 128:(which + 1) * 128], kc == 0, kc == DC - 1,
                                        [wb] + hbs, [self.psb[pi]])
                            if which == 2:
                                self.cp(dstv[0:P, ti, :], pso, [self.psb[pi]], [dstb])
                            else:
                                self.act(dstv[0:P, ti, :], pso, AF.Silu, [self.psb[pi]], [dstb])
                            yield
                ga, tmg = gates(), tmtiles()
                live = [ga, ga, tmg]
                while live:
                    for gen in list(live):
                        if gen not in live:
                            continue
                        try:
                            next(gen)
                        except StopIteration:
                            while gen in live:
                                live.remove(gen)
                for ti in range(NT):
                    t0, P = tiles[ti]
                    ptile, ptb = self.ptile()
                    self.tr(ptile[0:P, 0:128], kh[:, t0:t0 + P], self.ident_t[:], [khb, self.cb], [ptb])
                    self.cp(khtv[0:P, ti, :], ptile[0:P, 0:128], [ptb], [khtb], eng="act")
                ptile, ptb = self.ptile()
                for b in range(NSQ):
                    self.tr(ptile[0:TS, b * 128:(b + 1) * 128], khs[:, b * TS:(b + 1) * TS], self.ident_t[:], [khsb, self.cb], [ptb])
                self.cp(khts[0:TS, :], ptile[0:TS, 0:NSQ * 128], [ptb], [khtsb], eng="act")
                fmg = None

                def fm_step():
                    nonlocal fmg
                    if fmg is not None:
                        try:
                            next(fmg)
                        except StopIteration:
                            fmg = None
                if seg == 0:
                    self.mset(St2[0][:], 0.0, [St2b[0]])
                else:
                    self.dma("sp", St2[0][:], self.o_hsp[j, seg - 1, hd], [hspb[j][hd]], [St2b[0]])
                self.cp(Sbv[:, 0, :], St2[0][:], [St2b[0]], [Sbb[0]], eng="act")
                for c in range(NCK):
                    ti, a = divmod(c, HT // CH)
                    u = 4 + nU % 2
                    nU += 1
                    uo = self.ps[u][:, 0:128]
                    self.mm(uo, khtv[a * CH:(a + 1) * CH, ti, :], vtv[a * CH:(a + 1) * CH, ti, :], True, True,
                            [khtb, vtb], [self.psb[u]])
                    si, so = c % 2, (c + 1) % 2
                    self.stt(St2[so][:], St2[si][:], Dch[:, c:c + 1], uo, ALU.mult, ALU.add,
                             [St2b[si], Dchb, self.psb[u]], [St2b[so]])
                    if c < NCK - 1:
                        self.cp(Sbv[:, (c + 1) % NSB, :], St2[so][:], [St2b[so]], [Sbb[(c + 1) % NSB]], eng="act")
                    if c % 5 == 1:
                        fm_step()
                    if a == HT // CH - 1 or c == NCK - 1:
                        t0, P = tiles[ti]
                        ai = nA % 2
                        nA += 1

                        def tile_block(t0=t0, P=P, ti=ti, ai=ai, nog=nog, hd=hd):
                            pa = self.ps[ai]
                            self.mm(pa[0:P, 0:P], kt[:, t0:t0 + P], qt[:, t0:t0 + P], True, True, [ktb, qtb], [self.psb[ai]])
                            self.tt(Am[ai][0:P, 0:P], pa[0:P, 0:P], bd[0:P, 0:P], ALU.mult, [self.psb[ai], self.cb], [Amb[ai]])
                            ob_ = 2 + ai
                            po = self.ps[ob_][:, 0:128]
                            nchk = P // CH
                            self.mm(po[0:P, :], Am[ai][0:P, 0:P], vtv[0:P, ti, :], True, False, [Amb[ai], vtb], [self.psb[ob_]])
                            for a2 in range(nchk):
                                c2 = ti * (HT // CH) + a2
                                self.mm(po[a2 * CH:(a2 + 1) * CH, :], qt[:, t0 + a2 * CH:t0 + (a2 + 1) * CH], Sbv[:, c2 % NSB, :],
                                        False, True, [qtb, Sbb[c2 % NSB]], [self.psb[ob_]])

                            def o_post():
                                self.hgrn_out(po[0:P, :], self.psb[ob_], P, sgv[0:P, ti, :], sgb, ss, ssb, junk, junkb, og, ogb, nog,
                                              oTv[:, hd % 4, t0:t0 + P], oTb, t0, P, j, hd)
                            dO.push(o_post)
                        dT.push(tile_block)
                        nog += 1
                self.dma("sp", self.o_hsp[j, seg, hd], St2[NCK % 2][:], [St2b[NCK % 2]], [hspb[j][hd]])
                for b in range(NSQ):
                    self.dma("sp", S0v[:, b, :], self.d_st[j, seg * NSQ + b, hd], (), [S0b[b]])
                self.cp(S0h[:], S0[:], S0b, [S0hb], eng="act")
                dT.flush()
                dO.flush()
                while fmg is not None:
                    fm_step()
                ai = nA % 2
                nA += 1
                pa = self.ps[ai]
                self.mm(pa[0:TS, 0:TS], kt[:, TP:T], qt[:, TP:T], True, True, [ktb, qtb], [self.psb[ai]])
                self.tt(Am[ai][0:TS, 0:TS], pa[0:TS, 0:TS], bds, ALU.mult, [self.psb[ai], self.cb], [Amb[ai]])
                ob_ = 2 + ai
                po = self.ps[ob_][:, 0:128]
                self.mm(po[0:TS, :], Am[ai][0:TS, 0:TS], vtv[0:TS, NT, :], True, False, [Amb[ai], vtb], [self.psb[ob_]])
                for b in range(NSQ):
                    self.mm(po[0:TS, :], qts[:, b * TS:(b + 1) * TS], S0hv[:, b, :], False, b == NSQ - 1, [qtsb, S0hb], [self.psb[ob_]])
                self.hgrn_out(po[0:TS, :], self.psb[ob_], TS, sgv[0:TS, NT, :], sgb, ss, ssb, junk, junkb, og, ogb, nog,
                              oTv[:, hd % 4, TP:T], oTb, TP, TS, j, hd)
                nog += 1
                for b in range(NSQ):
                    u = 4 + nU % 2
                    nU += 1
                    uo = self.ps[u][:, 0:128]
                    self.mm(uo, khtsv[0:TS, b, :], vtv[0:TS, NT, :], True, True, [khtsb, vtb], [self.psb[u]])
                    self.stt(S0v[:, b, :], S0v[:, b, :], Dch[:, NCK + b:NCK + b + 1], uo, ALU.mult, ALU.add,
                             [S0b[b], Dchb, self.psb[u], S0hb], [S0b[b]])
                    self.dma("sp", self.o_hss[j, seg * NSQ + b, hd], S0v[:, b, :], [S0b[b]], ())
                if hd % 4 == 3:
                    g = hd // 4
                    w2, wb2 = self.wget([(wo_all[:, g * 4:(g + 1) * 4, :], 0, 2048)], 4, 2048)
                    self.proj_resid(w2, wb2, 4, oTv, lambda kc, k: [oTb[k]], l, 2, seg, gtokv, gtokb, stmp[:], stmpb)
            self.fence(bufs)

    def hgrn_out(self, po, pob, P, sgt, sgb, ss, ssb, junk, junkb, og, ogb, n, oT_dst, oTb, t0, P2, j, hd):
        i = n % 2
        col = n % 2
        self.act(junk[0:P, :], po, AF.Square, [pob], [junkb], accum_out=ss[0:P, col:col + 1])
        self.rsqrt(ss[0:P, 2 + col:3 + col], ss[0:P, col:col + 1], 1.0 / 128.0, EPS, [junkb], [ssb], junk[0:P, 0:1], junkb)
        self.stt(og[i][0:P, :], po, ss[0:P, 2 + col:3 + col], sgt, ALU.mult, ALU.mult, [pob, ssb, sgb], [ogb[i]])
        ptile, ptb = self.ptile()
        self.tr(ptile[:, 0:P], og[i][0:P, :], self.ident_t[0:P, 0:P], [ogb[i], self.cb], [ptb])
        ks = sorted(set([min(t0 // 512, 2), min((t0 + P - 1) // 512, 2)]))
        self.ts(oT_dst, ptile[:, 0:P], self.og_t[:, j * RH + hd:j * RH + hd + 1], None, ALU.mult, None,
                [ptb, self.cb], [oTb[k] for k in ks])

    def segment(self, seg):
        xT = self.d_xT[seg].rearrange("(c p) t -> p c t", p=128)
        for c in range(DC):
            self.dma("sp", self.xv[:, c, :], xT[:, c, :], (), self.xb[c])
        for l in self.layers:
            self.norm_mod(l, 0, seg)
            if l % 2 == 0:
                self.attn_layer(l, seg)
            else:
                self.hgrn_layer(l, seg)
            if self.do_mlp:
                self.norm_mod(l, 1, seg)
                self.mlp(l, seg)
        yT = self.o_yT[seg].rearrange("(c p) t -> p c t", p=128)
        for c in range(DC):
            self.dma("sp", yT[:, c, :], self.xv[:, c, :], self.xb[c], ())


def _rope_tables():
    half = ROT // 2
    inv = (np.float32(THETA) ** (-np.arange(half, dtype=np.float32) / np.float32(half))).astype(np.float32)
    out = np.zeros((NSEG, 128, 9, 16), np.float32)
    for seg in range(NSEG):
        for tt in range(9):
            if tt < 8:
                pos = (seg * TP + tt * 128 + np.arange(128)).astype(np.float32)
            else:
                pos = (PAST_LEN + (np.arange(128) % DEC_SEQ)).astype(np.float32)
            ang = pos[:, None] * inv[None, :]
            out[seg, :, tt, 0:8] = np.cos(ang)
            out[seg, :, tt, 8:16] = np.sin(ang)
    return out.reshape(NSEG, 128, 9 * 16)


def _masks():
    m = np.zeros((128, 1792), np.float32)
    s = np.arange(128)[:, None]
    t = np.arange(128)[None, :]
    cur = (s <= t).astype(np.float32)
    prev = (s > t).astype(np.float32)
    m[:, 0:512] = np.tile(cur, (1, 4))
    m[:, 512:1024] = np.tile(prev, (1, 4))
    ts_ = np.arange(TS)[None, :]
    for b in range(NSQ):
        mm_ = ((ts_ // DEC_SEQ == b) & (s > (ts_ % DEC_SEQ))).astype(np.float32)
        m[:, 1024 + b * 128:1024 + (b + 1) * 128] = np.tile(mm_, (1, 4))
    s32 = np.arange(TS)[:, None]
    sc = ((s32 // DEC_SEQ == ts_ // DEC_SEQ) & (s32 % DEC_SEQ <= ts_ % DEC_SEQ)).astype(np.float32)
    m[0:TS, 1536:1664] = np.tile(sc, (1, 4))
    m[:, 1664:1792] = np.eye(128, dtype=np.float32)
    hm = np.zeros((128, 128 + T + 128 + TS), np.float32)
    hm[:, 0:128] = ((s // CH == t // CH) & (s <= t)).astype(np.float32)
    reset = np.ones(T, np.float32)
    reset[0:TP:CH] = 0.0
    reset[TP:T:DEC_SEQ] = 0.0
    hm[:, 128:128 + T] = reset[None, :]
    sq = np.zeros((NSQ, TS), np.float32)
    for b in range(NSQ):
        sq[b, b * DEC_SEQ:(b + 1) * DEC_SEQ] = 1.0
    hm[:, 128 + T:128 + T + 128] = sq.reshape(1, -1)
    hm[0:TS, 128 + T + 128:] = ((s32 // DEC_SEQ == ts_ // DEC_SEQ) & (s32 <= ts_)).astype(np.float32)
    return m, hm


def _core_assign(core):
    b = core % BATCH
    return b, [b * 8 + i for i in range(NSEG * NSQ)]


_PROG_CACHE = {}


def _get_prog(nlayers):
    if nlayers not in _PROG_CACHE:
        p = Prog(nlayers=nlayers)
        p.hspb = [[Buf("hsp%d_%d" % (j, h)) for h in range(RH)] for j in range(2)]
        p.build()
        _PROG_CACHE[nlayers] = p
    return _PROG_CACHE[nlayers]


def kernel(x_prompt, x_sample, cache_k_win, cache_v_win, state_hgrn, c_prompt, c_sample,
           norm_gain, w_ada, b_ada, attn_w_qkv, attn_q_gain, attn_k_gain, attn_sinks, attn_w_o,
           rec_w_in, rec_lb_logits, rec_o_gain, rec_w_o, mlp_w_up, mlp_w_down, _nlayers=DEPTH, _trace=False):
    f = lambda a: np.ascontiguousarray(np.asarray(a, dtype=np.float32))
    x_prompt, x_sample = f(x_prompt), f(x_sample)
    cache_k_win, cache_v_win, state_hgrn = f(cache_k_win), f(cache_v_win), f(state_hgrn)
    c_prompt, c_sample = f(c_prompt), f(c_sample)
    masks, hmask = _masks()
    rope = _rope_tables()
    shared = {
        "masks": masks, "hmask": hmask, "rope": rope,
        "gainT": f(np.asarray(norm_gain).reshape(DEPTH, 2, DC, 128).transpose(3, 0, 1, 2).reshape(128, -1)),
        "badaT": f(np.asarray(b_ada).reshape(DEPTH, 96, 128).transpose(2, 0, 1).reshape(128, -1)),
        "qkg": f(np.stack([np.asarray(attn_q_gain), np.asarray(attn_k_gain)], axis=1)),
        "sinks": f(attn_sinks),
        "lbT": f(np.asarray(rec_lb_logits).reshape(2, RH, 128).transpose(2, 0, 1).reshape(128, -1)),
        "ogT": f(np.asarray(rec_o_gain).reshape(2, RH, 128).transpose(2, 0, 1).reshape(128, -1)),
        "w_ada": f(w_ada), "attn_w_qkv": f(attn_w_qkv), "attn_w_o": f(attn_w_o), "rec_w_in": f(rec_w_in),
        "rec_w_o": f(rec_w_o), "mlp_w_up": f(mlp_w_up), "mlp_w_down": f(mlp_w_down),
    }
    hflag = np.zeros((NSEG, 128, 1), np.float32)
    hflag[1:] = 1.0
    in_maps = []
    for core in range(NCORES):
        b, sq = _core_assign(core)
        xT = np.empty((NSEG, D, T), np.float32)
        for seg in range(NSEG):
            xT[seg, :, 0:TP] = x_prompt[b, seg * TP:(seg + 1) * TP, :].T
            xs = x_sample[sq[seg * NSQ:(seg + 1) * NSQ]].reshape(TS, D)
            xT[seg, :, TP:T] = xs.T
        cs = np.concatenate([c_prompt[b:b + 1], c_sample[sq]], axis=0)
        cT = cs.reshape(NSEQ, DC, 128).transpose(2, 1, 0).reshape(128, DC * NSEQ)
        ck = cache_k_win[:, sq].reshape(2, NSEG, NSQ, 128, 256).transpose(0, 1, 3, 2, 4).reshape(2, NSEG, 128, NSQ * 256)
        cv = cache_v_win[:, sq].reshape(2, NSEG, NSQ, 128, 256).transpose(0, 1, 3, 2, 4).reshape(2, NSEG, 128, NSQ * 256)
        m = dict(shared)
        m.update({"xT": xT, "cT": f(cT), "ck": f(ck), "cv": f(cv), "st": f(state_hgrn[:, sq]), "hflag": hflag})
        in_maps.append(m)
    prog = _get_prog(_nlayers)
    res = run_bass_kernel_spmd(prog.nc, in_maps, core_ids=list(range(NCORES)), trace=_trace)
    R = res.results
    y_prompt = np.empty((BATCH, SEQ, D), np.float32)
    y_sample = np.empty((DEC_BATCH, DEC_SEQ, D), np.float32)
    kwp = np.empty((2, BATCH, 128, NKV, HD), np.float32)
    vwp = np.empty_like(kwp)
    kws = np.empty((2, DEC_BATCH, 128, NKV, HD), np.float32)
    vws = np.empty_like(kws)
    hsp = np.empty((2, BATCH, RH, 128, 128), np.float32)
    hss = np.empty((2, DEC_BATCH, RH, 128, 128), np.float32)
    for core in range(BATCH):
        b, sq = _core_assign(core)
        r = R[core]
        for seg in range(NSEG):
            y_prompt[b, seg * TP:(seg + 1) * TP, :] = r["yT"][seg][:, 0:TP].T
            ys = r["yT"][seg][:, TP:T].T.reshape(NSQ, DEC_SEQ, D)
            y_sample[sq[seg * NSQ:(seg + 1) * NSQ]] = ys
        kwp[:, b] = r["kwp"][:, NSEG - 1].reshape(2, 128, NKV, HD)
        vwp[:, b] = r["vwp"][:, NSEG - 1].reshape(2, 128, NKV, HD)
        kws[:, sq] = r["kws"].reshape(2, NSEG * NSQ, 128, NKV, HD)
        vws[:, sq] = r["vws"].reshape(2, NSEG * NSQ, 128, NKV, HD)
        hsp[:, b] = r["hsp"][:, NSEG - 1]
        hss[:, sq] = r["hss"]
    if _trace:
        kernel.last_exec_ns = res.exec_time_ns
    return (y_prompt, y_sample, kwp, vwp, kws, vws, hsp, hss)
```
